# Optimizing a Trainium2 kernel written in Bass

```python
import math
import jax, jax.numpy as jnp
from jax import lax
import numpy as np

D_MODEL = 1024
BATCH = 2
SEQ = 16384
DEPTH = 2

GRID_W = 64
N_HEADS_A = 8
HEAD_DIM = 64
W_ATTN = N_HEADS_A * HEAD_DIM
WIN_H = 8
WIN_W = 16
N_BLOCKS_B = 8
BLOCK_W = 64
W_LRU = N_BLOCKS_B * BLOCK_W
CONV_W = 4
C_LRU = 8.0
SPLITS = (W_ATTN, W_ATTN, W_ATTN, W_ATTN, W_LRU, W_LRU, D_MODEL, D_MODEL)
D_IN = sum(SPLITS)
ALPHA = (2 * DEPTH) ** 0.25
BETA = (8 * DEPTH) ** -0.25
LN_EPS = 1e-5

kernel_name = "hybrid_natten_rglru_deepnorm_encoder"


def layer_norm(x, g, b):
    xf = x.astype(jnp.float32)
    mu = jnp.mean(xf, axis=-1, keepdims=True)
    var = jnp.mean(jnp.square(xf - mu), axis=-1, keepdims=True)
    y = (xf - mu) * lax.rsqrt(var + LN_EPS)
    return (y * g.astype(jnp.float32) + b.astype(jnp.float32)).astype(x.dtype)


def neighbourhood_attention(q, k, v, rpb):
    B, S, H, Dh = q.shape
    rows = S // GRID_W
    kh = min(WIN_H, rows)
    r = jnp.arange(rows)
    c = jnp.arange(GRID_W)
    rs = jnp.clip(r - kh // 2, 0, rows - kh)
    cs = jnp.clip(c - WIN_W // 2, 0, GRID_W - WIN_W)
    key_r = rs[:, None] + jnp.arange(kh)[None, :]
    key_c = cs[:, None] + jnp.arange(WIN_W)[None, :]
    dr = key_r - r[:, None] + (WIN_H - 1)
    dc = key_c - c[:, None] + (WIN_W - 1)
    n_keys = kh * WIN_W
    scale = Dh ** -0.5
    q_rows = (q * scale).reshape(B, rows, GRID_W, H, Dh).transpose(1, 0, 2, 3, 4)

    def row_fn(args):
        q_row, kr, drr = args
        idx = (kr[None, :, None] * GRID_W + key_c[:, None, :]).reshape(GRID_W, n_keys)
        k_g = k[:, idx]
        v_g = v[:, idx]
        bias = rpb[:, drr[:, None, None], dc[None, :, :]]
        bias = bias.transpose(0, 2, 1, 3).reshape(H, GRID_W, n_keys).astype(jnp.float32)
        s = jnp.einsum('bqhd,bqnhd->bhqn', q_row, k_g).astype(jnp.float32) + bias[None]
        p = jax.nn.softmax(s, axis=-1).astype(v.dtype)
        return jnp.einsum('bhqn,bqnhd->bqhd', p, v_g)

    o = lax.map(row_fn, (q_rows, key_r, dr))
    return o.transpose(1, 0, 2, 3, 4).reshape(B, S, H * Dh)


def centred_depthwise_conv(u, w, b):
    S = u.shape[1]
    left = CONV_W // 2
    up = jnp.pad(u, ((0, 0), (left, CONV_W - 1 - left), (0, 0)))
    out = sum(up[:, j:j + S] * w[j] for j in range(CONV_W))
    return out + b


def _lin_combine(e1, e2):
    a1, b1 = e1
    a2, b2 = e2
    return a1 * a2, a2 * b1 + b2


def rg_lru(x, w_gate, b_gate, lam, reverse):
    B, S, W = x.shape
    xb = x.reshape(B, S, N_BLOCKS_B, BLOCK_W)
    g = jnp.einsum('bsnd,gnde->gbsne', xb, w_gate) + b_gate[:, None, None]
    g = jax.nn.sigmoid(g.astype(jnp.float32)).reshape(2, B, S, W)
    r_gate, i_gate = g[0], g[1]
    log_a = -C_LRU * jax.nn.softplus(-lam.astype(jnp.float32)) * r_gate
    a = jnp.exp(log_a)
    mult = jnp.sqrt(-jnp.expm1(2.0 * log_a))
    first = S - 1 if reverse else 0
    pos = jnp.arange(S)[None, :, None]
    mult = jnp.where(pos == first, 1.0, mult)
    bx = mult * i_gate * x.astype(jnp.float32)
    _, h = lax.associative_scan(_lin_combine, (a, bx), axis=1, reverse=reverse)
    return h.astype(x.dtype)


def hybrid_layer(x, w_in, rpb, conv_w, conv_b, gate_w, gate_b, lam, w_ba, w_bb, b_merge, w_out, ln_g, ln_b):
    B, S, D = x.shape
    proj = x @ w_in
    offs = list(np.cumsum(SPLITS)[:-1])
    q, k, v, z_a, u, z_b, g_a, g_b = jnp.split(proj, offs, axis=-1)
    hs = (B, S, N_HEADS_A, HEAD_DIM)
    attn = neighbourhood_attention(q.reshape(hs), k.reshape(hs), v.reshape(hs), rpb)
    y_a = attn * jax.nn.silu(z_a)
    u = centred_depthwise_conv(u, conv_w, conv_b)
    h = rg_lru(u, gate_w[0], gate_b[0], lam[0], False) + rg_lru(u, gate_w[1], gate_b[1], lam[1], True)
    y_b = h * jax.nn.silu(z_b)
    m = jax.nn.sigmoid(g_a + b_merge[0]) * (y_a @ w_ba) + jax.nn.sigmoid(g_b + b_merge[1]) * (y_b @ w_bb)
    out = m @ w_out
    return layer_norm(ALPHA * x + out, ln_g, ln_b)


def setup_inputs(seed: int = 0) -> dict:
    key = jax.random.key(seed)
    ks = jax.random.split(key, 16)
    f32 = jnp.float32
    D = D_MODEL
    x = jax.random.normal(ks[0], (BATCH, SEQ, D), f32)
    emb_ln_g = 1.0 + 0.01 * jax.random.normal(ks[1], (D,), f32)
    emb_ln_b = 0.01 * jax.random.normal(ks[2], (D,), f32)
    w_in = jax.random.normal(ks[3], (DEPTH, D, D_IN), f32) * D ** -0.5
    rpb = 0.1 * jax.random.normal(ks[4], (DEPTH, N_HEADS_A, 2 * WIN_H - 1, 2 * WIN_W - 1), f32)
    conv_w = jax.random.normal(ks[5], (DEPTH, CONV_W, W_LRU), f32) * CONV_W ** -0.5
    conv_b = 0.01 * jax.random.normal(ks[6], (DEPTH, W_LRU), f32)
    lru_gate_w = jax.random.normal(ks[7], (DEPTH, 2, 2, N_BLOCKS_B, BLOCK_W, BLOCK_W), f32) * BLOCK_W ** -0.5
    lru_gate_b = 0.01 * jax.random.normal(ks[8], (DEPTH, 2, 2, N_BLOCKS_B, BLOCK_W), f32)
    a_c = jax.random.uniform(ks[9], (DEPTH, 2, W_LRU), f32, 0.9, 0.999)
    a_base = a_c ** (1.0 / C_LRU)
    lru_lambda = jnp.log(a_base) - jnp.log1p(-a_base)
    w_branch_attn = jax.random.normal(ks[10], (DEPTH, W_ATTN, D), f32) * (W_ATTN ** -0.5) * BETA
    w_branch_lru = jax.random.normal(ks[11], (DEPTH, W_LRU, D), f32) * (W_LRU ** -0.5) * BETA
    b_merge = 0.01 * jax.random.normal(ks[12], (DEPTH, 2, D), f32)
    w_out = jax.random.normal(ks[13], (DEPTH, D, D), f32) * (D ** -0.5) * BETA
    ln_g = 1.0 + 0.01 * jax.random.normal(ks[14], (DEPTH, D), f32)
    ln_b = 0.01 * jax.random.normal(ks[15], (DEPTH, D), f32)
    return {"x": x, "emb_ln_g": emb_ln_g, "emb_ln_b": emb_ln_b, "w_in": w_in, "rpb": rpb,
            "conv_w": conv_w, "conv_b": conv_b, "lru_gate_w": lru_gate_w, "lru_gate_b": lru_gate_b,
            "lru_lambda": lru_lambda, "w_branch_attn": w_branch_attn, "w_branch_lru": w_branch_lru,
            "b_merge": b_merge, "w_out": w_out, "ln_g": ln_g, "ln_b": ln_b}


def reference(x, emb_ln_g, emb_ln_b, w_in, rpb, conv_w, conv_b, lru_gate_w, lru_gate_b, lru_lambda,
              w_branch_attn, w_branch_lru, b_merge, w_out, ln_g, ln_b):
    h = layer_norm(x, emb_ln_g, emb_ln_b)
    for l in range(DEPTH):
        h = hybrid_layer(h, w_in[l], rpb[l], conv_w[l], conv_b[l], lru_gate_w[l], lru_gate_b[l],
                         lru_lambda[l], w_branch_attn[l], w_branch_lru[l], b_merge[l], w_out[l],
                         ln_g[l], ln_b[l])
    return h
```

```python
import numpy as np
from contextlib import ExitStack
import concourse.bass as bass
import concourse.mybir as mybir
from concourse.bass_utils import run_bass_kernel_spmd

F32 = mybir.dt.float32
BF16 = mybir.dt.bfloat16
AF = mybir.ActivationFunctionType
ALU = mybir.AluOpType

NCORES = 8
NT = 4096
SEQ = 16384
D = 1024
ALPHA = 4.0 ** 0.25
NEG = -30000.0


class Sched:
    ENG = ("pe", "act", "dve", "pool", "sp")

    def __init__(self, nc, stack):
        self.nc = nc
        self.stack = stack
        self.sems = {}
        self.count = {}
        for e in self.ENG:
            self._sem("e:" + e)
        self.lastw = {}
        self.reads = {}
        self.known = {e: {} for e in self.ENG}
        self.stream = {e: [] for e in self.ENG}

    def _sem(self, name):
        if name not in self.sems:
            self.sems[name] = self.stack.enter_context(self.nc.semaphore("s%d" % len(self.sems)))
            self.count[name] = 0
        return self.sems[name]

    def op(self, eng, name, reads=(), writes=(), dma=False, semkey=None, signal=True, **kw):
        fn = (name, kw)
        deps = {}

        def add(s, v):
            if v > deps.get(s, 0):
                deps[s] = v

        for k in reads:
            if k in self.lastw:
                add(*self.lastw[k])
        for k in writes:
            if k in self.lastw:
                add(*self.lastw[k])
            for s, v in self.reads.get(k, {}).items():
                add(s, v)
        waits = []
        for s, v in deps.items():
            if s == "e:pe" and eng == "pe":
                continue
            if self.known[eng].get(s, 0) >= v:
                continue
            self.known[eng][s] = v
            waits.append((s, v))
        if dma:
            sk = "d:" + str(semkey if semkey is not None else (list(writes) + list(reads))[0])
            self._sem(sk)
            self.count[sk] += 16
            tok = (sk, self.count[sk])
            emit_tok = tok
        else:
            sk = "e:" + eng
            if signal:
                self.count[sk] += 1
                tok = (sk, self.count[sk])
                emit_tok = tok
            else:
                assert eng == "pe"
                tok = (sk, self.count[sk] + 1)
                emit_tok = None
        for k in reads:
            d = self.reads.setdefault(k, {})
            if tok[1] > d.get(tok[0], 0):
                d[tok[0]] = tok[1]
        for k in writes:
            self.lastw[k] = tok
            self.reads[k] = {}
        self.stream[eng].append((waits, fn, emit_tok, dma))
        return tok

    def finish(self, eng="sp"):
        waits = [(s, v) for s, v in self.count.items() if s.startswith("d:") and v > 0]
        self.stream[eng].append((waits, None, None, False))

    def build(self):
        nc = self.nc
        with nc.Block() as block:
            def run(ename, e):
                for waits, fn, tok, dma in self.stream[ename]:
                    for s, v in waits:
                        e.wait_ge(self.sems[s], v)
                    if fn is None:
                        continue
                    ins = getattr(e, fn[0])(**fn[1])
                    if tok is not None:
                        ins.then_inc(self.sems[tok[0]], 16 if dma else 1)

            @block.tensor
            def _(e):
                run("pe", e)

            @block.scalar
            def _(e):
                run("act", e)

            @block.vector
            def _(e):
                run("dve", e)

            @block.gpsimd
            def _(e):
                run("pool", e)

            @block.sync
            def _(e):
                run("sp", e)


class Ctx:
    def __init__(self):
        self.nc = bass.Bass("TRN2", target_bir_lowering=False)
        self.st = ExitStack()
        self.S = Sched(self.nc, self.st)

    def din(self, name, shape, dt=F32):
        return self.nc.dram_tensor(name, list(shape), dt, kind="ExternalInput").ap()

    def dout(self, name, shape, dt=F32):
        return self.nc.dram_tensor(name, list(shape), dt, kind="ExternalOutput").ap()

    def sb(self, name, shape, dt=F32):
        return self.st.enter_context(self.nc.sbuf_tensor(name, list(shape), dt))

    def ps(self, name, shape, dt=F32):
        return self.st.enter_context(self.nc.psum_tensor(name, list(shape), dt))

    def done(self):
        self.S.finish()
        self.S.build()
        self.st.close()
        return self.nc


def _dma(S, eng, out, in_, reads=(), writes=(), semkey=None):
    return S.op(eng, "dma_start", reads=reads, writes=writes, dma=True, semkey=semkey, out=out, in_=in_)


def build_k1(do_proj=True):
    C = Ctx()
    S = C.S
    yin = C.din("yin", [NT, D])
    lng = C.din("lng", [128, D])
    lnb = C.din("lnb", [128, D])
    xres = C.dout("xres", [NT, D])
    if do_proj:
        identf = C.din("ident", [128, 128])
        w_in = C.din("w_in", [D, 5120])
        bm = C.din("bm", [128, 16])
        qT = C.dout("qT", [512, NT], BF16)
        kT = C.dout("kT", [512, NT], BF16)
        vo = C.dout("v", [NT, 512], BF16)
        szaT = C.dout("szaT", [512, NT])
        uT = C.dout("uT", [512, NT])
        szbT = C.dout("szbT", [512, NT])
        sgaT = C.dout("sgaT", [D, NT])
        sgbT = C.dout("sgbT", [D, NT])

    lng_sb = C.sb("lng_sb", [128, D])
    lnb_sb = C.sb("lnb_sb", [128, D])
    yt = C.sb("yt", [128, 2, D])
    xn = C.sb("xn", [128, 2, D])
    stats = C.sb("stats", [128, 2, 12])
    mv = C.sb("mv", [128, 2, 2])
    rstd = C.sb("rstd", [128, 2, 1])
    _dma(S, "sp", lng_sb[:], lng[:, :], writes=["lng"])
    _dma(S, "sp", lnb_sb[:], lnb[:, :], writes=["lnb"])
    if do_proj:
        wbf = C.sb("wbf", [128, 8, 5120], BF16)
        wst = C.sb("wst", [128, 2, 2560])
        ident_f = C.sb("ident_f", [128, 128])
        identb = C.sb("identb", [128, 128], BF16)
        bm_sb = C.sb("bm_sb", [128, 16])
        xnb = C.sb("xnb", [128, 2, D], BF16)
        xT = C.sb("xT", [128, 2, 8, 512], BF16)
        NOF, NOB = 6, 4
        ostf = C.sb("ostf", [128, NOF, 512])
        ostb = C.sb("ostb", [128, NOB, 512], BF16)
        pst = [C.ps("pst%d" % i, [128, 8, 128], BF16) for i in range(2)]
        NPS = 6
        psm = [C.ps("psm%d" % i, [128, 512]) for i in range(NPS)]
        _dma(S, "sp", ident_f[:], identf[:, :], writes=["identf"])
        _dma(S, "sp", bm_sb[:], bm[:, :], writes=["bm"])
        S.op("dve", "tensor_copy", reads=["identf"], writes=["identb"], out=identb[:], in_=ident_f[:])
        cast_engs = ["dve", "pool", "act"]
        for kc in range(8):
            for half in range(2):
                i = kc * 2 + half
                sl = i % 2
                _dma(S, "sp", wst[:, sl, :], w_in[kc * 128:(kc + 1) * 128, half * 2560:(half + 1) * 2560],
                     writes=[("wst", sl)])
                ce = cast_engs[i % 3]
                o = wbf[:, kc, half * 2560:(half + 1) * 2560]
                if ce == "act":
                    S.op("act", "activation", reads=[("wst", sl)], writes=[("wbf", kc, half)],
                         out=o, in_=wst[:, sl, :], func=AF.Copy)
                else:
                    S.op(ce, "tensor_copy", reads=[("wst", sl)], writes=[("wbf", kc, half)], out=o, in_=wst[:, sl, :])

    cnt = {"f": 0, "b": 0, "ps": 0}

    def ln_tile(g):
        s2 = g % 2
        _dma(S, "sp", yt[:, s2, :], yin[g * 128:(g + 1) * 128, :], writes=[("yt", s2)])
        for hh in range(2):
            S.op("dve", "bn_stats", reads=[("yt", s2)], writes=[("stats", s2, hh)],
                 out=stats[:, s2, hh * 6:(hh + 1) * 6], in_=yt[:, s2, hh * 512:(hh + 1) * 512])
        S.op("dve", "bn_aggr", reads=[("stats", s2, 0), ("stats", s2, 1)], writes=[("mv", s2)],
             out=mv[:, s2, :], in_=stats[:, s2, :])
        S.op("act", "activation", reads=[("mv", s2)], writes=[("rstd", s2)],
             out=rstd[:, s2, :], in_=mv[:, s2, 1:2], func=AF.Sqrt, bias=1e-5, scale=1.0)
        S.op("dve", "reciprocal", reads=[("rstd", s2)], writes=[("rstd", s2)], out=rstd[:, s2, :], in_=rstd[:, s2, :])
        S.op("dve", "tensor_scalar", reads=[("yt", s2), ("mv", s2), ("rstd", s2)], writes=[("xn", s2)],
             out=xn[:, s2, :], in0=yt[:, s2, :], scalar1=mv[:, s2, 0:1], scalar2=rstd[:, s2, :],
             op0=ALU.subtract, op1=ALU.mult)
        S.op("pool", "tensor_tensor", reads=[("xn", s2), "lng"], writes=[("xn", s2)],
             out=xn[:, s2, :], in0=xn[:, s2, :], in1=lng_sb[:], op=ALU.mult)
        S.op("pool", "tensor_tensor", reads=[("xn", s2), "lnb"], writes=[("xn", s2)],
             out=xn[:, s2, :], in0=xn[:, s2, :], in1=lnb_sb[:], op=ALU.add)
        _dma(S, "sp", xres[g * 128:(g + 1) * 128, :], xn[:, s2, :], reads=[("xn", s2)], semkey=("xn_o", s2))

    def tr_tile(g):
        s2 = g % 2
        tb, tt = g // 4, g % 4
        bs = tb % 2
        S.op("act", "activation", reads=[("xn", s2)], writes=[("xnb", s2)],
             out=xnb[:, s2, :], in_=xn[:, s2, :], func=AF.Copy)
        for kc in range(8):
            S.op("pe", "transpose", reads=[("xnb", s2), "identb"], writes=[("pst", s2)], signal=(kc == 7),
                 out=pst[s2][:, kc, :], in_=xnb[:, s2, kc * 128:(kc + 1) * 128], identity=identb[:])
        S.op("dve", "tensor_copy", reads=[("pst", s2)], writes=[("xT", bs, tt)],
             out=xT[:, bs, :, tt * 128:(tt + 1) * 128], in_=pst[s2][:, :, :])

    def evac(kind, c, bank, tb):
        src = psm[bank][:, :]
        tsl = slice(tb * 512, (tb + 1) * 512)
        if kind in ("q", "k"):
            sl = cnt["b"] % NOB
            cnt["b"] += 1
            dst = ostb[:, sl, :]
            if kind == "q":
                S.op("dve", "tensor_scalar", reads=[("psm", bank)], writes=[("ostb", sl)],
                     out=dst, in0=src, scalar1=0.125, scalar2=None, op0=ALU.mult)
                tgt = qT
            else:
                S.op("dve", "tensor_copy", reads=[("psm", bank)], writes=[("ostb", sl)], out=dst, in_=src)
                tgt = kT
            _dma(S, "sp", tgt[c * 128:(c + 1) * 128, tsl], dst, reads=[("ostb", sl)], semkey=("ostb", sl))
            return
        sl = cnt["f"] % NOF
        cnt["f"] += 1
        dst = ostf[:, sl, :]
        if kind == "u":
            S.op("dve", "tensor_copy", reads=[("psm", bank)], writes=[("ostf", sl)], out=dst, in_=src)
            tgt = uT
        elif kind in ("za", "zb"):
            S.op("act", "activation", reads=[("psm", bank)], writes=[("ostf", sl)], out=dst, in_=src, func=AF.Silu)
            tgt = szaT if kind == "za" else szbT
        else:
            col = c if kind == "ga" else 8 + c
            S.op("act", "activation", reads=[("psm", bank), "bm"], writes=[("ostf", sl)],
                 out=dst, in_=src, func=AF.Sigmoid, bias=bm_sb[:, col:col + 1])
            tgt = sgaT if kind == "ga" else sgbT
        _dma(S, "sp", tgt[c * 128:(c + 1) * 128, tsl], dst, reads=[("ostf", sl)], semkey=("ostf", sl))

    fam = ([("q", i) for i in range(4)] + [("k", i) for i in range(4)] + [("v", i) for i in range(4)]
           + [("za", i) for i in range(4)] + [("u", i) for i in range(4)] + [("zb", i) for i in range(4)]
           + [("ga", i) for i in range(8)] + [("gb", i) for i in range(8)])

    def proj_block(tb):
        bs = tb % 2
        xkeys = [("xT", bs, tt) for tt in range(4)]
        for occ, (kind, ci) in enumerate(fam):
            if kind == "v":
                continue
            bank = cnt["ps"] % NPS
            cnt["ps"] += 1
            half = occ // 20
            for kc in range(8):
                S.op("pe", "matmul", reads=xkeys + [("wbf", kc, half)], writes=[("psm", bank)], signal=(kc == 7),
                     out=psm[bank][:, :], lhsT=wbf[:, kc, occ * 128:(occ + 1) * 128], rhs=xT[:, bs, kc, :],
                     start=(kc == 0), stop=(kc == 7))
            evac(kind, ci, bank, tb)
        for tt in range(4):
            bank = cnt["ps"] % NPS
            cnt["ps"] += 1
            for kc in range(8):
                S.op("pe", "matmul", reads=[("xT", bs, tt), ("wbf", kc, 0)], writes=[("psm", bank)], signal=(kc == 7),
                     out=psm[bank][:, :], lhsT=xT[:, bs, kc, tt * 128:(tt + 1) * 128], rhs=wbf[:, kc, 1024:1536],
                     start=(kc == 0), stop=(kc == 7))
            sl = cnt["b"] % NOB
            cnt["b"] += 1
            S.op("dve", "tensor_copy", reads=[("psm", bank)], writes=[("ostb", sl)], out=ostb[:, sl, :], in_=psm[bank][:, :])
            g = tb * 4 + tt
            _dma(S, "sp", vo[g * 128:(g + 1) * 128, :], ostb[:, sl, :], reads=[("ostb", sl)], semkey=("ostb", sl))

    NTILE = NT // 128
    if not do_proj:
        for g in range(NTILE):
            ln_tile(g)
    else:
        for tb in range(NT // 512):
            for tt in range(4):
                ln_tile(tb * 4 + tt)
                tr_tile(tb * 4 + tt)
            proj_block(tb)
    return C.done()


CLS_NT = [6, 6, 5, 6, 5]
CLS_OFF = [0, 6, 12, 17, 23]
NTAB = 28
KROWS = 72


def tile_class(t):
    if t == 0:
        return 0, 0
    if t == 1:
        return 1, 1
    if t == 31:
        return 3, 30
    if t == 30:
        return 4, 30
    return 2, t


def build_k2():
    C = Ctx()
    S = C.S
    NK = KROWS * 64
    qT = C.din("qT", [512, NT], BF16)
    kTh = C.din("kTh", [512, NK], BF16)
    vh = C.din("vh", [NK, 512], BF16)
    szaT = C.din("szaT", [512, NT])
    bt = C.din("bt", [4, 128, 2, NTAB, 128])
    yaT = C.dout("yaT", [512, NT], BF16)

    q_sb = C.sb("q_sb", [128, 2, NT], BF16)
    k_sb = C.sb("k_sb", [128, 2, NK], BF16)
    v_sb = C.sb("v_sb", [128, 2, NK // 128, 128], BF16)
    sza_sb = C.sb("sza_sb", [128, 2, NT])
    bt_raw = C.sb("bt_raw", [128, 2, NTAB, 128])
    bt_sb = C.sb("bt_sb", [128, 2, NTAB, 128])
    ya_sb = C.sb("ya_sb", [128, 2, NT], BF16)
    ones_b = C.sb("ones_b", [128, 64], BF16)
    E = C.sb("E", [128, 2, 768])
    Pm = C.sb("Pm", [128, 2, 768], BF16)
    rden = C.sb("rden", [128, 2, 128])
    tmp = C.sb("tmp", [128, 2, 128])
    sps = [[C.ps("sps%d_%d" % (i, j), [128, 512]) for j in range(2)] for i in range(2)]
    nd = [C.ps("nd%d" % i, [128, 128]) for i in range(2)]
    dn = [C.ps("dn%d" % i, [128, 128]) for i in range(2)]

    S.op("dve", "memset", writes=["ones"], ap=ones_b[:], constant=1.0)

    def load_pair(p):
        s = p % 2
        _dma(S, "sp", q_sb[:, s, :], qT[p * 128:(p + 1) * 128, :], writes=[("q", s)])
        _dma(S, "sp", k_sb[:, s, :], kTh[p * 128:(p + 1) * 128, :], writes=[("k", s)])
        _dma(S, "sp", v_sb[:, s, :, :], vh[:, p * 128:(p + 1) * 128].rearrange("(t p) f -> p t f", p=128),
             writes=[("v", s)])
        _dma(S, "sp", sza_sb[:, s, :], szaT[p * 128:(p + 1) * 128, :], writes=[("sza", s)])
        _dma(S, "sp", bt_raw[:], bt[p], writes=["bt_raw"])

    load_pair(0)
    hs_cnt = 0
    for p in range(4):
        s = p % 2
        for hh in range(2):
            S.op("act", "activation", reads=["bt_raw"], writes=[("bt", hh)],
                 out=bt_sb[:, hh, :, :], in_=bt_raw[:, hh, :, :], func=AF.Exp)
        if p + 1 < 4:
            load_pair(p + 1)
        for t in range(32):
            cls, vt0 = tile_class(t)
            nkt = CLS_NT[cls]
            toff = CLS_OFF[cls]
            par = t % 2
            for hh in range(2):
                hs = hs_cnt % 2
                hs_cnt += 1
                pr = slice(hh * 64, (hh + 1) * 64)
                for kt in range(nkt):
                    bank = sps[hs][kt // 4]
                    co = (kt % 4) * 128
                    kcol = (vt0 + kt) * 128
                    last = (kt == nkt - 1) or (kt == 3)
                    S.op("pe", "matmul", reads=[("k", s), ("q", s)], writes=[("sps", hs, kt // 4)], signal=last,
                         out=bank[:, co:co + 128], lhsT=k_sb[pr, s, kcol:kcol + 128],
                         rhs=q_sb[pr, s, t * 128:(t + 1) * 128], start=True, stop=True)
                n0 = min(nkt, 4) * 128
                S.op("act", "activation", reads=[("sps", hs, 0)], writes=[("E", hs, 0)],
                     out=E[:, hs, 0:n0], in_=sps[hs][0][:, 0:n0], func=AF.Exp)
                n1 = (nkt - 4) * 128
                S.op("act", "activation", reads=[("sps", hs, 1)], writes=[("E", hs, 1)],
                     out=E[:, hs, 512:512 + n1], in_=sps[hs][1][:, 0:n1], func=AF.Exp)
                S.op("dve", "tensor_tensor", reads=[("E", hs, 0), ("E", hs, 1), ("bt", hh)], writes=[("Pm", hs)],
                     out=Pm[:, hs, 0:nkt * 128], in0=E[:, hs, 0:nkt * 128],
                     in1=bt_sb[:, hh, toff:toff + nkt, :].rearrange("p a b -> p (a b)"), op=ALU.mult)
                for kt in range(nkt):
                    S.op("pe", "matmul", reads=[("Pm", hs), ("v", s)], writes=[("nd", par)], signal=False,
                         out=nd[par][pr, 0:128], lhsT=v_sb[:, s, vt0 + kt, hh * 64:(hh + 1) * 64],
                         rhs=Pm[:, hs, kt * 128:(kt + 1) * 128], start=(kt == 0), stop=(kt == nkt - 1))
                    S.op("pe", "matmul", reads=[("Pm", hs), "ones"], writes=[("nd", par)], signal=(kt == nkt - 1),
                         out=dn[par][pr, 0:128], lhsT=ones_b[:, :],
                         rhs=Pm[:, hs, kt * 128:(kt + 1) * 128], start=(kt == 0), stop=(kt == nkt - 1))
            S.op("dve", "reciprocal", reads=[("nd", par)], writes=[("rden", par)],
                 out=rden[:, par, :], in_=dn[par][:, 0:128])
            S.op("dve", "tensor_tensor", reads=[("nd", par), ("rden", par)], writes=[("tmp", par)],
                 out=tmp[:, par, :], in0=nd[par][:, 0:128], in1=rden[:, par, :], op=ALU.mult)
            S.op("pool", "tensor_tensor", reads=[("tmp", par), ("sza", s)], writes=[("ya", s)],
                 out=ya_sb[:, s, t * 128:(t + 1) * 128], in0=tmp[:, par, :],
                 in1=sza_sb[:, s, t * 128:(t + 1) * 128], op=ALU.mult)
        _dma(S, "sp", yaT[p * 128:(p + 1) * 128, :], ya_sb[:, s, :], reads=[("ya", s)], semkey=("ya_o", s))
    return C.done()


def build_k3():
    C = Ctx()
    S = C.S
    CH = 1024
    NCH = SEQ // CH
    uT = C.din("uT", [128, SEQ])
    szbT = C.din("szbT", [128, SEQ])
    cw = C.din("cw", [128, 5])
    gw = C.din("gw", [128, 4, 128])
    gb = C.din("gb", [128, 4])
    lam = C.din("lam", [128, 2])
    ybT = C.dout("ybT", [128, SEQ], BF16)

    u_sb = C.sb("u_sb", [128, SEQ + 3])
    uc = C.sb("uc", [128, SEQ])
    cw_sb = C.sb("cw_sb", [128, 5])
    gw_sb = C.sb("gw_sb", [128, 4, 128])
    gb_sb = C.sb("gb_sb", [128, 4])
    lam_sb = C.sb("lam_sb", [128, 2])
    cs = C.sb("cs", [128, 12, 2])
    csp = C.sb("csp", [128, 2])
    csp2 = C.sb("csp2", [128, 2])
    r_sb = C.sb("r_sb", [128, 2, CH])
    i_sb = C.sb("i_sb", [128, 2, CH])
    a_sb = C.sb("a_sb", [128, 2, CH])
    m_sb = C.sb("m_sb", [128, 2, CH])
    hr_sb = C.sb("hr_sb", [128, 2, CH])
    szb_sb = C.sb("szb_sb", [128, 2, CH])
    yb_sb = C.sb("yb_sb", [128, 2, CH], BF16)
    carry = C.sb("carry", [128, 2])
    psr = [[C.ps("psr%d_%d" % (i, j), [128, 512]) for j in range(2)] for i in range(2)]
    psi = [[C.ps("psi%d_%d" % (i, j), [128, 512]) for j in range(2)] for i in range(2)]

    _dma(S, "sp", cw_sb[:], cw[:, :], writes=["cw"])
    _dma(S, "sp", gw_sb[:], gw[:, :, :], writes=["gw"])
    _dma(S, "sp", gb_sb[:], gb[:, :], writes=["gb"])
    _dma(S, "sp", lam_sb[:], lam[:, :], writes=["lam"])
    S.op("dve", "memset", writes=["upad0"], ap=u_sb[:, 0:2], constant=0.0)
    S.op("dve", "memset", writes=["upad1"], ap=u_sb[:, SEQ + 2:SEQ + 3], constant=0.0)
    LCH = 2048
    NL = SEQ // LCH
    for c in range(NL):
        _dma(S, "sp", u_sb[:, 2 + c * LCH:2 + (c + 1) * LCH], uT[:, c * LCH:(c + 1) * LCH], writes=[("u", c)],
             semkey=("u", c % 2))

    V = lambda i: cs[:, i, :]
    k = dict(reads=["cs", "lam"], writes=["cs"])
    S.op("dve", "tensor_scalar", out=V(8), in0=lam_sb[:], scalar1=-1.0, scalar2=None, op0=ALU.mult, **k)
    S.op("dve", "tensor_tensor", out=V(0), in0=V(8), in1=lam_sb[:], op=ALU.max, **k)
    S.op("act", "activation", out=V(1), in_=V(0), func=AF.Exp, scale=-1.0, **k)
    S.op("dve", "tensor_scalar", out=V(2), in0=V(1), scalar1=2.0, scalar2=None, op0=ALU.add, **k)
    S.op("dve", "reciprocal", out=V(2), in_=V(2), **k)
    S.op("dve", "tensor_tensor", out=V(3), in0=V(1), in1=V(2), op=ALU.mult, **k)
    S.op("dve", "tensor_tensor", out=V(4), in0=V(3), in1=V(3), op=ALU.mult, **k)
    S.op("dve", "tensor_scalar", out=V(5), in0=V(4), scalar1=1.0 / 13, scalar2=1.0 / 11, op0=ALU.mult, op1=ALU.add, **k)
    for cden in (9.0, 7.0, 5.0, 3.0, 1.0):
        S.op("dve", "tensor_tensor", out=V(5), in0=V(5), in1=V(4), op=ALU.mult, **k)
        S.op("dve", "tensor_scalar", out=V(5), in0=V(5), scalar1=1.0 / cden, scalar2=None, op0=ALU.add, **k)
    S.op("dve", "tensor_tensor", out=V(5), in0=V(5), in1=V(3), op=ALU.mult, **k)
    S.op("dve", "tensor_scalar", out=V(6), in0=lam_sb[:], scalar1=-1.0, scalar2=0.0, op0=ALU.mult, op1=ALU.max, **k)
    S.op("dve", "scalar_tensor_tensor", out=V(7), in0=V(5), scalar=2.0, in1=V(6), op0=ALU.mult, op1=ALU.add, **k)
    S.op("dve", "tensor_scalar", reads=["cs"], writes=["csp"], out=csp[:], in0=V(7), scalar1=-8.0, scalar2=None, op0=ALU.mult)
    S.op("dve", "tensor_scalar", reads=["cs"], writes=["csp2"], out=csp2[:], in0=V(7), scalar1=-16.0, scalar2=None, op0=ALU.mult)

    for c in range(NL):
        lo = c * LCH
        rd = [("u", c), "cw", "upad0", "upad1"] + ([("u", c - 1)] if c > 0 else []) + ([("u", c + 1)] if c + 1 < NL else [])
        eng = "dve"
        o = uc[:, lo:lo + LCH]
        S.op(eng, "tensor_scalar", reads=rd, writes=[("uc", c)], out=o, in0=u_sb[:, lo:lo + LCH],
             scalar1=cw_sb[:, 0:1], scalar2=cw_sb[:, 4:5], op0=ALU.mult, op1=ALU.add)
        for j in range(1, 4):
            S.op(eng, "scalar_tensor_tensor", reads=rd + [("uc", c)], writes=[("uc", c)], out=o,
                 in0=u_sb[:, lo + j:lo + j + LCH], scalar=cw_sb[:, j:j + 1], in1=o, op0=ALU.mult, op1=ALU.add)
    UCK = [("uc", c) for c in range(NL)]
    UK = [("u", c) for c in range(NL)]

    def gates(d, n, it):
        sl = it % 2
        lo = n * CH
        uck = ("uc", lo // LCH)
        for g, pst in ((0, psr), (1, psi)):
            for hf in range(2):
                S.op("pe", "matmul", reads=["gw", uck], writes=[("ps", g, sl, hf)],
                     out=pst[sl][hf][:, :], lhsT=gw_sb[:, d * 2 + g, :],
                     rhs=uc[:, lo + hf * 512:lo + (hf + 1) * 512], start=True, stop=True)
        for hf in range(2):
            fs = slice(hf * 512, (hf + 1) * 512)
            S.op("act", "activation", reads=[("ps", 0, sl, hf), "gb"], writes=[("r", sl, hf)],
                 out=r_sb[:, sl, fs], in_=psr[sl][hf][:, :], func=AF.Sigmoid, bias=gb_sb[:, d * 2:d * 2 + 1])
            S.op("act", "activation", reads=[("ps", 1, sl, hf), "gb"], writes=[("i", sl, hf)],
                 out=i_sb[:, sl, fs], in_=psi[sl][hf][:, :], func=AF.Sigmoid, bias=gb_sb[:, d * 2 + 1:d * 2 + 2])
        rk = [("r", sl, 0), ("r", sl, 1)]
        ik = [("i", sl, 0), ("i", sl, 1)]
        S.op("act", "activation", reads=rk + ["csp"], writes=[("a", sl)],
             out=a_sb[:, sl, :], in_=r_sb[:, sl, :], func=AF.Exp, scale=csp[:, d:d + 1])
        S.op("act", "activation", reads=rk + ["csp2"], writes=[("m", sl)],
             out=m_sb[:, sl, :], in_=r_sb[:, sl, :], func=AF.Exp, scale=csp2[:, d:d + 1])
        S.op("act", "activation", reads=[("m", sl)], writes=[("m", sl)], out=m_sb[:, sl, :], in_=m_sb[:, sl, :],
             func=AF.Sqrt, scale=-1.0, bias=1.0)
        S.op("pool", "tensor_tensor", reads=ik + [uck], writes=ik, out=i_sb[:, sl, :], in0=i_sb[:, sl, :],
             in1=uc[:, lo:lo + CH], op=ALU.mult)
        first = (d == 0 and n == 0) or (d == 1 and n == NCH - 1)
        if first:
            col = 0 if d == 0 else CH - 1
            S.op("dve", "memset", reads=[("m", sl)], writes=[("m", sl)], ap=m_sb[:, sl, col:col + 1], constant=1.0)
        S.op("dve", "tensor_tensor", reads=[("m", sl)] + ik, writes=[("m", sl)], out=m_sb[:, sl, :],
             in0=m_sb[:, sl, :], in1=i_sb[:, sl, :], op=ALU.mult)
        return sl

    it = 0
    for n in range(NCH):
        sl = gates(0, n, it)
        it += 1
        lo = n * CH
        init = 0.0 if n == 0 else u_sb[:, 2 + lo - 1:2 + lo]
        S.op("dve", "tensor_tensor_scan", reads=[("a", sl), ("m", sl)] + UCK + UK + ["hf"], writes=["hf"],
             out=u_sb[:, 2 + lo:2 + lo + CH], data0=a_sb[:, sl, :], data1=m_sb[:, sl, :], initial=init,
             op0=ALU.mult, op1=ALU.add)
    for n in range(NCH - 1, -1, -1):
        sl = gates(1, n, it)
        it += 1
        lo = n * CH
        _dma(S, "sp", szb_sb[:, sl, :], szbT[:, lo:lo + CH], writes=[("szb", sl)])
        init = 0.0 if n == NCH - 1 else carry[:, 0:1]
        S.op("dve", "tensor_tensor_scan", reads=[("a", sl), ("m", sl), "carry"], writes=[("hr", sl)],
             out=hr_sb[:, sl, ::-1], data0=a_sb[:, sl, ::-1], data1=m_sb[:, sl, ::-1], initial=init,
             op0=ALU.mult, op1=ALU.add)
        S.op("dve", "tensor_copy", reads=[("hr", sl)], writes=["carry"], out=carry[:, 0:1], in_=hr_sb[:, sl, 0:1])
        S.op("pool", "tensor_tensor", reads=[("hr", sl), "hf"], writes=[("hr", sl)], out=hr_sb[:, sl, :],
             in0=hr_sb[:, sl, :], in1=u_sb[:, 2 + lo:2 + lo + CH], op=ALU.add)
        S.op("pool", "tensor_tensor", reads=[("hr", sl), ("szb", sl)], writes=[("yb", sl)], out=yb_sb[:, sl, :],
             in0=hr_sb[:, sl, :], in1=szb_sb[:, sl, :], op=ALU.mult)
        _dma(S, "sp", ybT[:, lo:lo + CH], yb_sb[:, sl, :], reads=[("yb", sl)], semkey=("yb_o", sl))
    return C.done()


def build_k4():
    C = Ctx()
    S = C.S
    yaT = C.din("yaT", [512, NT], BF16)
    ybT = C.din("ybT", [512, NT], BF16)
    sgaT = C.din("sgaT", [D, NT])
    sgbT = C.din("sgbT", [D, NT])
    xres = C.din("xres", [NT, D])
    w_ba = C.din("w_ba", [512, D])
    w_bb = C.din("w_bb", [512, D])
    w_out = C.din("w_out", [D, D])
    yout = C.dout("yout", [NT, D])

    wba = C.sb("wba", [128, 4, D], BF16)
    wbb = C.sb("wbb", [128, 4, D], BF16)
    wo = C.sb("wo", [128, 8, D], BF16)
    wst = C.sb("wst", [128, 2, D])
    ya = C.sb("ya", [128, 2, 4, 512], BF16)
    yb = C.sb("yb", [128, 2, 4, 512], BF16)
    sga = C.sb("sga", [128, 2, 8, 512])
    sgb = C.sb("sgb", [128, 2, 8, 512])
    t1 = C.sb("t1", [128, 2, 512])
    t2 = C.sb("t2", [128, 2, 512])
    mT = C.sb("mT", [128, 2, 8, 512], BF16)
    xr = C.sb("xr", [128, 2, D])
    yo = C.sb("yo", [128, 2, D])
    pa = [C.ps("pa%d" % i, [128, 512]) for i in range(2)]
    pb = [C.ps("pb%d" % i, [128, 512]) for i in range(2)]
    po = [C.ps("po%d" % i, [128, 512]) for i in range(4)]

    i = 0
    for (src, dst, nk, key) in ((w_ba, wba, 4, "wba"), (w_bb, wbb, 4, "wbb"), (w_out, wo, 8, "wo")):
        for kc in range(nk):
            sl = i % 2
            _dma(S, "sp", wst[:, sl, :], src[kc * 128:(kc + 1) * 128, :], writes=[("wst", sl)])
            eng = "dve" if i % 2 == 0 else "pool"
            S.op(eng, "tensor_copy", reads=[("wst", sl)], writes=[(key, kc)], out=dst[:, kc, :], in_=wst[:, sl, :])
            i += 1

    def load_blk(tb):
        bs = tb % 2
        tsl = slice(tb * 512, (tb + 1) * 512)
        _dma(S, "sp", ya[:, bs, :, :], yaT[:, tsl].rearrange("(k p) t -> p k t", p=128), writes=[("ya", bs)])
        _dma(S, "sp", yb[:, bs, :, :], ybT[:, tsl].rearrange("(k p) t -> p k t", p=128), writes=[("yb", bs)])
        _dma(S, "sp", sga[:, bs, :, :], sgaT[:, tsl].rearrange("(k p) t -> p k t", p=128), writes=[("sga", bs)])
        _dma(S, "sp", sgb[:, bs, :, :], sgbT[:, tsl].rearrange("(k p) t -> p k t", p=128), writes=[("sgb", bs)])

    load_blk(0)
    pcnt = 0
    ocnt = 0
    for tb in range(NT // 512):
        bs = tb % 2
        if tb + 1 < NT // 512:
            load_blk(tb + 1)
        for oc in range(8):
            ps_ = pcnt % 2
            pcnt += 1
            for kc in range(4):
                S.op("pe", "matmul", reads=[("wba", kc), ("ya", bs)], writes=[("pa", ps_)], signal=(kc == 3),
                     out=pa[ps_][:, :], lhsT=wba[:, kc, oc * 128:(oc + 1) * 128], rhs=ya[:, bs, kc, :],
                     start=(kc == 0), stop=(kc == 3))
            for kc in range(4):
                S.op("pe", "matmul", reads=[("wbb", kc), ("yb", bs)], writes=[("pb", ps_)], signal=(kc == 3),
                     out=pb[ps_][:, :], lhsT=wbb[:, kc, oc * 128:(oc + 1) * 128], rhs=yb[:, bs, kc, :],
                     start=(kc == 0), stop=(kc == 3))
            S.op("dve", "tensor_tensor", reads=[("pa", ps_), ("sga", bs)], writes=[("t1", ps_)],
                 out=t1[:, ps_, :], in0=pa[ps_][:, :], in1=sga[:, bs, oc, :], op=ALU.mult)
            S.op("dve", "tensor_tensor", reads=[("pb", ps_), ("sgb", bs)], writes=[("t2", ps_)],
                 out=t2[:, ps_, :], in0=pb[ps_][:, :], in1=sgb[:, bs, oc, :], op=ALU.mult)
            S.op("pool", "tensor_tensor", reads=[("t1", ps_), ("t2", ps_)], writes=[("mT", bs, oc)],
                 out=mT[:, bs, oc, :], in0=t1[:, ps_, :], in1=t2[:, ps_, :], op=ALU.add)
        for tt in range(4):
            g = tb * 4 + tt
            xs = g % 2
            _dma(S, "sp", xr[:, xs, :], xres[g * 128:(g + 1) * 128, :], writes=[("xr", xs)])
            for half in range(2):
                ob = ocnt % 4
                ocnt += 1
                for kc in range(8):
                    S.op("pe", "matmul", reads=[("mT", bs, kc), ("wo", kc)], writes=[("po", ob)], signal=(kc == 7),
                         out=po[ob][:, :], lhsT=mT[:, bs, kc, tt * 128:(tt + 1) * 128],
                         rhs=wo[:, kc, half * 512:(half + 1) * 512], start=(kc == 0), stop=(kc == 7))
                S.op("dve", "scalar_tensor_tensor", reads=[("xr", xs), ("po", ob)], writes=[("yo", xs, half)],
                     out=yo[:, xs, half * 512:(half + 1) * 512], in0=xr[:, xs, half * 512:(half + 1) * 512],
                     scalar=ALPHA, in1=po[ob][:, :], op0=ALU.mult, op1=ALU.add)
            _dma(S, "sp", yout[g * 128:(g + 1) * 128, :], yo[:, xs, :], reads=[("yo", xs, 0), ("yo", xs, 1)],
                 semkey=("yo_o", xs))
    return C.done()


def _bias_tables(rpb_l, j):
    H = 8
    tab = np.full((H, NTAB, 128, 128), NEG, np.float32)
    kp = np.arange(128)
    qi = np.arange(128)
    ck = (kp % 64)[:, None]
    cq = (qi % 64)[None, :]
    cs = np.clip(cq - 8, 0, 48)
    colok = (ck >= cs) & (ck <= cs + 15)
    dc = ck - cq + 15
    for cls, (r_loc, win_lo) in enumerate(((0, -4), (2, -4), (8, -4), (62, -6), (60, -4))):
        r = 64 * j + r_loc
        rq = r + (qi // 64)[None, :]
        rs = np.clip(rq - 4, 0, 248)
        for kt in range(CLS_NT[cls]):
            rk = r + win_lo + 2 * kt + (kp // 64)[:, None]
            ok = colok & (rk >= rs) & (rk <= rs + 7)
            dr = rk - rq + 7
            drc = np.clip(dr, 0, 14)
            dcc = np.clip(dc, 0, 30)
            vals = rpb_l[:, drc, dcc]
            tab[:, CLS_OFF[cls] + kt] = np.where(ok[None], vals, NEG)
    t = tab.reshape(4, 2, NTAB, 128, 128).transpose(0, 3, 1, 2, 4)
    return np.ascontiguousarray(t)


_NC_CACHE = {}


def _get_nc(name):
    if name not in _NC_CACHE:
        _NC_CACHE[name] = {"k1": lambda: build_k1(True), "k5": lambda: build_k1(False), "k2": build_k2,
                           "k3": build_k3, "k4": build_k4}[name]()
    return _NC_CACHE[name]


def _run(name, in_maps):
    res = run_bass_kernel_spmd(_get_nc(name), in_maps, core_ids=list(range(NCORES)))
    return res.results


def _rep(v):
    return np.ascontiguousarray(np.broadcast_to(np.asarray(v, np.float32)[None, :], (128, v.shape[0])))


def kernel(x, emb_ln_g, emb_ln_b, w_in, rpb, conv_w, conv_b, lru_gate_w, lru_gate_b, lru_lambda,
           w_branch_attn, w_branch_lru, b_merge, w_out, ln_g, ln_b):
    f = lambda a: np.ascontiguousarray(np.asarray(a, np.float32))
    x = f(x)
    B = x.shape[0]
    y_sh = [f(x[c // 4, (c % 4) * NT:(c % 4 + 1) * NT]) for c in range(NCORES)]
    ident = np.eye(128, dtype=np.float32)
    for l in range(2):
        g_, b_ = (emb_ln_g, emb_ln_b) if l == 0 else (ln_g[l - 1], ln_b[l - 1])
        lng, lnb = _rep(f(g_)), _rep(f(b_))
        bm = f(np.asarray(b_merge[l]).reshape(2, 8, 128).transpose(2, 0, 1).reshape(128, 16))
        wl = f(w_in[l])
        r1 = _run("k1", [dict(yin=y_sh[c], lng=lng, lnb=lnb, ident=ident, w_in=wl, bm=bm) for c in range(NCORES)])
        bf = r1[0]["kT"].dtype
        kfull = [np.concatenate([np.zeros((512, 256), bf)] + [r1[b * 4 + j]["kT"] for j in range(4)]
                                + [np.zeros((512, 256), bf)], axis=1) for b in range(B)]
        vfull = [np.concatenate([np.zeros((256, 512), bf)] + [r1[b * 4 + j]["v"] for j in range(4)]
                                + [np.zeros((256, 512), bf)], axis=0) for b in range(B)]
        tabs = [_bias_tables(f(rpb[l]), j) for j in range(4)]
        in2 = []
        for c in range(NCORES):
            b, j = c // 4, c % 4
            lo = j * NT
            in2.append(dict(qT=r1[c]["qT"], kTh=np.ascontiguousarray(kfull[b][:, lo:lo + KROWS * 64]),
                            vh=np.ascontiguousarray(vfull[b][lo:lo + KROWS * 64]), szaT=r1[c]["szaT"], bt=tabs[j]))
        r2 = _run("k2", in2)
        in3 = []
        gwl = f(lru_gate_w[l])
        gbl = f(lru_gate_b[l])
        for c in range(NCORES):
            b, j = c // 4, c % 4
            ch = slice(j * 128, (j + 1) * 128)
            uT = np.concatenate([r1[b * 4 + jj]["uT"][ch] for jj in range(4)], axis=1)
            szbT = np.concatenate([r1[b * 4 + jj]["szbT"][ch] for jj in range(4)], axis=1)
            cw = np.concatenate([f(conv_w[l])[:, ch].T, f(conv_b[l])[ch][:, None]], axis=1)
            gw = np.zeros((128, 4, 128), np.float32)
            gb = np.zeros((128, 4), np.float32)
            for d in range(2):
                for g in range(2):
                    for bb in range(2):
                        gw[bb * 64:(bb + 1) * 64, d * 2 + g, bb * 64:(bb + 1) * 64] = gwl[d, g, 2 * j + bb]
                        gb[bb * 64:(bb + 1) * 64, d * 2 + g] = gbl[d, g, 2 * j + bb]
            lam = f(f(lru_lambda[l])[:, ch].T)
            in3.append(dict(uT=f(uT), szbT=f(szbT), cw=f(cw), gw=gw, gb=gb, lam=lam))
        r3 = _run("k3", in3)
        in4 = []
        wba, wbb, wo = f(w_branch_attn[l]), f(w_branch_lru[l]), f(w_out[l])
        for c in range(NCORES):
            b, j = c // 4, c % 4
            ybT = np.concatenate([r3[b * 4 + jj]["ybT"][:, j * NT:(j + 1) * NT] for jj in range(4)], axis=0)
            in4.append(dict(yaT=r2[c]["yaT"], ybT=np.ascontiguousarray(ybT), sgaT=r1[c]["sgaT"], sgbT=r1[c]["sgbT"],
                            xres=r1[c]["xres"], w_ba=wba, w_bb=wbb, w_out=wo))
        r4 = _run("k4", in4)
        y_sh = [r4[c]["yout"] for c in range(NCORES)]
    lng, lnb = _rep(f(ln_g[1])), _rep(f(ln_b[1]))
    r5 = _run("k5", [dict(yin=y_sh[c], lng=lng, lnb=lnb) for c in range(NCORES)])
    out = np.empty((B, SEQ, D), np.float32)
    for c in range(NCORES):
        out[c // 4, (c % 4) * NT:(c % 4 + 1) * NT] = r5[c]["xres"]
    return out
```

```python
import numpy as np
from contextlib import ExitStack
import concourse.bass as bass
import concourse.mybir as mybir
from concourse.bass_utils import run_bass_kernel_spmd

F32 = mybir.dt.float32
BF16 = mybir.dt.bfloat16
AF = mybir.ActivationFunctionType
ALU = mybir.AluOpType

NCORES = 8
NT = 4096
SEQ = 16384
D = 1024
ALPHA = 4.0 ** 0.25
NEG = -30000.0


class Sched:
    ENG = ("pe", "act", "dve", "pool", "sp")

    def __init__(self, nc, stack):
        self.nc = nc
        self.stack = stack
        self.sems = {}
        self.count = {}
        for e in self.ENG:
            self._sem("e:" + e)
        self.lastw = {}
        self.reads = {}
        self.known = {e: {} for e in self.ENG}
        self.stream = {e: [] for e in self.ENG}

    def _sem(self, name):
        if name not in self.sems:
            self.sems[name] = self.stack.enter_context(self.nc.semaphore("s%d" % len(self.sems)))
            self.count[name] = 0
        return self.sems[name]

    def op(self, eng, name, reads=(), writes=(), dma=False, semkey=None, signal=True, inc=16, **kw):
        fn = (name, kw)
        deps = {}

        def add(s, v):
            if v > deps.get(s, 0):
                deps[s] = v

        for k in reads:
            if k in self.lastw:
                add(*self.lastw[k])
        for k in writes:
            if k in self.lastw:
                add(*self.lastw[k])
            for s, v in self.reads.get(k, {}).items():
                add(s, v)
        waits = []
        for s, v in deps.items():
            if s == "e:pe" and eng == "pe":
                continue
            if self.known[eng].get(s, 0) >= v:
                continue
            self.known[eng][s] = v
            waits.append((s, v))
        if dma:
            sk = "d:" + str(semkey if semkey is not None else (list(writes) + list(reads))[0])
            self._sem(sk)
            self.count[sk] += inc
            tok = (sk, self.count[sk])
            emit_tok = (sk, inc)
        else:
            sk = "e:" + eng
            if signal:
                self.count[sk] += 1
                tok = (sk, self.count[sk])
                emit_tok = (sk, 1)
            else:
                assert eng == "pe"
                tok = (sk, self.count[sk] + 1)
                emit_tok = None
        for k in reads:
            d = self.reads.setdefault(k, {})
            if tok[1] > d.get(tok[0], 0):
                d[tok[0]] = tok[1]
        for k in writes:
            self.lastw[k] = tok
            self.reads[k] = {}
        self.stream[eng].append((waits, fn, emit_tok, dma))
        return tok

    def cond_dma(self, eng, cases, reads=(), writes=(), semkey=None):
        n = len(next(iter(cases.values())))
        assert all(len(v) == n for v in cases.values())
        return self.op(eng, "__cond__", reads=reads, writes=writes, dma=True, semkey=semkey, inc=16 * n, cases=cases)

    def barrier(self, exclude="d:cc_"):
        snap = [(s, v) for s, v in self.count.items() if v > 0 and not s.startswith(exclude)]
        for e in self.ENG:
            waits = []
            for s, v in snap:
                if self.known[e].get(s, 0) >= v:
                    continue
                self.known[e][s] = v
                waits.append((s, v))
            self.stream[e].append((waits, None, None, False))
        self.lastw = {k: t for k, t in self.lastw.items() if t[0].startswith(exclude)}
        self.reads = {}

    def finish(self, eng="sp"):
        waits = [(s, v) for s, v in self.count.items() if s.startswith("d:") and v > 0]
        self.stream[eng].append((waits, None, None, False))

    def build(self):
        nc = self.nc
        with nc.Block() as block:
            def run(ename, e):
                core = None
                for waits, fn, tok, dma in self.stream[ename]:
                    for s, v in waits:
                        e.wait_ge(self.sems[s], v)
                    if fn is None:
                        continue
                    if fn[0] == "__cond__":
                        if core is None:
                            core = e.partition_id()
                        for c, lst in fn[1]["cases"].items():
                            with e.If(core == c):
                                for (o, i) in lst:
                                    e.dma_start(out=o, in_=i).then_inc(self.sems[tok[0]], 16)
                        continue
                    ins = getattr(e, fn[0])(**fn[1])
                    if tok is not None:
                        ins.then_inc(self.sems[tok[0]], tok[1])

            @block.tensor
            def _(e):
                run("pe", e)

            @block.scalar
            def _(e):
                run("act", e)

            @block.vector
            def _(e):
                run("dve", e)

            @block.gpsimd
            def _(e):
                run("pool", e)

            @block.sync
            def _(e):
                run("sp", e)


class Ctx:
    def __init__(self):
        self.nc = bass.Bass("TRN2", target_bir_lowering=False)
        self.st = ExitStack()
        self.S = Sched(self.nc, self.st)
        self.pst = None
        self.uid = 0

    def din(self, name, shape, dt=F32):
        return self.nc.dram_tensor(name, list(shape), dt, kind="ExternalInput").ap()

    def dout(self, name, shape, dt=F32):
        return self.nc.dram_tensor(name, list(shape), dt, kind="ExternalOutput").ap()

    def dint(self, name, shape, dt=F32):
        return self.nc.dram_tensor(name, list(shape), dt, kind="Internal").ap()

    def phase(self):
        self.pst = ExitStack()
        return self.pst

    def sb(self, name, shape, dt=F32):
        self.uid += 1
        return self.pst.enter_context(self.nc.sbuf_tensor("%s_%d" % (name, self.uid), list(shape), dt))

    def ps(self, name, shape, dt=F32):
        self.uid += 1
        return self.pst.enter_context(self.nc.psum_tensor("%s_%d" % (name, self.uid), list(shape), dt))

    def done(self):
        self.S.finish()
        self.S.build()
        self.st.close()
        return self.nc


def _dma(S, eng, out, in_, reads=(), writes=(), semkey=None):
    return S.op(eng, "dma_start", reads=reads, writes=writes, dma=True, semkey=semkey, out=out, in_=in_)


GROUPS = [[0, 1, 2, 3], [4, 5, 6, 7]]
CLS_NT = [6, 6, 5, 6, 5]
CLS_OFF = [0, 6, 12, 17, 23]
NTAB = 28
KROWS = 72
NK = KROWS * 64


def tile_class(t):
    if t == 0:
        return 0, 0
    if t == 1:
        return 1, 1
    if t == 31:
        return 3, 30
    if t == 30:
        return 4, 30
    return 2, t


def phase_k1(C, T, yin, lng, lnb, xres, w_in=None, bm=None, do_proj=True):
    S = C.S
    with C.phase():
        lng_sb = C.sb("lng_sb", [128, D])
        lnb_sb = C.sb("lnb_sb", [128, D])
        NLS = 4
        yt = C.sb("yt", [128, NLS, D])
        xn = C.sb("xn", [128, NLS, D])
        stats = C.sb("stats", [128, NLS, 12])
        mv = C.sb("mv", [128, NLS, 2])
        rstd = C.sb("rstd", [128, NLS, 1])
        _dma(S, "sp", lng_sb[:], lng, writes=["lng"])
        _dma(S, "sp", lnb_sb[:], lnb, writes=["lnb"])
        if do_proj:
            qT, kT, vo, szaT, uT, szbT, sgaT, sgbT = (T[k] for k in ("qT", "kT", "v", "szaT", "uT", "szbT", "sgaT", "sgbT"))
            wbf = C.sb("wbf", [128, 8, 5120], BF16)
            wst = C.sb("wst", [128, 4, 1280])
            ident_f = C.sb("ident_f", [128, 128])
            identb = C.sb("identb", [128, 128], BF16)
            bm_sb = C.sb("bm_sb", [128, 16])
            xnb = C.sb("xnb", [128, 2, D], BF16)
            xT = C.sb("xT", [128, 2, 8, 512], BF16)
            NOF, NOB = 4, 8
            ostf = C.sb("ostf", [128, NOF, 512])
            ostb = C.sb("ostb", [128, NOB, 512], BF16)
            pst = [C.ps("pst%d" % i, [128, 8, 128], BF16) for i in range(2)]
            NPS = 6
            psm = [C.ps("psm%d" % i, [128, 512]) for i in range(NPS)]
            _dma(S, "sp", ident_f[:], T["ident"], writes=["identf"])
            _dma(S, "sp", bm_sb[:], bm, writes=["bm"])
            S.op("dve", "tensor_copy", reads=["identf"], writes=["identb"], out=identb[:], in_=ident_f[:])
            cast_engs = ["dve", "act", "act"]
            for i, (half, kc) in enumerate([(h, k) for h in range(4) for k in range(8)]):
                if True:
                    sl = i % 4
                    _dma(S, "sp", wst[:, sl, :], w_in[kc * 128:(kc + 1) * 128, half * 1280:(half + 1) * 1280],
                         writes=[("wst", sl)])
                    ce = cast_engs[i % 3]
                    o = wbf[:, kc, half * 1280:(half + 1) * 1280]
                    if ce == "act":
                        S.op("act", "activation", reads=[("wst", sl)], writes=[("wbf", kc, half)],
                             out=o, in_=wst[:, sl, :], func=AF.Copy)
                    else:
                        S.op(ce, "tensor_copy", reads=[("wst", sl)], writes=[("wbf", kc, half)], out=o, in_=wst[:, sl, :])

        cnt = {"f": 0, "b": 0, "ps": 0}
        hb_keys = []

        def ln_tile(g):
            s2 = g % NLS
            _dma(S, "sp", yt[:, s2, :], yin[g * 128:(g + 1) * 128, :], writes=[("yt", s2)])
            for hh in range(2):
                S.op("dve", "bn_stats", reads=[("yt", s2)], writes=[("stats", s2, hh)],
                     out=stats[:, s2, hh * 6:(hh + 1) * 6], in_=yt[:, s2, hh * 512:(hh + 1) * 512])
            S.op("dve", "bn_aggr", reads=[("stats", s2, 0), ("stats", s2, 1)], writes=[("mv", s2)],
                 out=mv[:, s2, :], in_=stats[:, s2, :])
            S.op("act", "activation", reads=[("mv", s2)], writes=[("rstd", s2)],
                 out=rstd[:, s2, :], in_=mv[:, s2, 1:2], func=AF.Sqrt, bias=1e-5, scale=1.0)
            S.op("dve", "reciprocal", reads=[("rstd", s2)], writes=[("rstd", s2)], out=rstd[:, s2, :], in_=rstd[:, s2, :])
            S.op("dve", "scalar_tensor_tensor", reads=[("mv", s2), ("rstd", s2)], writes=[("nb", s2)],
                 out=mv[:, s2, 1:2], in0=mv[:, s2, 0:1], scalar=-1.0, in1=rstd[:, s2, :], op0=ALU.mult, op1=ALU.mult)
            S.op("act", "activation", reads=[("yt", s2), ("nb", s2), ("rstd", s2)], writes=[("xn", s2)],
                 out=xn[:, s2, :], in_=yt[:, s2, :], func=AF.Identity, scale=rstd[:, s2, :], bias=mv[:, s2, 1:2])
            S.op("dve", "tensor_tensor", reads=[("xn", s2), "lng"], writes=[("xn", s2)],
                 out=xn[:, s2, :], in0=xn[:, s2, :], in1=lng_sb[:], op=ALU.mult)
            S.op("dve", "tensor_tensor", reads=[("xn", s2), "lnb"], writes=[("xn", s2)],
                 out=xn[:, s2, :], in0=xn[:, s2, :], in1=lnb_sb[:], op=ALU.add)
            _dma(S, "sp", xres[g * 128:(g + 1) * 128, :], xn[:, s2, :], reads=[("xn", s2)], semkey=("xn_o", s2))

        def tr_tile(g):
            s4 = g % NLS
            s2 = g % 2
            tb, tt = g // 4, g % 4
            bs = tb % 2
            S.op("act", "activation", reads=[("xn", s4)], writes=[("xnb", s2)],
                 out=xnb[:, s2, :], in_=xn[:, s4, :], func=AF.Copy)
            for kc in range(8):
                S.op("pe", "transpose", reads=[("xnb", s2), "identb"], writes=[("pst", s2)], signal=(kc == 7),
                     out=pst[s2][:, kc, :], in_=xnb[:, s2, kc * 128:(kc + 1) * 128], identity=identb[:])
            S.op("act", "activation", reads=[("pst", s2)], writes=[("xT", bs, tt)],
                 out=xT[:, bs, :, tt * 128:(tt + 1) * 128], in_=pst[s2][:, :, :], func=AF.Copy)

        def evac(kind, c, bank, tb):
            src = psm[bank][:, :]
            tsl = slice(tb * 512, (tb + 1) * 512)
            if kind in ("q", "k"):
                sl = cnt["b"] % NOB
                cnt["b"] += 1
                dst = ostb[:, sl, :]
                if kind == "q":
                    S.op("dve", "tensor_scalar", reads=[("psm", bank)], writes=[("ostb", sl)],
                         out=dst, in0=src, scalar1=0.125, scalar2=None, op0=ALU.mult)
                    tgt = qT
                else:
                    S.op("dve", "tensor_copy", reads=[("psm", bank)], writes=[("ostb", sl)], out=dst, in_=src)
                    tgt = kT
                    if tb == 0:
                        _dma(S, "sp", T["hb"][c * 128:(c + 1) * 128, 0:256], ostb[:, sl, 0:256], reads=[("ostb", sl)],
                             writes=[("hb_d", "kt", c)], semkey=("hbs", "kt", c))
                        hb_keys.append(("hb_d", "kt", c))
                    if tb == NT // 512 - 1:
                        _dma(S, "sp", T["hb"][c * 128:(c + 1) * 128, 256:512], ostb[:, sl, 256:512], reads=[("ostb", sl)],
                             writes=[("hb_d", "kb", c)], semkey=("hbs", "kb", c))
                        hb_keys.append(("hb_d", "kb", c))
                _dma(S, "sp", tgt[c * 128:(c + 1) * 128, tsl], dst, reads=[("ostb", sl)], semkey=("ostb", sl))
                return
            if kind == "u":
                sl = cnt["f"] % NOF
                cnt["f"] += 1
                dst = ostf[:, sl, :]
                S.op("dve", "tensor_copy", reads=[("psm", bank)], writes=[("ostf", sl)], out=dst, in_=src)
                _dma(S, "sp", uT[tb][c * 128:(c + 1) * 128, :], dst, reads=[("ostf", sl)], writes=[("uT_d", tb, c)],
                     semkey=("ostf", sl))
                return
            sl = cnt["b"] % NOB
            cnt["b"] += 1
            dst = ostb[:, sl, :]
            if kind in ("za", "zb"):
                S.op("act", "activation", reads=[("psm", bank)], writes=[("ostb", sl)], out=dst, in_=src, func=AF.Silu)
                tgt = szaT if kind == "za" else szbT
            else:
                col = c if kind == "ga" else 8 + c
                S.op("act", "activation", reads=[("psm", bank), "bm"], writes=[("ostb", sl)],
                     out=dst, in_=src, func=AF.Sigmoid, bias=bm_sb[:, col:col + 1])
                tgt = sgaT if kind == "ga" else sgbT
            _dma(S, "sp", tgt[c * 128:(c + 1) * 128, tsl], dst, reads=[("ostb", sl)], semkey=("ostb", sl))

        fam = ([("q", i) for i in range(4)] + [("k", i) for i in range(4)] + [("v", i) for i in range(4)]
               + [("za", i) for i in range(4)] + [("u", i) for i in range(4)] + [("zb", i) for i in range(4)]
               + [("ga", i) for i in range(8)] + [("gb", i) for i in range(8)])

        def proj_block(tb):
            bs = tb % 2
            xkeys = [("xT", bs, tt) for tt in range(4)]
            nxt = 0

            def v_part():
                for tt in range(4):
                    bank = cnt["ps"] % NPS
                    cnt["ps"] += 1
                    for kc in range(8):
                        S.op("pe", "matmul", reads=[("xT", bs, tt), ("wbf", kc, 0), ("wbf", kc, 1)], writes=[("psm", bank)],
                             signal=(kc == 7),
                             out=psm[bank][:, :], lhsT=xT[:, bs, kc, tt * 128:(tt + 1) * 128], rhs=wbf[:, kc, 1024:1536],
                             start=(kc == 0), stop=(kc == 7))
                    sl = cnt["b"] % NOB
                    cnt["b"] += 1
                    S.op("dve", "tensor_copy", reads=[("psm", bank)], writes=[("ostb", sl)], out=ostb[:, sl, :], in_=psm[bank][:, :])
                    g = tb * 4 + tt
                    _dma(S, "sp", vo[g * 128:(g + 1) * 128, :], ostb[:, sl, :], reads=[("ostb", sl)], semkey=("ostb", sl))
                    if g < 2 or g >= NT // 128 - 2:
                        hr0 = g * 128 if g < 2 else 256 + (g - (NT // 128 - 2)) * 128
                        _dma(S, "sp", T["hb"][512 + hr0:512 + hr0 + 128, :], ostb[:, sl, :], reads=[("ostb", sl)],
                             writes=[("hb_d", "v", g)], semkey=("hbs", "v", g))
                        hb_keys.append(("hb_d", "v", g))

            last = (tb == NT // 512 - 1)
            if last:
                v_part()
            for occ, (kind, ci) in enumerate(fam):
                if kind == "v":
                    continue
                if tb == NT // 512 - 1 and occ == 12:
                    S.op("pool", "collective_compute", dma=True, inc=1, semkey="cc_a", reads=list(hb_keys), writes=["hbg"],
                         kind="AllGather", op=ALU.bypass, replica_groups=GROUPS, ins=[T["hb"]], outs=[T["hbg"]])
                if tb + 1 < NT // 512 and occ in (2, 12, 22, 32):
                    ln_tile((tb + 1) * 4 + nxt)
                    nxt += 1
                bank = cnt["ps"] % NPS
                cnt["ps"] += 1
                half = occ // 10
                for kc in range(8):
                    S.op("pe", "matmul", reads=xkeys + [("wbf", kc, half)], writes=[("psm", bank)], signal=(kc == 7),
                         out=psm[bank][:, :], lhsT=wbf[:, kc, occ * 128:(occ + 1) * 128], rhs=xT[:, bs, kc, :],
                         start=(kc == 0), stop=(kc == 7))
                evac(kind, ci, bank, tb)
            if not last:
                v_part()
            if tb + 1 < NT // 512:
                for tt in range(4):
                    tr_tile((tb + 1) * 4 + tt)

        NTILE = NT // 128
        if not do_proj:
            for g in range(NTILE):
                ln_tile(g)
        else:
            for tt in range(4):
                ln_tile(tt)
                tr_tile(tt)
            for tb in range(NT // 512):
                proj_block(tb)
    S.barrier()


def phase_ln(C, yin, lng, lnb, xout):
    S = C.S
    G, NS = 4, 3
    with C.phase():
        lng_sb = C.sb("lng_sb", [128, D])
        lnb_sb = C.sb("lnb_sb", [128, D])
        yt = C.sb("yt", [128, NS, G, D])
        xn = C.sb("xn", [128, NS, G, D])
        stats = C.sb("stats", [128, NS, G, 12])
        mv = C.sb("mv", [128, NS, G, 2])
        rstd = C.sb("rstd", [128, NS, G, 1])
        _dma(S, "sp", lng_sb[:], lng, writes=["lng"])
        _dma(S, "sp", lnb_sb[:], lnb, writes=["lnb"])
        for gi in range(NT // (128 * G)):
            s = gi % NS
            rows = slice(gi * 128 * G, (gi + 1) * 128 * G)
            _dma(S, "sp", yt[:, s, :, :], yin[rows, :].rearrange("(t p) d -> p t d", p=128), writes=[("yt", s)])
            for t in range(G):
                for hh in range(2):
                    S.op("dve", "bn_stats", reads=[("yt", s)], writes=[("stats", s, t, hh)],
                         out=stats[:, s, t, hh * 6:(hh + 1) * 6], in_=yt[:, s, t, hh * 512:(hh + 1) * 512])
                S.op("dve", "bn_aggr", reads=[("stats", s, t, 0), ("stats", s, t, 1)], writes=[("mv", s, t)],
                     out=mv[:, s, t, :], in_=stats[:, s, t, :])
                S.op("act", "activation", reads=[("mv", s, t)], writes=[("rstd", s, t)],
                     out=rstd[:, s, t, :], in_=mv[:, s, t, 1:2], func=AF.Sqrt, bias=1e-5, scale=1.0)
                S.op("dve", "reciprocal", reads=[("rstd", s, t)], writes=[("rstd", s, t)], out=rstd[:, s, t, :],
                     in_=rstd[:, s, t, :])
                S.op("dve", "scalar_tensor_tensor", reads=[("mv", s, t), ("rstd", s, t)], writes=[("nb", s, t)],
                     out=mv[:, s, t, 1:2], in0=mv[:, s, t, 0:1], scalar=-1.0, in1=rstd[:, s, t, :], op0=ALU.mult, op1=ALU.mult)
                S.op("act", "activation", reads=[("yt", s), ("nb", s, t), ("rstd", s, t)], writes=[("xn", s, t)],
                     out=xn[:, s, t, :], in_=yt[:, s, t, :], func=AF.Identity, scale=rstd[:, s, t, :], bias=mv[:, s, t, 1:2])
                S.op("dve", "tensor_tensor", reads=[("xn", s, t), "lng"], writes=[("xn", s, t)],
                     out=xn[:, s, t, :], in0=xn[:, s, t, :], in1=lng_sb[:], op=ALU.mult)
                S.op("dve", "tensor_tensor", reads=[("xn", s, t), "lnb"], writes=[("xn", s, t)],
                     out=xn[:, s, t, :], in0=xn[:, s, t, :], in1=lnb_sb[:], op=ALU.add)
            _dma(S, "sp", xout[rows, :].rearrange("(t p) d -> p t d", p=128), xn[:, s, :, :],
                 reads=[("xn", s, t) for t in range(G)], semkey=("xn_o", s))
    S.barrier()


def exchange_after_k1(C, T):
    S = C.S
    for tb in range(8):
        S.op("pool", "collective_compute", dma=True, inc=1, semkey="cc_a", writes=[("ug", tb)], kind="AllGather",
             op=ALU.bypass, replica_groups=GROUPS, ins=[T["uT"][tb]], outs=[T["ug"][tb]])


def exchange_after_k3(C, T):
    pass


def phase_k2(C, T, bt):
    S = C.S
    qT, kT, vo, szaT, yaT = T["qT"], T["kT"], T["v"], T["szaT"], T["yaT"]
    hbg, zk, zv = T["hbg"], T["zk"], T["zv"]
    with C.phase():
        q_sb = C.sb("q_sb", [128, 2, NT], BF16)
        k_sb = C.sb("k_sb", [128, 2, NK], BF16)
        v_sb = C.sb("v_sb", [128, 2, NK // 128, 128], BF16)
        sza_sb = C.sb("sza_sb", [128, 2, NT], BF16)
        bt_raw = C.sb("bt_raw", [128, 2, NTAB, 128])
        bt_sb = C.sb("bt_sb", [128, 2, NTAB, 128])
        ya_sb = C.sb("ya_sb", [128, 2, NT], BF16)
        ones_b = C.sb("ones_b", [128, 64], BF16)
        NU = 3
        E = C.sb("E", [128, NU, 768])
        Pm = C.sb("Pm", [128, NU, 768], BF16)
        rden = C.sb("rden", [128, 2, 128])
        tmp = C.sb("tmp", [128, 2, 128])
        spa = [C.ps("spa%d" % i, [128, 512]) for i in range(NU)]
        spb = [C.ps("spb%d" % i, [128, 256]) for i in range(NU)]
        sps = [[spa[i][:, :], spb[i][:, :]] for i in range(NU)]
        nd = [C.ps("nd0", [128, 128])] * 2
        dn = [C.ps("dn0", [128, 128])] * 2

        S.op("dve", "memset", writes=["ones"], ap=ones_b[:], constant=1.0)

        def load_pair(p):
            s = p % 2
            prow = slice(p * 128, (p + 1) * 128)
            _dma(S, "sp", q_sb[:, s, :], qT[prow, :], writes=[("q", s)])
            _dma(S, "sp", k_sb[:, s, 256:256 + NT], kT[prow, :], writes=[("k", s)], semkey=("kmain", s))
            _dma(S, "sp", v_sb[:, s, 2:34, :], vo[:, prow].rearrange("(t p) f -> p t f", p=128),
                 writes=[("v", s)], semkey=("vmain", s))
            cases = {}
            for c in range(NCORES):
                j = c % 4
                lst = []
                lst.append((k_sb[:, s, 0:256], zk[:, :] if j == 0 else hbg[(j - 1) * 1024 + p * 128:(j - 1) * 1024 + (p + 1) * 128, 256:512]))
                lst.append((k_sb[:, s, 256 + NT:NK], zk[:, :] if j == 3 else hbg[(j + 1) * 1024 + p * 128:(j + 1) * 1024 + (p + 1) * 128, 0:256]))
                lst.append((v_sb[:, s, 0:2, :], (zv[:, :] if j == 0 else hbg[(j - 1) * 1024 + 768:(j - 1) * 1024 + 1024, prow]).rearrange("(t p) f -> p t f", p=128)))
                lst.append((v_sb[:, s, 34:36, :], (zv[:, :] if j == 3 else hbg[(j + 1) * 1024 + 512:(j + 1) * 1024 + 768, prow]).rearrange("(t p) f -> p t f", p=128)))
                cases[c] = lst
            S.cond_dma("sp", cases, reads=["hbg"], writes=[("k", s), ("v", s)], semkey=("halo", s))
            _dma(S, "sp", sza_sb[:, s, :], szaT[prow, :], writes=[("sza", s)])
            _dma(S, "sp", bt_raw[:], bt[p], writes=["bt_raw"])

        load_pair(0)
        torder = list(range(32))
        units = [(t, hh) for t in torder for hh in range(2)]

        def emit_qk(idx, s):
            t, hh = units[idx]
            cls, vt0 = tile_class(t)
            nkt = CLS_NT[cls]
            hs = idx % NU
            pr = slice(hh * 64, (hh + 1) * 64)
            for kt in range(nkt):
                bank = sps[hs][kt // 4]
                co = (kt % 4) * 128
                kcol = (vt0 + kt) * 128
                last = (kt == nkt - 1) or (kt == 3)
                S.op("pe", "matmul", reads=[("k", s), ("q", s)],
                     writes=[("sps", hs, kt // 4)], signal=last,
                     out=bank[:, co:co + 128], lhsT=k_sb[pr, s, kcol:kcol + 128],
                     rhs=q_sb[pr, s, t * 128:(t + 1) * 128], start=True, stop=True)

        def emit_rest(idx, s):
            t, hh = units[idx]
            cls, vt0 = tile_class(t)
            nkt = CLS_NT[cls]
            toff = CLS_OFF[cls]
            par = t % 2
            hs = idx % NU
            pr = slice(hh * 64, (hh + 1) * 64)
            n0 = min(nkt, 4) * 128
            S.op("act", "activation", reads=[("sps", hs, 0)], writes=[("E", hs, 0)],
                 out=E[:, hs, 0:n0], in_=sps[hs][0][:, 0:n0], func=AF.Exp)
            n1 = (nkt - 4) * 128
            S.op("act", "activation", reads=[("sps", hs, 1)], writes=[("E", hs, 1)],
                 out=E[:, hs, 512:512 + n1], in_=sps[hs][1][:, 0:n1], func=AF.Exp)
            S.op("dve", "tensor_tensor", reads=[("E", hs, 0), ("bt", hh)], writes=[("Pm", hs, 0)],
                 out=Pm[:, hs, 0:512], in0=E[:, hs, 0:512],
                 in1=bt_sb[:, hh, toff:toff + 4, :].rearrange("p a b -> p (a b)"), op=ALU.mult)
            S.op("dve", "tensor_tensor", reads=[("E", hs, 1), ("bt", hh)], writes=[("Pm", hs, 1)],
                 out=Pm[:, hs, 512:nkt * 128], in0=E[:, hs, 512:nkt * 128],
                 in1=bt_sb[:, hh, toff + 4:toff + nkt, :].rearrange("p a b -> p (a b)"), op=ALU.mult)
            for kt in range(nkt):
                S.op("pe", "matmul", reads=[("Pm", hs, kt // 4), ("v", s)],
                     writes=["nd"], signal=False,
                     out=nd[par][pr, 0:128], lhsT=v_sb[:, s, vt0 + kt, hh * 64:(hh + 1) * 64],
                     rhs=Pm[:, hs, kt * 128:(kt + 1) * 128], start=(kt == 0), stop=(kt == nkt - 1))
                S.op("pe", "matmul", reads=[("Pm", hs, kt // 4), "ones"], writes=["dn"], signal=(kt == nkt - 1),
                     out=dn[par][pr, 0:128], lhsT=ones_b[:, :],
                     rhs=Pm[:, hs, kt * 128:(kt + 1) * 128], start=(kt == 0), stop=(kt == nkt - 1))
            if hh == 1:
                S.op("dve", "reciprocal", reads=["dn"], writes=[("rden", par)],
                     out=rden[:, par, :], in_=dn[par][:, 0:128])
                S.op("dve", "tensor_tensor", reads=["nd", ("rden", par)], writes=[("tmp", par)],
                     out=tmp[:, par, :], in0=nd[par][:, 0:128], in1=rden[:, par, :], op=ALU.mult)
                S.op("pool", "tensor_tensor", reads=[("tmp", par), ("sza", s)], writes=[("ya", s)],
                     out=ya_sb[:, s, t * 128:(t + 1) * 128], in0=tmp[:, par, :],
                     in1=sza_sb[:, s, t * 128:(t + 1) * 128], op=ALU.mult)

        for p in range(4):
            s = p % 2
            for hh in range(2):
                S.op("act", "activation", reads=["bt_raw"], writes=[("bt", hh)],
                     out=bt_sb[:, hh, :, :], in_=bt_raw[:, hh, :, :], func=AF.Exp)
            if p + 1 < 4:
                load_pair(p + 1)
            emit_qk(0, s)
            emit_qk(1, s)
            for idx in range(len(units)):
                if idx + 2 < len(units):
                    emit_qk(idx + 2, s)
                emit_rest(idx, s)
            _dma(S, "sp", yaT[p * 128:(p + 1) * 128, :], ya_sb[:, s, :], reads=[("ya", s)], semkey=("ya_o", s))
    S.barrier()


def phase_k3(C, T, cw, gw, gb, lam):
    S = C.S
    ug, hT = T["ug"], T["hT"]
    CH = 2048
    NCH = SEQ // CH
    with C.phase():
        u_sb = C.sb("u_sb", [128, SEQ + 3])
        uc = C.sb("uc", [128, SEQ])
        cw_sb = C.sb("cw_sb", [128, 5])
        gw_sb = C.sb("gw_sb", [128, 4, 128])
        gb_sb = C.sb("gb_sb", [128, 4])
        lam_sb = C.sb("lam_sb", [128, 2])
        cs = C.sb("cs", [128, 12, 2])
        csp = C.sb("csp", [128, 2])
        csp2 = C.sb("csp2", [128, 2])
        r_sb = C.sb("r_sb", [128, 1, CH])
        i_sb = C.sb("i_sb", [128, 1, CH])
        a_sb = C.sb("a_sb", [128, 2, CH])
        m_sb = C.sb("m_sb", [128, 2, CH])
        hr_sb = C.sb("hr_sb", [128, 1, CH])
        yb_sb = C.sb("yb_sb", [128, 2, CH], BF16)
        carry = C.sb("carry", [128, 2])
        NH = CH // 512
        psr = [[C.ps("psr%d_%d" % (i, j), [128, 512]) for j in range(NH)] for i in range(1)]
        psi = [[C.ps("psi%d_%d" % (i, j), [128, 512]) for j in range(NH)] for i in range(1)]

        _dma(S, "sp", cw_sb[:], cw, writes=["cw"])
        _dma(S, "sp", gw_sb[:], gw, writes=["gw"])
        _dma(S, "sp", gb_sb[:], gb, writes=["gb"])
        _dma(S, "sp", lam_sb[:], lam, writes=["lam"])
        S.op("dve", "memset", writes=["upad0"], ap=u_sb[:, 0:2], constant=0.0)
        S.op("dve", "memset", writes=["upad1"], ap=u_sb[:, SEQ + 2:SEQ + 3], constant=0.0)
        LCH = 2048
        NL = SEQ // LCH
        for c in range(NL):
            i, off = (c * LCH) // NT, (c * LCH) % NT
            cases = {}
            for cc in range(NCORES):
                j = cc % 4
                cases[cc] = [(u_sb[:, 2 + c * LCH + q * 512:2 + c * LCH + (q + 1) * 512],
                              ug[off // 512 + q][i * 512 + j * 128:i * 512 + (j + 1) * 128, :]) for q in range(LCH // 512)]
            S.cond_dma("sp", cases, reads=[("ug", off // 512 + q) for q in range(LCH // 512)], writes=[("u", c)],
                       semkey=("u", c % 2))

        V = lambda i: cs[:, i, :]
        k = dict(reads=["cs", "lam"], writes=["cs"])
        S.op("dve", "tensor_scalar", out=V(8), in0=lam_sb[:], scalar1=-1.0, scalar2=None, op0=ALU.mult, **k)
        S.op("dve", "tensor_tensor", out=V(0), in0=V(8), in1=lam_sb[:], op=ALU.max, **k)
        S.op("act", "activation", out=V(1), in_=V(0), func=AF.Exp, scale=-1.0, **k)
        S.op("dve", "tensor_scalar", out=V(2), in0=V(1), scalar1=2.0, scalar2=None, op0=ALU.add, **k)
        S.op("dve", "reciprocal", out=V(2), in_=V(2), **k)
        S.op("dve", "tensor_tensor", out=V(3), in0=V(1), in1=V(2), op=ALU.mult, **k)
        S.op("dve", "tensor_tensor", out=V(4), in0=V(3), in1=V(3), op=ALU.mult, **k)
        S.op("dve", "tensor_scalar", out=V(5), in0=V(4), scalar1=1.0 / 13, scalar2=1.0 / 11, op0=ALU.mult, op1=ALU.add, **k)
        for cden in (9.0, 7.0, 5.0, 3.0, 1.0):
            S.op("dve", "tensor_tensor", out=V(5), in0=V(5), in1=V(4), op=ALU.mult, **k)
            S.op("dve", "tensor_scalar", out=V(5), in0=V(5), scalar1=1.0 / cden, scalar2=None, op0=ALU.add, **k)
        S.op("dve", "tensor_tensor", out=V(5), in0=V(5), in1=V(3), op=ALU.mult, **k)
        S.op("dve", "tensor_scalar", out=V(6), in0=lam_sb[:], scalar1=-1.0, scalar2=0.0, op0=ALU.mult, op1=ALU.max, **k)
        S.op("dve", "scalar_tensor_tensor", out=V(7), in0=V(5), scalar=2.0, in1=V(6), op0=ALU.mult, op1=ALU.add, **k)
        S.op("dve", "tensor_scalar", reads=["cs"], writes=["csp"], out=csp[:], in0=V(7), scalar1=-8.0, scalar2=None, op0=ALU.mult)
        S.op("dve", "tensor_scalar", reads=["cs"], writes=["csp2"], out=csp2[:], in0=V(7), scalar1=-16.0, scalar2=None, op0=ALU.mult)

        def conv_chunk(c):
            lo = c * LCH
            rd = [("u", c), "cw", "upad0", "upad1"] + ([("u", c - 1)] if c > 0 else []) + ([("u", c + 1)] if c + 1 < NL else [])
            o = uc[:, lo:lo + LCH]
            S.op("dve", "tensor_scalar", reads=rd, writes=[("uc", c)], out=o, in0=u_sb[:, lo:lo + LCH],
                 scalar1=cw_sb[:, 0:1], scalar2=cw_sb[:, 4:5], op0=ALU.mult, op1=ALU.add)
            for j in range(1, 4):
                S.op("dve", "scalar_tensor_tensor", reads=rd + [("uc", c)], writes=[("uc", c)], out=o,
                     in0=u_sb[:, lo + j:lo + j + LCH], scalar=cw_sb[:, j:j + 1], in1=o, op0=ALU.mult, op1=ALU.add)
        conv_chunk(0)
        conv_chunk(1)
        UCK = [("uc", c) for c in range(NL)]
        UK = [("u", c) for c in range(NL)]

        def gates(d, n, it):
            sl = it % 2
            lo = n * CH
            uck = ("uc", lo // LCH)
            for g, pst in ((0, psr), (1, psi)):
                for hf in range(NH):
                    S.op("pe", "matmul", reads=["gw", uck], writes=[("ps", g, hf)],
                         out=pst[0][hf][:, :], lhsT=gw_sb[:, d * 2 + g, :],
                         rhs=uc[:, lo + hf * 512:lo + (hf + 1) * 512], start=True, stop=True)
            for hf in range(NH):
                fs = slice(hf * 512, (hf + 1) * 512)
                S.op("act", "activation", reads=[("ps", 0, hf), "gb"], writes=[("r", hf)],
                     out=r_sb[:, 0, fs], in_=psr[0][hf][:, :], func=AF.Sigmoid, bias=gb_sb[:, d * 2:d * 2 + 1])
            for hf in range(NH):
                fs = slice(hf * 512, (hf + 1) * 512)
                S.op("act", "activation", reads=[("ps", 1, hf), "gb"], writes=[("i", hf)],
                     out=i_sb[:, 0, fs], in_=psi[0][hf][:, :], func=AF.Sigmoid, bias=gb_sb[:, d * 2 + 1:d * 2 + 2])
            rk = [("r", hf) for hf in range(NH)]
            ik = [("i", hf) for hf in range(NH)]
            S.op("act", "activation", reads=rk + ["csp"], writes=[("a", sl)],
                 out=a_sb[:, sl, :], in_=r_sb[:, 0, :], func=AF.Exp, scale=csp[:, d:d + 1])
            S.op("dve", "tensor_tensor", reads=[("a", sl)], writes=[("m", sl)],
                 out=m_sb[:, sl, :], in0=a_sb[:, sl, :], in1=a_sb[:, sl, :], op=ALU.mult)
            S.op("act", "activation", reads=[("m", sl)], writes=[("m", sl)], out=m_sb[:, sl, :], in_=m_sb[:, sl, :],
                 func=AF.Sqrt, scale=-1.0, bias=1.0)
            S.op("dve", "tensor_tensor", reads=ik + [uck], writes=ik, out=i_sb[:, 0, :], in0=i_sb[:, 0, :],
                 in1=uc[:, lo:lo + CH], op=ALU.mult)
            first = (d == 0 and n == 0) or (d == 1 and n == NCH - 1)
            if first:
                col = 0 if d == 0 else CH - 1
                S.op("dve", "memset", reads=[("m", sl)], writes=[("m", sl)], ap=m_sb[:, sl, col:col + 1], constant=1.0)
            S.op("dve", "tensor_tensor", reads=[("m", sl)] + ik, writes=[("m", sl)], out=m_sb[:, sl, :],
                 in0=m_sb[:, sl, :], in1=i_sb[:, 0, :], op=ALU.mult)
            return sl

        it = 0
        for n in range(NCH):
            sl = gates(0, n, it)
            if n + 2 < NL:
                conv_chunk(n + 2)
            it += 1
            lo = n * CH
            init = 0.0 if n == 0 else u_sb[:, 2 + lo - 1:2 + lo]
            S.op("dve", "tensor_tensor_scan", reads=[("a", sl), ("m", sl), "hf"], writes=["hf", ("u", n)],
                 out=u_sb[:, 2 + lo:2 + lo + CH], data0=a_sb[:, sl, :], data1=m_sb[:, sl, :], initial=init,
                 op0=ALU.mult, op1=ALU.add)
        for n in range(NCH - 1, -1, -1):
            sl = gates(1, n, it)
            it += 1
            lo = n * CH
            init = 0.0 if n == NCH - 1 else carry[:, 0:1]
            S.op("dve", "tensor_tensor_scan", reads=[("a", sl), ("m", sl), "carry"], writes=["hr"],
                 out=hr_sb[:, 0, ::-1], data0=a_sb[:, sl, ::-1], data1=m_sb[:, sl, ::-1], initial=init,
                 op0=ALU.mult, op1=ALU.add)
            S.op("dve", "tensor_copy", reads=["hr"], writes=["carry"], out=carry[:, 0:1], in_=hr_sb[:, 0, 0:1])
            S.op("dve" if (n % 2 == 0) else "pool", "tensor_tensor", reads=["hr", "hf"], writes=[("yb", sl)],
                 out=yb_sb[:, sl, :], in0=hr_sb[:, 0, :], in1=u_sb[:, 2 + lo:2 + lo + CH], op=ALU.add)
            _dma(S, "sp", hT[lo // NT][:, lo % NT:lo % NT + CH], yb_sb[:, sl, :], reads=[("yb", sl)],
                 writes=[("hT_d", n)], semkey=("yb_o", sl))
            if lo % NT == 0:
                i4 = lo // NT
                S.op("pool", "collective_compute", dma=True, inc=1, semkey="cc_h",
                     reads=[("hT_d", n + q) for q in range(NT // CH)], writes=[("hg", i4)], kind="AllGather",
                     op=ALU.bypass, replica_groups=GROUPS, ins=[hT[i4]], outs=[T["hg"][i4]])
    S.barrier()


def phase_k4(C, T, w_ba, w_bb, w_out, yout, fin=None):
    S = C.S
    yaT, hg, szbT, sgaT, sgbT, xres = T["yaT"], T["hg"], T["szbT"], T["sgaT"], T["sgbT"], T["xres"]
    with C.phase():
        wba = C.sb("wba", [128, 4, D], BF16)
        wbb = C.sb("wbb", [128, 4, D], BF16)
        wo = C.sb("wo", [128, 8, D], BF16)
        wst = C.sb("wst", [128, 2, D])
        ya = C.sb("ya", [128, 2, 4, 512], BF16)
        hb = C.sb("hb", [128, 2, 4, 512], BF16)
        szb = C.sb("szb", [128, 2, 4, 512], BF16)
        yb = C.sb("yb", [128, 2, 4, 512], BF16)
        sga = C.sb("sga", [128, 4, 512], BF16)
        sgb = C.sb("sgb", [128, 4, 512], BF16)
        t1 = C.sb("t1", [128, 2, 512])
        t2 = C.sb("t2", [128, 2, 512])
        mT = C.sb("mT", [128, 2, 8, 512], BF16)
        xr = C.sb("xr", [128, 2, D])
        yo = C.sb("yo", [128, 2, D])
        pa = [C.ps("pa%d" % i, [128, 512]) for i in range(2)]
        pb = [C.ps("pb%d" % i, [128, 512]) for i in range(2)]
        po = [C.ps("po%d" % i, [128, 512]) for i in range(4)]
        if fin is not None:
            lng_sb = C.sb("lng_sb", [128, D])
            lnb_sb = C.sb("lnb_sb", [128, D])
            stats = C.sb("stats", [128, 2, 12])
            mv = C.sb("mv", [128, 2, 2])
            rstd = C.sb("rstd", [128, 2, 1])
            _dma(S, "sp", lng_sb[:], fin[0], writes=["lng"])
            _dma(S, "sp", lnb_sb[:], fin[1], writes=["lnb"])

        def load_w(lst, i):
          for (src, dst, nk, key) in lst:
            for kc in range(nk):
                sl = i % 2
                _dma(S, "sp", wst[:, sl, :], src[kc * 128:(kc + 1) * 128, :], writes=[("wst", sl)])
                if i % 2 == 0:
                    S.op("dve", "tensor_copy", reads=[("wst", sl)], writes=[(key, kc)], out=dst[:, kc, :], in_=wst[:, sl, :])
                else:
                    S.op("act", "activation", reads=[("wst", sl)], writes=[(key, kc)], out=dst[:, kc, :], in_=wst[:, sl, :],
                         func=AF.Copy)
                i += 1

        def load_blk(tb):
            bs = tb % 2
            tsl = slice(tb * 512, (tb + 1) * 512)
            _dma(S, "sp", ya[:, bs, :, :], yaT[:, tsl].rearrange("(k p) t -> p k t", p=128), writes=[("ya", bs)])
            cases = {}
            for cc in range(NCORES):
                j = cc % 4
                cases[cc] = [(hb[:, bs, :, :], hg[j][:, tb * 512:(tb + 1) * 512].rearrange("(k p) t -> p k t", p=128))]
            S.cond_dma("sp", cases, reads=[("hg", i4) for i4 in range(4)], writes=[("hb", bs)], semkey=("hb", bs))
            _dma(S, "sp", szb[:, bs, :, :], szbT[:, tsl].rearrange("(k p) t -> p k t", p=128), writes=[("szb", bs)])

        def mk_yb(tb):
            bs_ = tb % 2
            S.op("dve", "tensor_tensor", reads=[("hb", bs_), ("szb", bs_)], writes=[("yb", bs_)],
                 out=yb[:, bs_, :, :], in0=hb[:, bs_, :, :], in1=szb[:, bs_, :, :], op=ALU.mult)

        load_w(((w_ba, wba, 4, "wba"), (w_bb, wbb, 4, "wbb")), 0)
        load_blk(0)
        mk_yb(0)
        load_w(((w_out, wo, 8, "wo"),), 0)
        pcnt = 0
        ocnt = 0
        for tb in range(NT // 512):
            bs = tb % 2
            if tb + 1 < NT // 512:
                load_blk(tb + 1)
            for oc in range(8):
                ps_ = pcnt % 2
                gs = pcnt % 4
                pcnt += 1
                tsl_ = slice(tb * 512, (tb + 1) * 512)
                _dma(S, "sp", sga[:, gs, :], sgaT[oc * 128:(oc + 1) * 128, tsl_], writes=[("sga", gs)])
                _dma(S, "sp", sgb[:, gs, :], sgbT[oc * 128:(oc + 1) * 128, tsl_], writes=[("sgb", gs)])
                for kc in range(4):
                    S.op("pe", "matmul", reads=[("wba", kc), ("ya", bs)], writes=[("pa", ps_)], signal=(kc == 3),
                         out=pa[ps_][:, :], lhsT=wba[:, kc, oc * 128:(oc + 1) * 128], rhs=ya[:, bs, kc, :],
                         start=(kc == 0), stop=(kc == 3))
                for kc in range(4):
                    S.op("pe", "matmul", reads=[("wbb", kc), ("yb", bs)], writes=[("pb", ps_)], signal=(kc == 3),
                         out=pb[ps_][:, :], lhsT=wbb[:, kc, oc * 128:(oc + 1) * 128], rhs=yb[:, bs, kc, :],
                         start=(kc == 0), stop=(kc == 3))
                S.op("dve", "tensor_tensor", reads=[("pa", ps_), ("sga", gs)], writes=[("t1", ps_)],
                     out=t1[:, ps_, :], in0=pa[ps_][:, :], in1=sga[:, gs, :], op=ALU.mult)
                S.op("dve", "tensor_tensor", reads=[("pb", ps_), ("sgb", gs)], writes=[("t2", ps_)],
                     out=t2[:, ps_, :], in0=pb[ps_][:, :], in1=sgb[:, gs, :], op=ALU.mult)
                S.op("pool" if oc % 2 == 0 else "dve", "tensor_tensor", reads=[("t1", ps_), ("t2", ps_)],
                     writes=[("mT", bs, oc)], out=mT[:, bs, oc, :], in0=t1[:, ps_, :], in1=t2[:, ps_, :], op=ALU.add)
                if oc == 7 and tb + 1 < NT // 512:
                    mk_yb(tb + 1)
            for tt in range(4):
                g = tb * 4 + tt
                xs = g % 2
                _dma(S, "sp", xr[:, xs, :], xres[g * 128:(g + 1) * 128, :], writes=[("xr", xs)])
                for half in range(2):
                    ob = ocnt % 4
                    ocnt += 1
                    for kc in range(8):
                        S.op("pe", "matmul", reads=[("mT", bs, kc), ("wo", kc)], writes=[("po", ob)], signal=(kc == 7),
                             out=po[ob][:, :], lhsT=mT[:, bs, kc, tt * 128:(tt + 1) * 128],
                             rhs=wo[:, kc, half * 512:(half + 1) * 512], start=(kc == 0), stop=(kc == 7))
                    S.op("dve", "scalar_tensor_tensor", reads=[("xr", xs), ("po", ob)], writes=[("yo", xs, half)],
                         out=yo[:, xs, half * 512:(half + 1) * 512], in0=xr[:, xs, half * 512:(half + 1) * 512],
                         scalar=ALPHA, in1=po[ob][:, :], op0=ALU.mult, op1=ALU.add)
                yk = [("yo", xs, 0), ("yo", xs, 1)]
                if fin is not None:
                    for hh in range(2):
                        S.op("dve", "bn_stats", reads=yk, writes=[("stats", xs, hh)],
                             out=stats[:, xs, hh * 6:(hh + 1) * 6], in_=yo[:, xs, hh * 512:(hh + 1) * 512])
                    S.op("dve", "bn_aggr", reads=[("stats", xs, 0), ("stats", xs, 1)], writes=[("mv", xs)],
                         out=mv[:, xs, :], in_=stats[:, xs, :])
                    S.op("act", "activation", reads=[("mv", xs)], writes=[("rstd", xs)],
                         out=rstd[:, xs, :], in_=mv[:, xs, 1:2], func=AF.Sqrt, bias=1e-5, scale=1.0)
                    S.op("dve", "reciprocal", reads=[("rstd", xs)], writes=[("rstd", xs)], out=rstd[:, xs, :],
                         in_=rstd[:, xs, :])
                    S.op("dve", "scalar_tensor_tensor", reads=[("mv", xs), ("rstd", xs)], writes=[("nb", xs)],
                         out=mv[:, xs, 1:2], in0=mv[:, xs, 0:1], scalar=-1.0, in1=rstd[:, xs, :], op0=ALU.mult, op1=ALU.mult)
                    S.op("act", "activation", reads=yk + [("nb", xs), ("rstd", xs)], writes=yk,
                         out=yo[:, xs, :], in_=yo[:, xs, :], func=AF.Identity, scale=rstd[:, xs, :], bias=mv[:, xs, 1:2])
                    S.op("dve", "tensor_tensor", reads=yk + ["lng"], writes=yk,
                         out=yo[:, xs, :], in0=yo[:, xs, :], in1=lng_sb[:], op=ALU.mult)
                    S.op("dve", "tensor_tensor", reads=yk + ["lnb"], writes=yk,
                         out=yo[:, xs, :], in0=yo[:, xs, :], in1=lnb_sb[:], op=ALU.add)
                    _dma(S, "sp", fin[2][g * 128:(g + 1) * 128, :], yo[:, xs, :], reads=yk, semkey=("yo_o", xs))
                else:
                    _dma(S, "sp", yout[g * 128:(g + 1) * 128, :], yo[:, xs, :], reads=yk, semkey=("yo_o", xs))
    S.barrier()


def build_fused():
    C = Ctx()
    xin = C.din("xin", [NT, D])
    lns = C.din("lns", [3, 2, 128, D])
    ident = C.din("ident", [128, 128])
    w_in = C.din("w_in", [2, D, 5120])
    bm = C.din("bm", [2, 128, 16])
    bt = C.din("bt", [2, 4, 128, 2, NTAB, 128])
    cw = C.din("cw", [2, 128, 5])
    gw = C.din("gw", [2, 128, 4, 128])
    gb = C.din("gb", [2, 128, 4])
    lam = C.din("lam", [2, 128, 2])
    w_ba = C.din("w_ba", [2, 512, D])
    w_bb = C.din("w_bb", [2, 512, D])
    w_out = C.din("w_out", [2, D, D])
    zk = C.din("zk", [128, 256], BF16)
    zv = C.din("zv", [256, 128], BF16)
    out = C.dout("out", [NT, D])
    T = dict(ident=ident[:, :], zk=zk, zv=zv)
    T["xres"] = C.dint("xres", [NT, D])
    T["ybuf"] = C.dint("ybuf", [NT, D])
    for n in ("qT", "kT"):
        T[n] = C.dint(n, [512, NT], BF16)
    T["v"] = C.dint("v", [NT, 512], BF16)
    for n in ("szaT", "szbT"):
        T[n] = C.dint(n, [512, NT], BF16)
    T["uT"] = [C.dint("uT%d" % i, [512, 512]) for i in range(8)]
    for n in ("sgaT", "sgbT"):
        T[n] = C.dint(n, [D, NT], BF16)
    T["ug"] = [C.dint("ug%d" % i, [4 * 512, 512]) for i in range(8)]
    T["hb"] = C.dint("hb", [1024, 512], BF16)
    T["hbg"] = C.dint("hbg", [4 * 1024, 512], BF16)
    T["yaT"] = C.dint("yaT", [512, NT], BF16)
    T["hT"] = [C.dint("hT%d" % i, [128, NT], BF16) for i in range(4)]
    T["hg"] = [C.dint("hg%d" % i, [4 * 128, NT], BF16) for i in range(4)]
    yin = xin
    for l in range(2):
        phase_k1(C, T, yin, lns[l, 0], lns[l, 1], T["xres"], w_in=w_in[l], bm=bm[l])
        exchange_after_k1(C, T)
        phase_k2(C, T, bt[l])
        phase_k3(C, T, cw[l], gw[l], gb[l], lam[l])
        exchange_after_k3(C, T)
        phase_k4(C, T, w_ba[l], w_bb[l], w_out[l], T["ybuf"], fin=(lns[2, 0], lns[2, 1], out) if l == 1 else None)
        yin = T["ybuf"]
    return C.done()


def _bias_tables(rpb_l, j):
    H = 8
    tab = np.full((H, NTAB, 128, 128), NEG, np.float32)
    kp = np.arange(128)
    qi = np.arange(128)
    ck = (kp % 64)[:, None]
    cq = (qi % 64)[None, :]
    cs = np.clip(cq - 8, 0, 48)
    colok = (ck >= cs) & (ck <= cs + 15)
    dc = ck - cq + 15
    for cls, (r_loc, win_lo) in enumerate(((0, -4), (2, -4), (8, -4), (62, -6), (60, -4))):
        r = 64 * j + r_loc
        rq = r + (qi // 64)[None, :]
        rs = np.clip(rq - 4, 0, 248)
        for kt in range(CLS_NT[cls]):
            rk = r + win_lo + 2 * kt + (kp // 64)[:, None]
            ok = colok & (rk >= rs) & (rk <= rs + 7)
            dr = rk - rq + 7
            drc = np.clip(dr, 0, 14)
            dcc = np.clip(dc, 0, 30)
            vals = rpb_l[:, drc, dcc]
            tab[:, CLS_OFF[cls] + kt] = np.where(ok[None], vals, NEG)
    t = tab.reshape(4, 2, NTAB, 128, 128).transpose(0, 3, 1, 2, 4)
    return np.ascontiguousarray(t)


_NC = []


def _rep(v):
    v = np.asarray(v, np.float32)
    return np.broadcast_to(v[None, :], (128, v.shape[0]))


def kernel(x, emb_ln_g, emb_ln_b, w_in, rpb, conv_w, conv_b, lru_gate_w, lru_gate_b, lru_lambda,
           w_branch_attn, w_branch_lru, b_merge, w_out, ln_g, ln_b):
    import ml_dtypes
    f = lambda a: np.ascontiguousarray(np.asarray(a, np.float32))
    x = f(x)
    B = x.shape[0]
    if not _NC:
        _NC.append(build_fused())
    nc = _NC[0]
    lns = f(np.stack([np.stack([_rep(emb_ln_g), _rep(emb_ln_b)]), np.stack([_rep(ln_g[0]), _rep(ln_b[0])]),
                      np.stack([_rep(ln_g[1]), _rep(ln_b[1])])]))
    bm = f(np.stack([np.asarray(b_merge[l]).reshape(2, 8, 128).transpose(2, 0, 1).reshape(128, 16) for l in range(2)]))
    common = dict(lns=lns, ident=np.eye(128, dtype=np.float32), w_in=f(w_in), bm=bm, w_ba=f(w_branch_attn),
                  w_bb=f(w_branch_lru), w_out=f(w_out), zk=np.zeros((128, 256), ml_dtypes.bfloat16),
                  zv=np.zeros((256, 128), ml_dtypes.bfloat16))
    perj = []
    for j in range(4):
        ch = slice(j * 128, (j + 1) * 128)
        bt = np.stack([_bias_tables(f(rpb[l]), j) for l in range(2)])
        cw = np.stack([np.concatenate([f(conv_w[l])[:, ch].T, f(conv_b[l])[ch][:, None]], axis=1) for l in range(2)])
        gw = np.zeros((2, 128, 4, 128), np.float32)
        gb = np.zeros((2, 128, 4), np.float32)
        for l in range(2):
            for d in range(2):
                for g in range(2):
                    for bb in range(2):
                        gw[l, bb * 64:(bb + 1) * 64, d * 2 + g, bb * 64:(bb + 1) * 64] = f(lru_gate_w[l][d, g, 2 * j + bb])
                        gb[l, bb * 64:(bb + 1) * 64, d * 2 + g] = f(lru_gate_b[l][d, g, 2 * j + bb])
        lam = np.stack([f(lru_lambda[l])[:, ch].T for l in range(2)])
        perj.append(dict(bt=f(bt), cw=f(cw), gw=gw, gb=gb, lam=f(lam)))
    in_maps = []
    for c in range(NCORES):
        b, j = c // 4, c % 4
        m = dict(common)
        m.update(perj[j])
        m["xin"] = f(x[b, j * NT:(j + 1) * NT])
        in_maps.append(m)
    res = run_bass_kernel_spmd(nc, in_maps, core_ids=list(range(NCORES)))
    out = np.empty((B, SEQ, D), np.float32)
    for c in range(NCORES):
        out[c // 4, (c % 4) * NT:(c % 4 + 1) * NT] = res.results[c]["out"]
    return out
```

```python
import numpy as np
from contextlib import ExitStack
import concourse.bass as bass
import concourse.mybir as mybir
from concourse.bass_utils import run_bass_kernel_spmd

F32 = mybir.dt.float32
BF16 = mybir.dt.bfloat16
AF = mybir.ActivationFunctionType
ALU = mybir.AluOpType

NCORES = 8
NT = 4096
SEQ = 16384
D = 1024
ALPHA = 4.0 ** 0.25
NEG = -30000.0


class Sched:
    ENG = ("pe", "act", "dve", "pool", "sp")

    def __init__(self, nc, stack):
        self.nc = nc
        self.stack = stack
        self.sems = {}
        self.count = {}
        for e in self.ENG:
            self._sem("e:" + e)
        self.lastw = {}
        self.reads = {}
        self.known = {e: {} for e in self.ENG}
        self.stream = {e: [] for e in self.ENG}

    def _sem(self, name):
        if name not in self.sems:
            self.sems[name] = self.stack.enter_context(self.nc.semaphore("s%d" % len(self.sems)))
            self.count[name] = 0
        return self.sems[name]

    def op(self, eng, name, reads=(), writes=(), dma=False, semkey=None, signal=True, inc=16, **kw):
        fn = (name, kw)
        deps = {}

        def add(s, v):
            if v > deps.get(s, 0):
                deps[s] = v

        for k in reads:
            if k in self.lastw:
                add(*self.lastw[k])
        for k in writes:
            if k in self.lastw:
                add(*self.lastw[k])
            for s, v in self.reads.get(k, {}).items():
                add(s, v)
        waits = []
        for s, v in deps.items():
            if s == "e:pe" and eng == "pe":
                continue
            if self.known[eng].get(s, 0) >= v:
                continue
            self.known[eng][s] = v
            waits.append((s, v))
        if dma:
            sk = "d:" + str(semkey if semkey is not None else (list(writes) + list(reads))[0])
            self._sem(sk)
            self.count[sk] += inc
            tok = (sk, self.count[sk])
            emit_tok = (sk, inc)
        else:
            sk = "e:" + eng
            if signal:
                self.count[sk] += 1
                tok = (sk, self.count[sk])
                emit_tok = (sk, 1)
            else:
                assert eng == "pe"
                tok = (sk, self.count[sk] + 1)
                emit_tok = None
        for k in reads:
            d = self.reads.setdefault(k, {})
            if tok[1] > d.get(tok[0], 0):
                d[tok[0]] = tok[1]
        for k in writes:
            self.lastw[k] = tok
            self.reads[k] = {}
        self.stream[eng].append((waits, fn, emit_tok, dma))
        return tok

    def cond_dma(self, eng, cases, reads=(), writes=(), semkey=None):
        n = len(next(iter(cases.values())))
        assert all(len(v) == n for v in cases.values())
        return self.op(eng, "__cond__", reads=reads, writes=writes, dma=True, semkey=semkey, inc=16 * n, cases=cases)

    def barrier(self, exclude="d:cc_"):
        snap = [(s, v) for s, v in self.count.items() if v > 0 and not s.startswith(exclude)]
        for e in self.ENG:
            waits = []
            for s, v in snap:
                if self.known[e].get(s, 0) >= v:
                    continue
                self.known[e][s] = v
                waits.append((s, v))
            self.stream[e].append((waits, None, None, False))
        self.lastw = {k: t for k, t in self.lastw.items() if t[0].startswith(exclude)}
        self.reads = {}

    def finish(self, eng="sp"):
        waits = [(s, v) for s, v in self.count.items() if s.startswith("d:") and v > 0]
        self.stream[eng].append((waits, None, None, False))

    def build(self):
        nc = self.nc
        with nc.Block() as block:
            def run(ename, e):
                core = None
                for waits, fn, tok, dma in self.stream[ename]:
                    for s, v in waits:
                        e.wait_ge(self.sems[s], v)
                    if fn is None:
                        continue
                    if fn[0] == "__cond__":
                        if core is None:
                            core = e.partition_id()
                        for c, lst in fn[1]["cases"].items():
                            with e.If(core == c):
                                for (o, i) in lst:
                                    e.dma_start(out=o, in_=i).then_inc(self.sems[tok[0]], 16)
                        continue
                    ins = getattr(e, fn[0])(**fn[1])
                    if tok is not None:
                        ins.then_inc(self.sems[tok[0]], tok[1])

            @block.tensor
            def _(e):
                run("pe", e)

            @block.scalar
            def _(e):
                run("act", e)

            @block.vector
            def _(e):
                run("dve", e)

            @block.gpsimd
            def _(e):
                run("pool", e)

            @block.sync
            def _(e):
                run("sp", e)


class Ctx:
    def __init__(self):
        self.nc = bass.Bass("TRN2", target_bir_lowering=False)
        self.st = ExitStack()
        self.S = Sched(self.nc, self.st)
        self.pst = None
        self.uid = 0

    def din(self, name, shape, dt=F32):
        return self.nc.dram_tensor(name, list(shape), dt, kind="ExternalInput").ap()

    def dout(self, name, shape, dt=F32):
        return self.nc.dram_tensor(name, list(shape), dt, kind="ExternalOutput").ap()

    def dint(self, name, shape, dt=F32):
        return self.nc.dram_tensor(name, list(shape), dt, kind="Internal").ap()

    def phase(self):
        self.pst = ExitStack()
        return self.pst

    def sb(self, name, shape, dt=F32):
        self.uid += 1
        return self.pst.enter_context(self.nc.sbuf_tensor("%s_%d" % (name, self.uid), list(shape), dt))

    def ps(self, name, shape, dt=F32):
        self.uid += 1
        return self.pst.enter_context(self.nc.psum_tensor("%s_%d" % (name, self.uid), list(shape), dt))

    def done(self):
        self.S.finish()
        self.S.build()
        self.st.close()
        return self.nc


def _dma(S, eng, out, in_, reads=(), writes=(), semkey=None):
    return S.op(eng, "dma_start", reads=reads, writes=writes, dma=True, semkey=semkey, out=out, in_=in_)


GROUPS = [[0, 1, 2, 3], [4, 5, 6, 7]]
CLS_NT = [6, 6, 5, 6, 5]
CLS_OFF = [0, 6, 12, 17, 23]
NTAB = 28
KROWS = 72
NK = KROWS * 64


def tile_class(t):
    if t == 0:
        return 0, 0
    if t == 1:
        return 1, 1
    if t == 31:
        return 3, 30
    if t == 30:
        return 4, 30
    return 2, t


def phase_k1(C, T, yin, lng, lnb, xres, w_in=None, bm=None, do_proj=True):
    S = C.S
    with C.phase():
        lng_sb = C.sb("lng_sb", [128, D])
        lnb_sb = C.sb("lnb_sb", [128, D])
        NLS = 4
        yt = C.sb("yt", [128, NLS, D])
        xn = C.sb("xn", [128, NLS, D])
        stats = C.sb("stats", [128, NLS, 12])
        mv = C.sb("mv", [128, NLS, 2])
        rstd = C.sb("rstd", [128, NLS, 1])
        _dma(S, "sp", lng_sb[:], lng, writes=["lng"])
        _dma(S, "sp", lnb_sb[:], lnb, writes=["lnb"])
        if do_proj:
            qT, kT, vo, szaT, uT, szbT, sgaT, sgbT = (T[k] for k in ("qT", "kT", "v", "szaT", "uT", "szbT", "sgaT", "sgbT"))
            wbf = C.sb("wbf", [128, 8, 5120], BF16)
            wst = C.sb("wst", [128, 4, 1280])
            ident_f = C.sb("ident_f", [128, 128])
            identb = C.sb("identb", [128, 128], BF16)
            bm_sb = C.sb("bm_sb", [128, 16])
            xnb = C.sb("xnb", [128, 2, D], BF16)
            xT = C.sb("xT", [128, 2, 8, 512], BF16)
            NOF, NOB = 4, 8
            ostf = C.sb("ostf", [128, NOF, 512])
            ostb = C.sb("ostb", [128, NOB, 512], BF16)
            pst = [C.ps("pst%d" % i, [128, 8, 128], BF16) for i in range(2)]
            NPS = 6
            psm = [C.ps("psm%d" % i, [128, 512]) for i in range(NPS)]
            _dma(S, "sp", ident_f[:], T["ident"], writes=["identf"])
            _dma(S, "sp", bm_sb[:], bm, writes=["bm"])
            S.op("dve", "tensor_copy", reads=["identf"], writes=["identb"], out=identb[:], in_=ident_f[:])
            cast_engs = ["dve", "act", "act"]
            def load_weights():
                for i, (half, kc) in enumerate([(h, k) for h in range(4) for k in range(8)]):
                    if True:
                        sl = i % 4
                        _dma(S, "sp", wst[:, sl, :], w_in[kc * 128:(kc + 1) * 128, half * 1280:(half + 1) * 1280],
                             writes=[("wst", sl)])
                        ce = cast_engs[i % 3]
                        o = wbf[:, kc, half * 1280:(half + 1) * 1280]
                        if ce == "act":
                            S.op("act", "activation", reads=[("wst", sl)], writes=[("wbf", kc, half)],
                                 out=o, in_=wst[:, sl, :], func=AF.Copy)
                        else:
                            S.op(ce, "tensor_copy", reads=[("wst", sl)], writes=[("wbf", kc, half)], out=o, in_=wst[:, sl, :])

        cnt = {"f": 0, "b": 0, "ps": 0}
        hb_keys = []

        def ln_tile(g):
            s2 = g % NLS
            _dma(S, "sp", yt[:, s2, :], yin[g * 128:(g + 1) * 128, :], writes=[("yt", s2)])
            for hh in range(2):
                S.op("dve", "bn_stats", reads=[("yt", s2)], writes=[("stats", s2, hh)],
                     out=stats[:, s2, hh * 6:(hh + 1) * 6], in_=yt[:, s2, hh * 512:(hh + 1) * 512])
            S.op("dve", "bn_aggr", reads=[("stats", s2, 0), ("stats", s2, 1)], writes=[("mv", s2)],
                 out=mv[:, s2, :], in_=stats[:, s2, :])
            S.op("act", "activation", reads=[("mv", s2)], writes=[("rstd", s2)],
                 out=rstd[:, s2, :], in_=mv[:, s2, 1:2], func=AF.Sqrt, bias=1e-5, scale=1.0)
            S.op("dve", "reciprocal", reads=[("rstd", s2)], writes=[("rstd", s2)], out=rstd[:, s2, :], in_=rstd[:, s2, :])
            S.op("dve", "scalar_tensor_tensor", reads=[("mv", s2), ("rstd", s2)], writes=[("nb", s2)],
                 out=mv[:, s2, 1:2], in0=mv[:, s2, 0:1], scalar=-1.0, in1=rstd[:, s2, :], op0=ALU.mult, op1=ALU.mult)
            S.op("act", "activation", reads=[("yt", s2), ("nb", s2), ("rstd", s2)], writes=[("xn", s2)],
                 out=xn[:, s2, :], in_=yt[:, s2, :], func=AF.Identity, scale=rstd[:, s2, :], bias=mv[:, s2, 1:2])
            S.op("dve", "tensor_tensor", reads=[("xn", s2), "lng"], writes=[("xn", s2)],
                 out=xn[:, s2, :], in0=xn[:, s2, :], in1=lng_sb[:], op=ALU.mult)
            S.op("dve", "tensor_tensor", reads=[("xn", s2), "lnb"], writes=[("xn", s2)],
                 out=xn[:, s2, :], in0=xn[:, s2, :], in1=lnb_sb[:], op=ALU.add)
            _dma(S, "sp", xres[g * 128:(g + 1) * 128, :], xn[:, s2, :], reads=[("xn", s2)], semkey=("xn_o", s2))

        def tr_tile(g):
            s4 = g % NLS
            s2 = g % 2
            tb, tt = g // 4, g % 4
            bs = tb % 2
            S.op("act", "activation", reads=[("xn", s4)], writes=[("xnb", s2)],
                 out=xnb[:, s2, :], in_=xn[:, s4, :], func=AF.Copy)
            for kc in range(8):
                S.op("pe", "transpose", reads=[("xnb", s2), "identb"], writes=[("pst", s2)], signal=(kc == 7),
                     out=pst[s2][:, kc, :], in_=xnb[:, s2, kc * 128:(kc + 1) * 128], identity=identb[:])
            S.op("act", "activation", reads=[("pst", s2)], writes=[("xT", bs, tt)],
                 out=xT[:, bs, :, tt * 128:(tt + 1) * 128], in_=pst[s2][:, :, :], func=AF.Copy)

        def evac(kind, c, bank, tb):
            src = psm[bank][:, :]
            tsl = slice(tb * 512, (tb + 1) * 512)
            if kind in ("q", "k"):
                sl = cnt["b"] % NOB
                cnt["b"] += 1
                dst = ostb[:, sl, :]
                if kind == "q":
                    S.op("dve", "tensor_scalar", reads=[("psm", bank)], writes=[("ostb", sl)],
                         out=dst, in0=src, scalar1=0.125, scalar2=None, op0=ALU.mult)
                    tgt = qT
                else:
                    S.op("dve", "tensor_copy", reads=[("psm", bank)], writes=[("ostb", sl)], out=dst, in_=src)
                    tgt = kT
                    if tb == 0:
                        _dma(S, "sp", T["hb"][c * 128:(c + 1) * 128, 0:256], ostb[:, sl, 0:256], reads=[("ostb", sl)],
                             writes=[("hb_d", "kt", c)], semkey=("hbs", "kt", c))
                        hb_keys.append(("hb_d", "kt", c))
                    if tb == NT // 512 - 1:
                        _dma(S, "sp", T["hb"][c * 128:(c + 1) * 128, 256:512], ostb[:, sl, 256:512], reads=[("ostb", sl)],
                             writes=[("hb_d", "kb", c)], semkey=("hbs", "kb", c))
                        hb_keys.append(("hb_d", "kb", c))
                _dma(S, "sp", tgt[c * 128:(c + 1) * 128, tsl], dst, reads=[("ostb", sl)], semkey=("ostb", sl))
                return
            if kind == "u":
                sl = cnt["f"] % NOF
                cnt["f"] += 1
                dst = ostf[:, sl, :]
                S.op("dve", "tensor_copy", reads=[("psm", bank)], writes=[("ostf", sl)], out=dst, in_=src)
                _dma(S, "sp", uT[tb][c * 128:(c + 1) * 128, :], dst, reads=[("ostf", sl)], writes=[("uT_d", tb, c)],
                     semkey=("ostf", sl))
                return
            sl = cnt["b"] % NOB
            cnt["b"] += 1
            dst = ostb[:, sl, :]
            if kind in ("za", "zb"):
                S.op("act", "activation", reads=[("psm", bank)], writes=[("ostb", sl)], out=dst, in_=src, func=AF.Silu)
                tgt = szaT if kind == "za" else szbT
            else:
                col = c if kind == "ga" else 8 + c
                S.op("act", "activation", reads=[("psm", bank), "bm"], writes=[("ostb", sl)],
                     out=dst, in_=src, func=AF.Sigmoid, bias=bm_sb[:, col:col + 1])
                tgt = sgaT if kind == "ga" else sgbT
            _dma(S, "sp", tgt[c * 128:(c + 1) * 128, tsl], dst, reads=[("ostb", sl)], semkey=("ostb", sl))

        fam = ([("q", i) for i in range(4)] + [("k", i) for i in range(4)] + [("v", i) for i in range(4)]
               + [("za", i) for i in range(4)] + [("u", i) for i in range(4)] + [("zb", i) for i in range(4)]
               + [("ga", i) for i in range(8)] + [("gb", i) for i in range(8)])

        def proj_block(tb):
            bs = tb % 2
            xkeys = [("xT", bs, tt) for tt in range(4)]
            nxt = 0

            def v_part():
                for tt in range(4):
                    bank = cnt["ps"] % NPS
                    cnt["ps"] += 1
                    for kc in range(8):
                        S.op("pe", "matmul", reads=[("xT", bs, tt), ("wbf", kc, 0), ("wbf", kc, 1)], writes=[("psm", bank)],
                             signal=(kc == 7),
                             out=psm[bank][:, :], lhsT=xT[:, bs, kc, tt * 128:(tt + 1) * 128], rhs=wbf[:, kc, 1024:1536],
                             start=(kc == 0), stop=(kc == 7))
                    sl = cnt["b"] % NOB
                    cnt["b"] += 1
                    S.op("dve", "tensor_copy", reads=[("psm", bank)], writes=[("ostb", sl)], out=ostb[:, sl, :], in_=psm[bank][:, :])
                    g = tb * 4 + tt
                    _dma(S, "sp", vo[g * 128:(g + 1) * 128, :], ostb[:, sl, :], reads=[("ostb", sl)], semkey=("ostb", sl))
                    if g < 2 or g >= NT // 128 - 2:
                        hr0 = g * 128 if g < 2 else 256 + (g - (NT // 128 - 2)) * 128
                        _dma(S, "sp", T["hb"][512 + hr0:512 + hr0 + 128, :], ostb[:, sl, :], reads=[("ostb", sl)],
                             writes=[("hb_d", "v", g)], semkey=("hbs", "v", g))
                        hb_keys.append(("hb_d", "v", g))

            last = (tb == NT // 512 - 1)
            if last:
                v_part()
            for occ, (kind, ci) in enumerate(fam):
                if kind == "v":
                    continue
                if tb == NT // 512 - 1 and occ == 12:
                    S.op("pool", "collective_compute", dma=True, inc=1, semkey="cc_a", reads=list(hb_keys), writes=["hbg"],
                         kind="AllGather", op=ALU.bypass, replica_groups=GROUPS, ins=[T["hb"]], outs=[T["hbg"]])
                if tb + 1 < NT // 512 and occ in (2, 12, 22, 32):
                    ln_tile((tb + 1) * 4 + nxt)
                    nxt += 1
                bank = cnt["ps"] % NPS
                cnt["ps"] += 1
                half = occ // 10
                for kc in range(8):
                    S.op("pe", "matmul", reads=xkeys + [("wbf", kc, half)], writes=[("psm", bank)], signal=(kc == 7),
                         out=psm[bank][:, :], lhsT=wbf[:, kc, occ * 128:(occ + 1) * 128], rhs=xT[:, bs, kc, :],
                         start=(kc == 0), stop=(kc == 7))
                evac(kind, ci, bank, tb)
            if not last:
                v_part()
            if tb + 1 < NT // 512:
                for tt in range(4):
                    tr_tile((tb + 1) * 4 + tt)

        NTILE = NT // 128
        if not do_proj:
            for g in range(NTILE):
                ln_tile(g)
        else:
            for tt in range(4):
                ln_tile(tt)
                tr_tile(tt)
            load_weights()
            for tb in range(NT // 512):
                proj_block(tb)
    S.barrier()


def phase_ln(C, yin, lng, lnb, xout):
    S = C.S
    G, NS = 4, 3
    with C.phase():
        lng_sb = C.sb("lng_sb", [128, D])
        lnb_sb = C.sb("lnb_sb", [128, D])
        yt = C.sb("yt", [128, NS, G, D])
        xn = C.sb("xn", [128, NS, G, D])
        stats = C.sb("stats", [128, NS, G, 12])
        mv = C.sb("mv", [128, NS, G, 2])
        rstd = C.sb("rstd", [128, NS, G, 1])
        _dma(S, "sp", lng_sb[:], lng, writes=["lng"])
        _dma(S, "sp", lnb_sb[:], lnb, writes=["lnb"])
        for gi in range(NT // (128 * G)):
            s = gi % NS
            rows = slice(gi * 128 * G, (gi + 1) * 128 * G)
            _dma(S, "sp", yt[:, s, :, :], yin[rows, :].rearrange("(t p) d -> p t d", p=128), writes=[("yt", s)])
            for t in range(G):
                for hh in range(2):
                    S.op("dve", "bn_stats", reads=[("yt", s)], writes=[("stats", s, t, hh)],
                         out=stats[:, s, t, hh * 6:(hh + 1) * 6], in_=yt[:, s, t, hh * 512:(hh + 1) * 512])
                S.op("dve", "bn_aggr", reads=[("stats", s, t, 0), ("stats", s, t, 1)], writes=[("mv", s, t)],
                     out=mv[:, s, t, :], in_=stats[:, s, t, :])
                S.op("act", "activation", reads=[("mv", s, t)], writes=[("rstd", s, t)],
                     out=rstd[:, s, t, :], in_=mv[:, s, t, 1:2], func=AF.Sqrt, bias=1e-5, scale=1.0)
                S.op("dve", "reciprocal", reads=[("rstd", s, t)], writes=[("rstd", s, t)], out=rstd[:, s, t, :],
                     in_=rstd[:, s, t, :])
                S.op("dve", "scalar_tensor_tensor", reads=[("mv", s, t), ("rstd", s, t)], writes=[("nb", s, t)],
                     out=mv[:, s, t, 1:2], in0=mv[:, s, t, 0:1], scalar=-1.0, in1=rstd[:, s, t, :], op0=ALU.mult, op1=ALU.mult)
                S.op("act", "activation", reads=[("yt", s), ("nb", s, t), ("rstd", s, t)], writes=[("xn", s, t)],
                     out=xn[:, s, t, :], in_=yt[:, s, t, :], func=AF.Identity, scale=rstd[:, s, t, :], bias=mv[:, s, t, 1:2])
                S.op("dve", "tensor_tensor", reads=[("xn", s, t), "lng"], writes=[("xn", s, t)],
                     out=xn[:, s, t, :], in0=xn[:, s, t, :], in1=lng_sb[:], op=ALU.mult)
                S.op("dve", "tensor_tensor", reads=[("xn", s, t), "lnb"], writes=[("xn", s, t)],
                     out=xn[:, s, t, :], in0=xn[:, s, t, :], in1=lnb_sb[:], op=ALU.add)
            _dma(S, "sp", xout[rows, :].rearrange("(t p) d -> p t d", p=128), xn[:, s, :, :],
                 reads=[("xn", s, t) for t in range(G)], semkey=("xn_o", s))
    S.barrier()


def exchange_after_k1(C, T):
    S = C.S
    for tb in range(8):
        S.op("pool", "collective_compute", dma=True, inc=1, semkey="cc_a", writes=[("ug", tb)], kind="AllGather",
             op=ALU.bypass, replica_groups=GROUPS, ins=[T["uT"][tb]], outs=[T["ug"][tb]])


def exchange_after_k3(C, T):
    pass


def phase_k2(C, T, bt):
    S = C.S
    qT, kT, vo, szaT, yaT = T["qT"], T["kT"], T["v"], T["szaT"], T["yaT"]
    hbg, zk, zv = T["hbg"], T["zk"], T["zv"]
    with C.phase():
        q_sb = C.sb("q_sb", [128, 2, NT], BF16)
        k_sb = C.sb("k_sb", [128, 2, NK], BF16)
        v_sb = C.sb("v_sb", [128, 2, NK // 128, 128], BF16)
        sza_sb = C.sb("sza_sb", [128, 2, NT], BF16)
        bt_raw = C.sb("bt_raw", [128, 2, NTAB, 128])
        bt_sb = C.sb("bt_sb", [128, 2, NTAB, 128])
        ya_sb = C.sb("ya_sb", [128, 2, NT], BF16)
        ones_b = C.sb("ones_b", [128, 64], BF16)
        NU = 3
        E = C.sb("E", [128, NU, 768])
        Pm = C.sb("Pm", [128, NU, 768], BF16)
        rden = C.sb("rden", [128, 2, 128])
        tmp = C.sb("tmp", [128, 2, 128])
        spa = [C.ps("spa%d" % i, [128, 512]) for i in range(NU)]
        spb = [C.ps("spb%d" % i, [128, 256]) for i in range(NU)]
        sps = [[spa[i][:, :], spb[i][:, :]] for i in range(NU)]
        nd = [C.ps("nd0", [128, 128])] * 2
        dn = [C.ps("dn0", [128, 128])] * 2

        S.op("dve", "memset", writes=["ones"], ap=ones_b[:], constant=1.0)

        def load_pair(p):
            s = p % 2
            prow = slice(p * 128, (p + 1) * 128)
            _dma(S, "sp", q_sb[:, s, :], qT[prow, :], writes=[("q", s)])
            _dma(S, "sp", k_sb[:, s, 256:256 + NT], kT[prow, :], writes=[("k", s)], semkey=("kmain", s))
            _dma(S, "sp", v_sb[:, s, 2:34, :], vo[:, prow].rearrange("(t p) f -> p t f", p=128),
                 writes=[("v", s)], semkey=("vmain", s))
            cases = {}
            for c in range(NCORES):
                j = c % 4
                lst = []
                lst.append((k_sb[:, s, 0:256], zk[:, :] if j == 0 else hbg[(j - 1) * 1024 + p * 128:(j - 1) * 1024 + (p + 1) * 128, 256:512]))
                lst.append((k_sb[:, s, 256 + NT:NK], zk[:, :] if j == 3 else hbg[(j + 1) * 1024 + p * 128:(j + 1) * 1024 + (p + 1) * 128, 0:256]))
                lst.append((v_sb[:, s, 0:2, :], (zv[:, :] if j == 0 else hbg[(j - 1) * 1024 + 768:(j - 1) * 1024 + 1024, prow]).rearrange("(t p) f -> p t f", p=128)))
                lst.append((v_sb[:, s, 34:36, :], (zv[:, :] if j == 3 else hbg[(j + 1) * 1024 + 512:(j + 1) * 1024 + 768, prow]).rearrange("(t p) f -> p t f", p=128)))
                cases[c] = lst
            S.cond_dma("sp", cases, reads=["hbg"], writes=[("k", s), ("v", s)], semkey=("halo", s))
            _dma(S, "sp", sza_sb[:, s, :], szaT[prow, :], writes=[("sza", s)])
            _dma(S, "sp", bt_raw[:], bt[p], writes=["bt_raw"])

        load_pair(0)
        torder = list(range(32))
        units = [(t, hh) for t in torder for hh in range(2)]

        def emit_qk(idx, s):
            t, hh = units[idx]
            cls, vt0 = tile_class(t)
            nkt = CLS_NT[cls]
            hs = idx % NU
            pr = slice(hh * 64, (hh + 1) * 64)
            for kt in range(nkt):
                bank = sps[hs][kt // 4]
                co = (kt % 4) * 128
                kcol = (vt0 + kt) * 128
                last = (kt == nkt - 1) or (kt == 3)
                S.op("pe", "matmul", reads=[("k", s), ("q", s)],
                     writes=[("sps", hs, kt // 4)], signal=last,
                     out=bank[:, co:co + 128], lhsT=k_sb[pr, s, kcol:kcol + 128],
                     rhs=q_sb[pr, s, t * 128:(t + 1) * 128], start=True, stop=True)

        def emit_rest(idx, s):
            t, hh = units[idx]
            cls, vt0 = tile_class(t)
            nkt = CLS_NT[cls]
            toff = CLS_OFF[cls]
            par = t % 2
            hs = idx % NU
            pr = slice(hh * 64, (hh + 1) * 64)
            n0 = min(nkt, 4) * 128
            S.op("act", "activation", reads=[("sps", hs, 0)], writes=[("E", hs, 0)],
                 out=E[:, hs, 0:n0], in_=sps[hs][0][:, 0:n0], func=AF.Exp)
            n1 = (nkt - 4) * 128
            S.op("act", "activation", reads=[("sps", hs, 1)], writes=[("E", hs, 1)],
                 out=E[:, hs, 512:512 + n1], in_=sps[hs][1][:, 0:n1], func=AF.Exp)
            S.op("dve", "tensor_tensor", reads=[("E", hs, 0), ("bt", hh)], writes=[("Pm", hs, 0)],
                 out=Pm[:, hs, 0:512], in0=E[:, hs, 0:512],
                 in1=bt_sb[:, hh, toff:toff + 4, :].rearrange("p a b -> p (a b)"), op=ALU.mult)
            S.op("dve", "tensor_tensor", reads=[("E", hs, 1), ("bt", hh)], writes=[("Pm", hs, 1)],
                 out=Pm[:, hs, 512:nkt * 128], in0=E[:, hs, 512:nkt * 128],
                 in1=bt_sb[:, hh, toff + 4:toff + nkt, :].rearrange("p a b -> p (a b)"), op=ALU.mult)
            for kt in range(nkt):
                S.op("pe", "matmul", reads=[("Pm", hs, kt // 4), ("v", s)],
                     writes=["nd"], signal=False,
                     out=nd[par][pr, 0:128], lhsT=v_sb[:, s, vt0 + kt, hh * 64:(hh + 1) * 64],
                     rhs=Pm[:, hs, kt * 128:(kt + 1) * 128], start=(kt == 0), stop=(kt == nkt - 1))
                S.op("pe", "matmul", reads=[("Pm", hs, kt // 4), "ones"], writes=["dn"], signal=(kt == nkt - 1),
                     out=dn[par][pr, 0:128], lhsT=ones_b[:, :],
                     rhs=Pm[:, hs, kt * 128:(kt + 1) * 128], start=(kt == 0), stop=(kt == nkt - 1))
            if hh == 1:
                S.op("dve", "reciprocal", reads=["dn"], writes=[("rden", par)],
                     out=rden[:, par, :], in_=dn[par][:, 0:128])
                S.op("dve", "tensor_tensor", reads=["nd", ("rden", par)], writes=[("tmp", par)],
                     out=tmp[:, par, :], in0=nd[par][:, 0:128], in1=rden[:, par, :], op=ALU.mult)
                S.op("pool", "tensor_tensor", reads=[("tmp", par), ("sza", s)], writes=[("ya", s)],
                     out=ya_sb[:, s, t * 128:(t + 1) * 128], in0=tmp[:, par, :],
                     in1=sza_sb[:, s, t * 128:(t + 1) * 128], op=ALU.mult)

        for p in range(4):
            s = p % 2
            for hh in range(2):
                S.op("act", "activation", reads=["bt_raw"], writes=[("bt", hh)],
                     out=bt_sb[:, hh, :, :], in_=bt_raw[:, hh, :, :], func=AF.Exp)
            if p + 1 < 4:
                load_pair(p + 1)
            emit_qk(0, s)
            emit_qk(1, s)
            for idx in range(len(units)):
                if idx + 2 < len(units):
                    emit_qk(idx + 2, s)
                emit_rest(idx, s)
            _dma(S, "sp", yaT[p * 128:(p + 1) * 128, :], ya_sb[:, s, :], reads=[("ya", s)], semkey=("ya_o", s))
    S.barrier()


def phase_k3(C, T, cw, gw, gb, lam):
    S = C.S
    ug, hT = T["ug"], T["hT"]
    CH = 2048
    NCH = SEQ // CH
    with C.phase():
        u_sb = C.sb("u_sb", [128, SEQ + 3])
        uc = C.sb("uc", [128, SEQ])
        cw_sb = C.sb("cw_sb", [128, 5])
        gw_sb = C.sb("gw_sb", [128, 4, 128])
        gb_sb = C.sb("gb_sb", [128, 4])
        lam_sb = C.sb("lam_sb", [128, 2])
        cs = C.sb("cs", [128, 12, 2])
        csp = C.sb("csp", [128, 2])
        csp2 = C.sb("csp2", [128, 2])
        r_sb = C.sb("r_sb", [128, 1, CH])
        i_sb = C.sb("i_sb", [128, 1, CH])
        a_sb = C.sb("a_sb", [128, 2, CH])
        m_sb = C.sb("m_sb", [128, 2, CH])
        hr_sb = C.sb("hr_sb", [128, 1, CH])
        yb_sb = C.sb("yb_sb", [128, 2, CH], BF16)
        carry = C.sb("carry", [128, 2])
        NH = CH // 512
        psr = [[C.ps("psr%d_%d" % (i, j), [128, 512]) for j in range(NH)] for i in range(1)]
        psi = [[C.ps("psi%d_%d" % (i, j), [128, 512]) for j in range(NH)] for i in range(1)]

        _dma(S, "sp", cw_sb[:], cw, writes=["cw"])
        _dma(S, "sp", gw_sb[:], gw, writes=["gw"])
        _dma(S, "sp", gb_sb[:], gb, writes=["gb"])
        _dma(S, "sp", lam_sb[:], lam, writes=["lam"])
        S.op("dve", "memset", writes=["upad0"], ap=u_sb[:, 0:2], constant=0.0)
        S.op("dve", "memset", writes=["upad1"], ap=u_sb[:, SEQ + 2:SEQ + 3], constant=0.0)
        LCH = 2048
        NL = SEQ // LCH
        for c in range(NL):
            i, off = (c * LCH) // NT, (c * LCH) % NT
            cases = {}
            for cc in range(NCORES):
                j = cc % 4
                cases[cc] = [(u_sb[:, 2 + c * LCH + q * 512:2 + c * LCH + (q + 1) * 512],
                              ug[off // 512 + q][i * 512 + j * 128:i * 512 + (j + 1) * 128, :]) for q in range(LCH // 512)]
            S.cond_dma("sp", cases, reads=[("ug", off // 512 + q) for q in range(LCH // 512)], writes=[("u", c)],
                       semkey=("u", c % 2))

        V = lambda i: cs[:, i, :]
        k = dict(reads=["cs", "lam"], writes=["cs"])
        S.op("dve", "tensor_scalar", out=V(8), in0=lam_sb[:], scalar1=-1.0, scalar2=None, op0=ALU.mult, **k)
        S.op("dve", "tensor_tensor", out=V(0), in0=V(8), in1=lam_sb[:], op=ALU.max, **k)
        S.op("act", "activation", out=V(1), in_=V(0), func=AF.Exp, scale=-1.0, **k)
        S.op("dve", "tensor_scalar", out=V(2), in0=V(1), scalar1=2.0, scalar2=None, op0=ALU.add, **k)
        S.op("dve", "reciprocal", out=V(2), in_=V(2), **k)
        S.op("dve", "tensor_tensor", out=V(3), in0=V(1), in1=V(2), op=ALU.mult, **k)
        S.op("dve", "tensor_tensor", out=V(4), in0=V(3), in1=V(3), op=ALU.mult, **k)
        S.op("dve", "tensor_scalar", out=V(5), in0=V(4), scalar1=1.0 / 13, scalar2=1.0 / 11, op0=ALU.mult, op1=ALU.add, **k)
        for cden in (9.0, 7.0, 5.0, 3.0, 1.0):
            S.op("dve", "tensor_tensor", out=V(5), in0=V(5), in1=V(4), op=ALU.mult, **k)
            S.op("dve", "tensor_scalar", out=V(5), in0=V(5), scalar1=1.0 / cden, scalar2=None, op0=ALU.add, **k)
        S.op("dve", "tensor_tensor", out=V(5), in0=V(5), in1=V(3), op=ALU.mult, **k)
        S.op("dve", "tensor_scalar", out=V(6), in0=lam_sb[:], scalar1=-1.0, scalar2=0.0, op0=ALU.mult, op1=ALU.max, **k)
        S.op("dve", "scalar_tensor_tensor", out=V(7), in0=V(5), scalar=2.0, in1=V(6), op0=ALU.mult, op1=ALU.add, **k)
        S.op("dve", "tensor_scalar", reads=["cs"], writes=["csp"], out=csp[:], in0=V(7), scalar1=-8.0, scalar2=None, op0=ALU.mult)
        S.op("dve", "tensor_scalar", reads=["cs"], writes=["csp2"], out=csp2[:], in0=V(7), scalar1=-16.0, scalar2=None, op0=ALU.mult)

        def conv_chunk(c):
            lo = c * LCH
            rd = [("u", c), "cw", "upad0", "upad1"] + ([("u", c - 1)] if c > 0 else []) + ([("u", c + 1)] if c + 1 < NL else [])
            o = uc[:, lo:lo + LCH]
            S.op("dve", "tensor_scalar", reads=rd, writes=[("uc", c)], out=o, in0=u_sb[:, lo:lo + LCH],
                 scalar1=cw_sb[:, 0:1], scalar2=cw_sb[:, 4:5], op0=ALU.mult, op1=ALU.add)
            for j in range(1, 4):
                S.op("dve", "scalar_tensor_tensor", reads=rd + [("uc", c)], writes=[("uc", c)], out=o,
                     in0=u_sb[:, lo + j:lo + j + LCH], scalar=cw_sb[:, j:j + 1], in1=o, op0=ALU.mult, op1=ALU.add)
        conv_chunk(0)
        conv_chunk(1)
        UCK = [("uc", c) for c in range(NL)]
        UK = [("u", c) for c in range(NL)]

        def gates(d, n, it):
            sl = it % 2
            lo = n * CH
            uck = ("uc", lo // LCH)
            for g, pst in ((0, psr), (1, psi)):
                for hf in range(NH):
                    S.op("pe", "matmul", reads=["gw", uck], writes=[("ps", g, hf)],
                         out=pst[0][hf][:, :], lhsT=gw_sb[:, d * 2 + g, :],
                         rhs=uc[:, lo + hf * 512:lo + (hf + 1) * 512], start=True, stop=True)
            for hf in range(NH):
                fs = slice(hf * 512, (hf + 1) * 512)
                S.op("act", "activation", reads=[("ps", 0, hf), "gb"], writes=[("r", hf)],
                     out=r_sb[:, 0, fs], in_=psr[0][hf][:, :], func=AF.Sigmoid, bias=gb_sb[:, d * 2:d * 2 + 1])
            for hf in range(NH):
                fs = slice(hf * 512, (hf + 1) * 512)
                S.op("act", "activation", reads=[("ps", 1, hf), "gb"], writes=[("i", hf)],
                     out=i_sb[:, 0, fs], in_=psi[0][hf][:, :], func=AF.Sigmoid, bias=gb_sb[:, d * 2 + 1:d * 2 + 2])
            rk = [("r", hf) for hf in range(NH)]
            ik = [("i", hf) for hf in range(NH)]
            S.op("act", "activation", reads=rk + ["csp"], writes=[("a", sl)],
                 out=a_sb[:, sl, :], in_=r_sb[:, 0, :], func=AF.Exp, scale=csp[:, d:d + 1])
            S.op("dve", "tensor_tensor", reads=[("a", sl)], writes=[("m", sl)],
                 out=m_sb[:, sl, :], in0=a_sb[:, sl, :], in1=a_sb[:, sl, :], op=ALU.mult)
            S.op("act", "activation", reads=[("m", sl)], writes=[("m", sl)], out=m_sb[:, sl, :], in_=m_sb[:, sl, :],
                 func=AF.Sqrt, scale=-1.0, bias=1.0)
            S.op("dve", "tensor_tensor", reads=ik + [uck], writes=ik, out=i_sb[:, 0, :], in0=i_sb[:, 0, :],
                 in1=uc[:, lo:lo + CH], op=ALU.mult)
            first = (d == 0 and n == 0) or (d == 1 and n == NCH - 1)
            if first:
                col = 0 if d == 0 else CH - 1
                S.op("dve", "memset", reads=[("m", sl)], writes=[("m", sl)], ap=m_sb[:, sl, col:col + 1], constant=1.0)
            S.op("dve", "tensor_tensor", reads=[("m", sl)] + ik, writes=[("m", sl)], out=m_sb[:, sl, :],
                 in0=m_sb[:, sl, :], in1=i_sb[:, 0, :], op=ALU.mult)
            return sl

        it = 0
        for n in range(NCH):
            sl = gates(0, n, it)
            if n + 2 < NL:
                conv_chunk(n + 2)
            it += 1
            lo = n * CH
            init = 0.0 if n == 0 else u_sb[:, 2 + lo - 1:2 + lo]
            S.op("dve", "tensor_tensor_scan", reads=[("a", sl), ("m", sl), "hf"], writes=["hf", ("u", n)],
                 out=u_sb[:, 2 + lo:2 + lo + CH], data0=a_sb[:, sl, :], data1=m_sb[:, sl, :], initial=init,
                 op0=ALU.mult, op1=ALU.add)
        for n in range(NCH - 1, -1, -1):
            sl = gates(1, n, it)
            it += 1
            lo = n * CH
            init = 0.0 if n == NCH - 1 else carry[:, 0:1]
            S.op("dve", "tensor_tensor_scan", reads=[("a", sl), ("m", sl), "carry"], writes=["hr"],
                 out=hr_sb[:, 0, ::-1], data0=a_sb[:, sl, ::-1], data1=m_sb[:, sl, ::-1], initial=init,
                 op0=ALU.mult, op1=ALU.add)
            S.op("dve", "tensor_copy", reads=["hr"], writes=["carry"], out=carry[:, 0:1], in_=hr_sb[:, 0, 0:1])
            S.op("dve" if (n % 2 == 0) else "pool", "tensor_tensor", reads=["hr", "hf"], writes=[("yb", sl)],
                 out=yb_sb[:, sl, :], in0=hr_sb[:, 0, :], in1=u_sb[:, 2 + lo:2 + lo + CH], op=ALU.add)
            _dma(S, "sp", hT[lo // NT][:, lo % NT:lo % NT + CH], yb_sb[:, sl, :], reads=[("yb", sl)],
                 writes=[("hT_d", n)], semkey=("yb_o", sl))
            if lo % NT == 0:
                i4 = lo // NT
                S.op("pool", "collective_compute", dma=True, inc=1, semkey="cc_h",
                     reads=[("hT_d", n + q) for q in range(NT // CH)], writes=[("hg", i4)], kind="AllGather",
                     op=ALU.bypass, replica_groups=GROUPS, ins=[hT[i4]], outs=[T["hg"][i4]])
    S.barrier()


def phase_k4(C, T, w_ba, w_bb, w_out, yout, fin=None):
    S = C.S
    yaT, hg, szbT, sgaT, sgbT, xres = T["yaT"], T["hg"], T["szbT"], T["sgaT"], T["sgbT"], T["xres"]
    with C.phase():
        wba = C.sb("wba", [128, 4, D], BF16)
        wbb = C.sb("wbb", [128, 4, D], BF16)
        wo = C.sb("wo", [128, 8, D], BF16)
        wst = C.sb("wst", [128, 2, D])
        ya = C.sb("ya", [128, 2, 4, 512], BF16)
        hb = C.sb("hb", [128, 2, 4, 512], BF16)
        szb = C.sb("szb", [128, 2, 4, 512], BF16)
        yb = C.sb("yb", [128, 2, 4, 512], BF16)
        sga = C.sb("sga", [128, 4, 512], BF16)
        sgb = C.sb("sgb", [128, 4, 512], BF16)
        t1 = C.sb("t1", [128, 2, 512])
        t2 = C.sb("t2", [128, 2, 512])
        mT = C.sb("mT", [128, 2, 8, 512], BF16)
        xr = C.sb("xr", [128, 2, D])
        yo = C.sb("yo", [128, 2, D])
        pa = [C.ps("pa%d" % i, [128, 512]) for i in range(2)]
        pb = [C.ps("pb%d" % i, [128, 512]) for i in range(2)]
        po = [C.ps("po%d" % i, [128, 512]) for i in range(4)]
        if fin is not None:
            lng_sb = C.sb("lng_sb", [128, D])
            lnb_sb = C.sb("lnb_sb", [128, D])
            stats = C.sb("stats", [128, 2, 12])
            mv = C.sb("mv", [128, 2, 2])
            rstd = C.sb("rstd", [128, 2, 1])
            _dma(S, "sp", lng_sb[:], fin[0], writes=["lng"])
            _dma(S, "sp", lnb_sb[:], fin[1], writes=["lnb"])

        def load_w(lst, i):
          for (src, dst, nk, key) in lst:
            for kc in range(nk):
                sl = i % 2
                _dma(S, "sp", wst[:, sl, :], src[kc * 128:(kc + 1) * 128, :], writes=[("wst", sl)])
                if i % 2 == 0:
                    S.op("dve", "tensor_copy", reads=[("wst", sl)], writes=[(key, kc)], out=dst[:, kc, :], in_=wst[:, sl, :])
                else:
                    S.op("act", "activation", reads=[("wst", sl)], writes=[(key, kc)], out=dst[:, kc, :], in_=wst[:, sl, :],
                         func=AF.Copy)
                i += 1

        def load_blk(tb):
            bs = tb % 2
            tsl = slice(tb * 512, (tb + 1) * 512)
            _dma(S, "sp", ya[:, bs, :, :], yaT[:, tsl].rearrange("(k p) t -> p k t", p=128), writes=[("ya", bs)])
            cases = {}
            for cc in range(NCORES):
                j = cc % 4
                cases[cc] = [(hb[:, bs, :, :], hg[j][:, tb * 512:(tb + 1) * 512].rearrange("(k p) t -> p k t", p=128))]
            S.cond_dma("sp", cases, reads=[("hg", i4) for i4 in range(4)], writes=[("hb", bs)], semkey=("hb", bs))
            _dma(S, "sp", szb[:, bs, :, :], szbT[:, tsl].rearrange("(k p) t -> p k t", p=128), writes=[("szb", bs)])

        def mk_yb(tb):
            bs_ = tb % 2
            S.op("dve", "tensor_tensor", reads=[("hb", bs_), ("szb", bs_)], writes=[("yb", bs_)],
                 out=yb[:, bs_, :, :], in0=hb[:, bs_, :, :], in1=szb[:, bs_, :, :], op=ALU.mult)

        load_w(((w_ba, wba, 4, "wba"), (w_bb, wbb, 4, "wbb")), 0)
        load_blk(0)
        mk_yb(0)
        load_w(((w_out, wo, 8, "wo"),), 0)
        pcnt = 0
        ocnt = 0
        for tb in range(NT // 512):
            bs = tb % 2
            if tb + 1 < NT // 512:
                load_blk(tb + 1)
            for oc in range(8):
                ps_ = pcnt % 2
                gs = pcnt % 4
                pcnt += 1
                tsl_ = slice(tb * 512, (tb + 1) * 512)
                _dma(S, "sp", sga[:, gs, :], sgaT[oc * 128:(oc + 1) * 128, tsl_], writes=[("sga", gs)])
                _dma(S, "sp", sgb[:, gs, :], sgbT[oc * 128:(oc + 1) * 128, tsl_], writes=[("sgb", gs)])
                for kc in range(4):
                    S.op("pe", "matmul", reads=[("wba", kc), ("ya", bs)], writes=[("pa", ps_)], signal=(kc == 3),
                         out=pa[ps_][:, :], lhsT=wba[:, kc, oc * 128:(oc + 1) * 128], rhs=ya[:, bs, kc, :],
                         start=(kc == 0), stop=(kc == 3))
                for kc in range(4):
                    S.op("pe", "matmul", reads=[("wbb", kc), ("yb", bs)], writes=[("pb", ps_)], signal=(kc == 3),
                         out=pb[ps_][:, :], lhsT=wbb[:, kc, oc * 128:(oc + 1) * 128], rhs=yb[:, bs, kc, :],
                         start=(kc == 0), stop=(kc == 3))
                S.op("dve", "tensor_tensor", reads=[("pa", ps_), ("sga", gs)], writes=[("t1", ps_)],
                     out=t1[:, ps_, :], in0=pa[ps_][:, :], in1=sga[:, gs, :], op=ALU.mult)
                S.op("dve", "tensor_tensor", reads=[("pb", ps_), ("sgb", gs)], writes=[("t2", ps_)],
                     out=t2[:, ps_, :], in0=pb[ps_][:, :], in1=sgb[:, gs, :], op=ALU.mult)
                S.op("pool" if oc % 2 == 0 else "dve", "tensor_tensor", reads=[("t1", ps_), ("t2", ps_)],
                     writes=[("mT", bs, oc)], out=mT[:, bs, oc, :], in0=t1[:, ps_, :], in1=t2[:, ps_, :], op=ALU.add)
                if oc == 7 and tb + 1 < NT // 512:
                    mk_yb(tb + 1)
            for tt in range(4):
                g = tb * 4 + tt
                xs = g % 2
                _dma(S, "sp", xr[:, xs, :], xres[g * 128:(g + 1) * 128, :], writes=[("xr", xs)])
                for half in range(2):
                    ob = ocnt % 4
                    ocnt += 1
                    for kc in range(8):
                        S.op("pe", "matmul", reads=[("mT", bs, kc), ("wo", kc)], writes=[("po", ob)], signal=(kc == 7),
                             out=po[ob][:, :], lhsT=mT[:, bs, kc, tt * 128:(tt + 1) * 128],
                             rhs=wo[:, kc, half * 512:(half + 1) * 512], start=(kc == 0), stop=(kc == 7))
                    S.op("dve", "scalar_tensor_tensor", reads=[("xr", xs), ("po", ob)], writes=[("yo", xs, half)],
                         out=yo[:, xs, half * 512:(half + 1) * 512], in0=xr[:, xs, half * 512:(half + 1) * 512],
                         scalar=ALPHA, in1=po[ob][:, :], op0=ALU.mult, op1=ALU.add)
                yk = [("yo", xs, 0), ("yo", xs, 1)]
                if fin is not None:
                    for hh in range(2):
                        S.op("dve", "bn_stats", reads=yk, writes=[("stats", xs, hh)],
                             out=stats[:, xs, hh * 6:(hh + 1) * 6], in_=yo[:, xs, hh * 512:(hh + 1) * 512])
                    S.op("dve", "bn_aggr", reads=[("stats", xs, 0), ("stats", xs, 1)], writes=[("mv", xs)],
                         out=mv[:, xs, :], in_=stats[:, xs, :])
                    S.op("act", "activation", reads=[("mv", xs)], writes=[("rstd", xs)],
                         out=rstd[:, xs, :], in_=mv[:, xs, 1:2], func=AF.Sqrt, bias=1e-5, scale=1.0)
                    S.op("dve", "reciprocal", reads=[("rstd", xs)], writes=[("rstd", xs)], out=rstd[:, xs, :],
                         in_=rstd[:, xs, :])
                    S.op("dve", "scalar_tensor_tensor", reads=[("mv", xs), ("rstd", xs)], writes=[("nb", xs)],
                         out=mv[:, xs, 1:2], in0=mv[:, xs, 0:1], scalar=-1.0, in1=rstd[:, xs, :], op0=ALU.mult, op1=ALU.mult)
                    S.op("act", "activation", reads=yk + [("nb", xs), ("rstd", xs)], writes=yk,
                         out=yo[:, xs, :], in_=yo[:, xs, :], func=AF.Identity, scale=rstd[:, xs, :], bias=mv[:, xs, 1:2])
                    S.op("dve", "tensor_tensor", reads=yk + ["lng"], writes=yk,
                         out=yo[:, xs, :], in0=yo[:, xs, :], in1=lng_sb[:], op=ALU.mult)
                    S.op("dve", "tensor_tensor", reads=yk + ["lnb"], writes=yk,
                         out=yo[:, xs, :], in0=yo[:, xs, :], in1=lnb_sb[:], op=ALU.add)
                    _dma(S, "sp", fin[2][g * 128:(g + 1) * 128, :], yo[:, xs, :], reads=yk, semkey=("yo_o", xs))
                else:
                    _dma(S, "sp", yout[g * 128:(g + 1) * 128, :], yo[:, xs, :], reads=yk, semkey=("yo_o", xs))
    S.barrier()


def build_fused():
    C = Ctx()
    xin = C.din("xin", [NT, D])
    lns = C.din("lns", [3, 2, 128, D])
    ident = C.din("ident", [128, 128])
    w_in = C.din("w_in", [2, D, 5120])
    bm = C.din("bm", [2, 128, 16])
    bt = C.din("bt", [2, 4, 128, 2, NTAB, 128])
    cw = C.din("cw", [2, 128, 5])
    gw = C.din("gw", [2, 128, 4, 128])
    gb = C.din("gb", [2, 128, 4])
    lam = C.din("lam", [2, 128, 2])
    w_ba = C.din("w_ba", [2, 512, D])
    w_bb = C.din("w_bb", [2, 512, D])
    w_out = C.din("w_out", [2, D, D])
    zk = C.din("zk", [128, 256], BF16)
    zv = C.din("zv", [256, 128], BF16)
    out = C.dout("out", [NT, D])
    T = dict(ident=ident[:, :], zk=zk, zv=zv)
    T["xres"] = C.dint("xres", [NT, D])
    T["ybuf"] = C.dint("ybuf", [NT, D])
    for n in ("qT", "kT"):
        T[n] = C.dint(n, [512, NT], BF16)
    T["v"] = C.dint("v", [NT, 512], BF16)
    for n in ("szaT", "szbT"):
        T[n] = C.dint(n, [512, NT], BF16)
    T["uT"] = [C.dint("uT%d" % i, [512, 512]) for i in range(8)]
    for n in ("sgaT", "sgbT"):
        T[n] = C.dint(n, [D, NT], BF16)
    T["ug"] = [C.dint("ug%d" % i, [4 * 512, 512]) for i in range(8)]
    T["hb"] = C.dint("hb", [1024, 512], BF16)
    T["hbg"] = C.dint("hbg", [4 * 1024, 512], BF16)
    T["yaT"] = C.dint("yaT", [512, NT], BF16)
    T["hT"] = [C.dint("hT%d" % i, [128, NT], BF16) for i in range(4)]
    T["hg"] = [C.dint("hg%d" % i, [4 * 128, NT], BF16) for i in range(4)]
    yin = xin
    for l in range(2):
        phase_k1(C, T, yin, lns[l, 0], lns[l, 1], T["xres"], w_in=w_in[l], bm=bm[l])
        exchange_after_k1(C, T)
        phase_k2(C, T, bt[l])
        phase_k3(C, T, cw[l], gw[l], gb[l], lam[l])
        exchange_after_k3(C, T)
        phase_k4(C, T, w_ba[l], w_bb[l], w_out[l], T["ybuf"], fin=(lns[2, 0], lns[2, 1], out) if l == 1 else None)
        yin = T["ybuf"]
    return C.done()


def _bias_tables(rpb_l, j):
    H = 8
    tab = np.full((H, NTAB, 128, 128), NEG, np.float32)
    kp = np.arange(128)
    qi = np.arange(128)
    ck = (kp % 64)[:, None]
    cq = (qi % 64)[None, :]
    cs = np.clip(cq - 8, 0, 48)
    colok = (ck >= cs) & (ck <= cs + 15)
    dc = ck - cq + 15
    for cls, (r_loc, win_lo) in enumerate(((0, -4), (2, -4), (8, -4), (62, -6), (60, -4))):
        r = 64 * j + r_loc
        rq = r + (qi // 64)[None, :]
        rs = np.clip(rq - 4, 0, 248)
        for kt in range(CLS_NT[cls]):
            rk = r + win_lo + 2 * kt + (kp // 64)[:, None]
            ok = colok & (rk >= rs) & (rk <= rs + 7)
            dr = rk - rq + 7
            drc = np.clip(dr, 0, 14)
            dcc = np.clip(dc, 0, 30)
            vals = rpb_l[:, drc, dcc]
            tab[:, CLS_OFF[cls] + kt] = np.where(ok[None], vals, NEG)
    t = tab.reshape(4, 2, NTAB, 128, 128).transpose(0, 3, 1, 2, 4)
    return np.ascontiguousarray(t)


_NC = []


def _rep(v):
    v = np.asarray(v, np.float32)
    return np.broadcast_to(v[None, :], (128, v.shape[0]))


def kernel(x, emb_ln_g, emb_ln_b, w_in, rpb, conv_w, conv_b, lru_gate_w, lru_gate_b, lru_lambda,
           w_branch_attn, w_branch_lru, b_merge, w_out, ln_g, ln_b):
    import ml_dtypes
    f = lambda a: np.ascontiguousarray(np.asarray(a, np.float32))
    x = f(x)
    B = x.shape[0]
    if not _NC:
        _NC.append(build_fused())
    nc = _NC[0]
    lns = f(np.stack([np.stack([_rep(emb_ln_g), _rep(emb_ln_b)]), np.stack([_rep(ln_g[0]), _rep(ln_b[0])]),
                      np.stack([_rep(ln_g[1]), _rep(ln_b[1])])]))
    bm = f(np.stack([np.asarray(b_merge[l]).reshape(2, 8, 128).transpose(2, 0, 1).reshape(128, 16) for l in range(2)]))
    common = dict(lns=lns, ident=np.eye(128, dtype=np.float32), w_in=f(w_in), bm=bm, w_ba=f(w_branch_attn),
                  w_bb=f(w_branch_lru), w_out=f(w_out), zk=np.zeros((128, 256), ml_dtypes.bfloat16),
                  zv=np.zeros((256, 128), ml_dtypes.bfloat16))
    perj = []
    for j in range(4):
        ch = slice(j * 128, (j + 1) * 128)
        bt = np.stack([_bias_tables(f(rpb[l]), j) for l in range(2)])
        cw = np.stack([np.concatenate([f(conv_w[l])[:, ch].T, f(conv_b[l])[ch][:, None]], axis=1) for l in range(2)])
        gw = np.zeros((2, 128, 4, 128), np.float32)
        gb = np.zeros((2, 128, 4), np.float32)
        for l in range(2):
            for d in range(2):
                for g in range(2):
                    for bb in range(2):
                        gw[l, bb * 64:(bb + 1) * 64, d * 2 + g, bb * 64:(bb + 1) * 64] = f(lru_gate_w[l][d, g, 2 * j + bb])
                        gb[l, bb * 64:(bb + 1) * 64, d * 2 + g] = f(lru_gate_b[l][d, g, 2 * j + bb])
        lam = np.stack([f(lru_lambda[l])[:, ch].T for l in range(2)])
        perj.append(dict(bt=f(bt), cw=f(cw), gw=gw, gb=gb, lam=f(lam)))
    in_maps = []
    for c in range(NCORES):
        b, j = c // 4, c % 4
        m = dict(common)
        m.update(perj[j])
        m["xin"] = f(x[b, j * NT:(j + 1) * NT])
        in_maps.append(m)
    res = run_bass_kernel_spmd(nc, in_maps, core_ids=list(range(NCORES)))
    out = np.empty((B, SEQ, D), np.float32)
    for c in range(NCORES):
        out[c // 4, (c % 4) * NT:(c % 4 + 1) * NT] = res.results[c]["out"]
    return out
```

```python
import numpy as np
from contextlib import ExitStack
import concourse.bass as bass
import concourse.mybir as mybir
from concourse.bass_utils import run_bass_kernel_spmd

F32 = mybir.dt.float32
BF16 = mybir.dt.bfloat16
AF = mybir.ActivationFunctionType
ALU = mybir.AluOpType

NCORES = 8
NT = 4096
SEQ = 16384
D = 1024
ALPHA = 4.0 ** 0.25
NEG = -30000.0


class Sched:
    ENG = ("pe", "act", "dve", "pool", "sp")

    def __init__(self, nc, stack):
        self.nc = nc
        self.stack = stack
        self.sems = {}
        self.count = {}
        for e in self.ENG:
            self._sem("e:" + e)
        self.lastw = {}
        self.reads = {}
        self.known = {e: {} for e in self.ENG}
        self.stream = {e: [] for e in self.ENG}

    def _sem(self, name):
        if name not in self.sems:
            self.sems[name] = self.stack.enter_context(self.nc.semaphore("s%d" % len(self.sems)))
            self.count[name] = 0
        return self.sems[name]

    def op(self, eng, name, reads=(), writes=(), dma=False, semkey=None, signal=True, inc=16, **kw):
        fn = (name, kw)
        deps = {}

        def add(s, v):
            if v > deps.get(s, 0):
                deps[s] = v

        for k in reads:
            if k in self.lastw:
                add(*self.lastw[k])
        for k in writes:
            if k in self.lastw:
                add(*self.lastw[k])
            for s, v in self.reads.get(k, {}).items():
                add(s, v)
        waits = []
        for s, v in deps.items():
            if s == "e:pe" and eng == "pe":
                continue
            if self.known[eng].get(s, 0) >= v:
                continue
            self.known[eng][s] = v
            waits.append((s, v))
        if dma:
            sk = "d:" + str(semkey if semkey is not None else (list(writes) + list(reads))[0])
            self._sem(sk)
            self.count[sk] += inc
            tok = (sk, self.count[sk])
            emit_tok = (sk, inc)
        else:
            sk = "e:" + eng
            if signal:
                self.count[sk] += 1
                tok = (sk, self.count[sk])
                emit_tok = (sk, 1)
            else:
                assert eng == "pe"
                tok = (sk, self.count[sk] + 1)
                emit_tok = None
        for k in reads:
            d = self.reads.setdefault(k, {})
            if tok[1] > d.get(tok[0], 0):
                d[tok[0]] = tok[1]
        for k in writes:
            self.lastw[k] = tok
            self.reads[k] = {}
        self.stream[eng].append((waits, fn, emit_tok, dma))
        return tok

    def cond_dma(self, eng, cases, reads=(), writes=(), semkey=None):
        n = len(next(iter(cases.values())))
        assert all(len(v) == n for v in cases.values())
        return self.op(eng, "__cond__", reads=reads, writes=writes, dma=True, semkey=semkey, inc=16 * n, cases=cases)

    def barrier(self, exclude="d:cc_"):
        snap = [(s, v) for s, v in self.count.items() if v > 0 and not s.startswith(exclude)]
        for e in self.ENG:
            waits = []
            for s, v in snap:
                if self.known[e].get(s, 0) >= v:
                    continue
                self.known[e][s] = v
                waits.append((s, v))
            self.stream[e].append((waits, None, None, False))
        self.lastw = {k: t for k, t in self.lastw.items() if t[0].startswith(exclude)}
        self.reads = {}

    def finish(self, eng="sp"):
        waits = [(s, v) for s, v in self.count.items() if s.startswith("d:") and v > 0]
        self.stream[eng].append((waits, None, None, False))

    def build(self):
        nc = self.nc
        with nc.Block() as block:
            def run(ename, e):
                core = None
                for waits, fn, tok, dma in self.stream[ename]:
                    for s, v in waits:
                        e.wait_ge(self.sems[s], v)
                    if fn is None:
                        continue
                    if fn[0] == "__cond__":
                        if core is None:
                            core = e.partition_id()
                        for c, lst in fn[1]["cases"].items():
                            with e.If(core == c):
                                for (o, i) in lst:
                                    e.dma_start(out=o, in_=i).then_inc(self.sems[tok[0]], 16)
                        continue
                    ins = getattr(e, fn[0])(**fn[1])
                    if tok is not None:
                        ins.then_inc(self.sems[tok[0]], tok[1])

            @block.tensor
            def _(e):
                run("pe", e)

            @block.scalar
            def _(e):
                run("act", e)

            @block.vector
            def _(e):
                run("dve", e)

            @block.gpsimd
            def _(e):
                run("pool", e)

            @block.sync
            def _(e):
                run("sp", e)


class Ctx:
    def __init__(self):
        self.nc = bass.Bass("TRN2", target_bir_lowering=False)
        self.st = ExitStack()
        self.S = Sched(self.nc, self.st)
        self.pst = None
        self.uid = 0

    def din(self, name, shape, dt=F32):
        return self.nc.dram_tensor(name, list(shape), dt, kind="ExternalInput").ap()

    def dout(self, name, shape, dt=F32):
        return self.nc.dram_tensor(name, list(shape), dt, kind="ExternalOutput").ap()

    def dint(self, name, shape, dt=F32):
        return self.nc.dram_tensor(name, list(shape), dt, kind="Internal").ap()

    def phase(self):
        self.pst = ExitStack()
        return self.pst

    def sb(self, name, shape, dt=F32):
        self.uid += 1
        return self.pst.enter_context(self.nc.sbuf_tensor("%s_%d" % (name, self.uid), list(shape), dt))

    def ps(self, name, shape, dt=F32):
        self.uid += 1
        return self.pst.enter_context(self.nc.psum_tensor("%s_%d" % (name, self.uid), list(shape), dt))

    def done(self):
        self.S.finish()
        self.S.build()
        self.st.close()
        return self.nc


def _dma(S, eng, out, in_, reads=(), writes=(), semkey=None):
    return S.op(eng, "dma_start", reads=reads, writes=writes, dma=True, semkey=semkey, out=out, in_=in_)


GROUPS = [[0, 1, 2, 3], [4, 5, 6, 7]]
CLS_NT = [6, 6, 5, 6, 5]
CLS_OFF = [0, 6, 12, 17, 23]
NTAB = 28
KROWS = 72
NK = KROWS * 64


def tile_class(t):
    if t == 0:
        return 0, 0
    if t == 1:
        return 1, 1
    if t == 31:
        return 3, 30
    if t == 30:
        return 4, 30
    return 2, t


def phase_k1(C, T, yin, lng, lnb, xres, w_in=None, bm=None, do_proj=True):
    S = C.S
    with C.phase():
        lng_sb = C.sb("lng_sb", [128, D])
        lnb_sb = C.sb("lnb_sb", [128, D])
        NLS = 4
        yt = C.sb("yt", [128, NLS, D])
        xn = C.sb("xn", [128, NLS, D])
        stats = C.sb("stats", [128, NLS, 12])
        mv = C.sb("mv", [128, NLS, 2])
        rstd = C.sb("rstd", [128, NLS, 1])
        _dma(S, "sp", lng_sb[:], lng, writes=["lng"])
        _dma(S, "sp", lnb_sb[:], lnb, writes=["lnb"])
        if do_proj:
            qT, kT, vo, szaT, uT, szbT, sgaT, sgbT = (T[k] for k in ("qT", "kT", "v", "szaT", "uT", "szbT", "sgaT", "sgbT"))
            wbf = C.sb("wbf", [128, 8, 5120], BF16)
            wst = C.sb("wst", [128, 4, 1280])
            ident_f = C.sb("ident_f", [128, 128])
            identb = C.sb("identb", [128, 128], BF16)
            bm_sb = C.sb("bm_sb", [128, 16])
            xnb = C.sb("xnb", [128, 2, D], BF16)
            xT = C.sb("xT", [128, 2, 8, 512], BF16)
            NOF, NOB = 4, 8
            ostf = C.sb("ostf", [128, NOF, 512])
            ostb = C.sb("ostb", [128, NOB, 512], BF16)
            pst = [C.ps("pst%d" % i, [128, 8, 128], BF16) for i in range(2)]
            NPS = 6
            psm = [C.ps("psm%d" % i, [128, 512]) for i in range(NPS)]
            _dma(S, "sp", ident_f[:], T["ident"], writes=["identf"])
            _dma(S, "sp", bm_sb[:], bm, writes=["bm"])
            S.op("dve", "tensor_copy", reads=["identf"], writes=["identb"], out=identb[:], in_=ident_f[:])
            cast_engs = ["dve", "act", "act"]
            def load_weights():
                for i, (half, kc) in enumerate([(h, k) for h in range(4) for k in range(8)]):
                    if True:
                        sl = i % 4
                        _dma(S, "sp", wst[:, sl, :], w_in[kc * 128:(kc + 1) * 128, half * 1280:(half + 1) * 1280],
                             writes=[("wst", sl)])
                        ce = cast_engs[i % 3]
                        o = wbf[:, kc, half * 1280:(half + 1) * 1280]
                        if ce == "act":
                            S.op("act", "activation", reads=[("wst", sl)], writes=[("wbf", kc, half)],
                                 out=o, in_=wst[:, sl, :], func=AF.Copy)
                        else:
                            S.op(ce, "tensor_copy", reads=[("wst", sl)], writes=[("wbf", kc, half)], out=o, in_=wst[:, sl, :])

        cnt = {"f": 0, "b": 0, "ps": 0}
        hb_keys = []

        def ln_tile(g):
            s2 = g % NLS
            _dma(S, "sp", yt[:, s2, :], yin[g * 128:(g + 1) * 128, :], writes=[("yt", s2)])
            for hh in range(2):
                S.op("dve", "bn_stats", reads=[("yt", s2)], writes=[("stats", s2, hh)],
                     out=stats[:, s2, hh * 6:(hh + 1) * 6], in_=yt[:, s2, hh * 512:(hh + 1) * 512])
            S.op("dve", "bn_aggr", reads=[("stats", s2, 0), ("stats", s2, 1)], writes=[("mv", s2)],
                 out=mv[:, s2, :], in_=stats[:, s2, :])
            S.op("act", "activation", reads=[("mv", s2)], writes=[("rstd", s2)],
                 out=rstd[:, s2, :], in_=mv[:, s2, 1:2], func=AF.Sqrt, bias=1e-5, scale=1.0)
            S.op("dve", "reciprocal", reads=[("rstd", s2)], writes=[("rstd", s2)], out=rstd[:, s2, :], in_=rstd[:, s2, :])
            S.op("dve", "scalar_tensor_tensor", reads=[("mv", s2), ("rstd", s2)], writes=[("nb", s2)],
                 out=mv[:, s2, 1:2], in0=mv[:, s2, 0:1], scalar=-1.0, in1=rstd[:, s2, :], op0=ALU.mult, op1=ALU.mult)
            S.op("act", "activation", reads=[("yt", s2), ("nb", s2), ("rstd", s2)], writes=[("xn", s2)],
                 out=xn[:, s2, :], in_=yt[:, s2, :], func=AF.Identity, scale=rstd[:, s2, :], bias=mv[:, s2, 1:2])
            S.op("dve", "tensor_tensor", reads=[("xn", s2), "lng"], writes=[("xn", s2)],
                 out=xn[:, s2, :], in0=xn[:, s2, :], in1=lng_sb[:], op=ALU.mult)
            S.op("dve", "tensor_tensor", reads=[("xn", s2), "lnb"], writes=[("xn", s2)],
                 out=xn[:, s2, :], in0=xn[:, s2, :], in1=lnb_sb[:], op=ALU.add)
            _dma(S, "sp", xres[g * 128:(g + 1) * 128, :], xn[:, s2, :], reads=[("xn", s2)], semkey=("xn_o", s2))

        def tr_tile(g):
            s4 = g % NLS
            s2 = g % 2
            tb, tt = g // 4, g % 4
            bs = tb % 2
            S.op("act", "activation", reads=[("xn", s4)], writes=[("xnb", s2)],
                 out=xnb[:, s2, :], in_=xn[:, s4, :], func=AF.Copy)
            for kc in range(8):
                S.op("pe", "transpose", reads=[("xnb", s2), "identb"], writes=[("pst", s2)], signal=(kc == 7),
                     out=pst[s2][:, kc, :], in_=xnb[:, s2, kc * 128:(kc + 1) * 128], identity=identb[:])
            S.op("act", "activation", reads=[("pst", s2)], writes=[("xT", bs, tt)],
                 out=xT[:, bs, :, tt * 128:(tt + 1) * 128], in_=pst[s2][:, :, :], func=AF.Copy)

        def evac(kind, c, bank, tb):
            src = psm[bank][:, :]
            tsl = slice(tb * 512, (tb + 1) * 512)
            if kind in ("q", "k"):
                sl = cnt["b"] % NOB
                cnt["b"] += 1
                dst = ostb[:, sl, :]
                if kind == "q":
                    S.op("dve", "tensor_scalar", reads=[("psm", bank)], writes=[("ostb", sl)],
                         out=dst, in0=src, scalar1=0.125, scalar2=None, op0=ALU.mult)
                    tgt = qT
                else:
                    S.op("dve", "tensor_copy", reads=[("psm", bank)], writes=[("ostb", sl)], out=dst, in_=src)
                    tgt = kT
                    if tb == 0:
                        _dma(S, "sp", T["hb"][c * 128:(c + 1) * 128, 0:256], ostb[:, sl, 0:256], reads=[("ostb", sl)],
                             writes=[("hb_d", "kt", c)], semkey=("hbs", "kt", c))
                        hb_keys.append(("hb_d", "kt", c))
                    if tb == NT // 512 - 1:
                        _dma(S, "sp", T["hb"][c * 128:(c + 1) * 128, 256:512], ostb[:, sl, 256:512], reads=[("ostb", sl)],
                             writes=[("hb_d", "kb", c)], semkey=("hbs", "kb", c))
                        hb_keys.append(("hb_d", "kb", c))
                _dma(S, "sp", tgt[c * 128:(c + 1) * 128, tsl], dst, reads=[("ostb", sl)], semkey=("ostb", sl))
                return
            if kind == "u":
                sl = cnt["f"] % NOF
                cnt["f"] += 1
                dst = ostf[:, sl, :]
                S.op("dve", "tensor_copy", reads=[("psm", bank)], writes=[("ostf", sl)], out=dst, in_=src)
                _dma(S, "sp", uT[tb][c * 128:(c + 1) * 128, :], dst, reads=[("ostf", sl)], writes=[("uT_d", tb, c)],
                     semkey=("ostf", sl))
                return
            sl = cnt["b"] % NOB
            cnt["b"] += 1
            dst = ostb[:, sl, :]
            if kind in ("za", "zb"):
                S.op("act", "activation", reads=[("psm", bank)], writes=[("ostb", sl)], out=dst, in_=src, func=AF.Silu)
                tgt = szaT if kind == "za" else szbT
            else:
                col = c if kind == "ga" else 8 + c
                S.op("act", "activation", reads=[("psm", bank), "bm"], writes=[("ostb", sl)],
                     out=dst, in_=src, func=AF.Sigmoid, bias=bm_sb[:, col:col + 1])
                tgt = sgaT if kind == "ga" else sgbT
            _dma(S, "sp", tgt[c * 128:(c + 1) * 128, tsl], dst, reads=[("ostb", sl)], semkey=("ostb", sl))

        fam = ([("q", i) for i in range(4)] + [("k", i) for i in range(4)] + [("v", i) for i in range(4)]
               + [("za", i) for i in range(4)] + [("u", i) for i in range(4)] + [("zb", i) for i in range(4)]
               + [("ga", i) for i in range(8)] + [("gb", i) for i in range(8)])

        def proj_block(tb):
            bs = tb % 2
            xkeys = [("xT", bs, tt) for tt in range(4)]
            nxt = 0

            def v_part():
                for tt in range(4):
                    bank = cnt["ps"] % NPS
                    cnt["ps"] += 1
                    for kc in range(8):
                        S.op("pe", "matmul", reads=[("xT", bs, tt), ("wbf", kc, 0), ("wbf", kc, 1)], writes=[("psm", bank)],
                             signal=(kc == 7),
                             out=psm[bank][:, :], lhsT=xT[:, bs, kc, tt * 128:(tt + 1) * 128], rhs=wbf[:, kc, 1024:1536],
                             start=(kc == 0), stop=(kc == 7))
                    sl = cnt["b"] % NOB
                    cnt["b"] += 1
                    S.op("dve", "tensor_copy", reads=[("psm", bank)], writes=[("ostb", sl)], out=ostb[:, sl, :], in_=psm[bank][:, :])
                    g = tb * 4 + tt
                    _dma(S, "sp", vo[g * 128:(g + 1) * 128, :], ostb[:, sl, :], reads=[("ostb", sl)], semkey=("ostb", sl))
                    if g < 2 or g >= NT // 128 - 2:
                        hr0 = g * 128 if g < 2 else 256 + (g - (NT // 128 - 2)) * 128
                        _dma(S, "sp", T["hb"][512 + hr0:512 + hr0 + 128, :], ostb[:, sl, :], reads=[("ostb", sl)],
                             writes=[("hb_d", "v", g)], semkey=("hbs", "v", g))
                        hb_keys.append(("hb_d", "v", g))

            last = (tb == NT // 512 - 1)
            if last:
                v_part()
            for occ, (kind, ci) in enumerate(fam):
                if kind == "v":
                    continue
                if tb == NT // 512 - 1 and occ == 12:
                    S.op("pool", "collective_compute", dma=True, inc=1, semkey="cc_a", reads=list(hb_keys), writes=["hbg"],
                         kind="AllGather", op=ALU.bypass, replica_groups=GROUPS, ins=[T["hb"]], outs=[T["hbg"]])
                if tb + 1 < NT // 512 and occ in (2, 12, 22, 32):
                    ln_tile((tb + 1) * 4 + nxt)
                    nxt += 1
                bank = cnt["ps"] % NPS
                cnt["ps"] += 1
                half = occ // 10
                for kc in range(8):
                    S.op("pe", "matmul", reads=xkeys + [("wbf", kc, half)], writes=[("psm", bank)], signal=(kc == 7),
                         out=psm[bank][:, :], lhsT=wbf[:, kc, occ * 128:(occ + 1) * 128], rhs=xT[:, bs, kc, :],
                         start=(kc == 0), stop=(kc == 7))
                evac(kind, ci, bank, tb)
            if not last:
                v_part()
            if tb + 1 < NT // 512:
                for tt in range(4):
                    tr_tile((tb + 1) * 4 + tt)

        NTILE = NT // 128
        if not do_proj:
            for g in range(NTILE):
                ln_tile(g)
        else:
            for tt in range(4):
                ln_tile(tt)
                tr_tile(tt)
            load_weights()
            for tb in range(NT // 512):
                proj_block(tb)
    S.barrier()


def phase_ln(C, yin, lng, lnb, xout):
    S = C.S
    G, NS = 4, 3
    with C.phase():
        lng_sb = C.sb("lng_sb", [128, D])
        lnb_sb = C.sb("lnb_sb", [128, D])
        yt = C.sb("yt", [128, NS, G, D])
        xn = C.sb("xn", [128, NS, G, D])
        stats = C.sb("stats", [128, NS, G, 12])
        mv = C.sb("mv", [128, NS, G, 2])
        rstd = C.sb("rstd", [128, NS, G, 1])
        _dma(S, "sp", lng_sb[:], lng, writes=["lng"])
        _dma(S, "sp", lnb_sb[:], lnb, writes=["lnb"])
        for gi in range(NT // (128 * G)):
            s = gi % NS
            rows = slice(gi * 128 * G, (gi + 1) * 128 * G)
            _dma(S, "sp", yt[:, s, :, :], yin[rows, :].rearrange("(t p) d -> p t d", p=128), writes=[("yt", s)])
            for t in range(G):
                for hh in range(2):
                    S.op("dve", "bn_stats", reads=[("yt", s)], writes=[("stats", s, t, hh)],
                         out=stats[:, s, t, hh * 6:(hh + 1) * 6], in_=yt[:, s, t, hh * 512:(hh + 1) * 512])
                S.op("dve", "bn_aggr", reads=[("stats", s, t, 0), ("stats", s, t, 1)], writes=[("mv", s, t)],
                     out=mv[:, s, t, :], in_=stats[:, s, t, :])
                S.op("act", "activation", reads=[("mv", s, t)], writes=[("rstd", s, t)],
                     out=rstd[:, s, t, :], in_=mv[:, s, t, 1:2], func=AF.Sqrt, bias=1e-5, scale=1.0)
                S.op("dve", "reciprocal", reads=[("rstd", s, t)], writes=[("rstd", s, t)], out=rstd[:, s, t, :],
                     in_=rstd[:, s, t, :])
                S.op("dve", "scalar_tensor_tensor", reads=[("mv", s, t), ("rstd", s, t)], writes=[("nb", s, t)],
                     out=mv[:, s, t, 1:2], in0=mv[:, s, t, 0:1], scalar=-1.0, in1=rstd[:, s, t, :], op0=ALU.mult, op1=ALU.mult)
                S.op("act", "activation", reads=[("yt", s), ("nb", s, t), ("rstd", s, t)], writes=[("xn", s, t)],
                     out=xn[:, s, t, :], in_=yt[:, s, t, :], func=AF.Identity, scale=rstd[:, s, t, :], bias=mv[:, s, t, 1:2])
                S.op("dve", "tensor_tensor", reads=[("xn", s, t), "lng"], writes=[("xn", s, t)],
                     out=xn[:, s, t, :], in0=xn[:, s, t, :], in1=lng_sb[:], op=ALU.mult)
                S.op("dve", "tensor_tensor", reads=[("xn", s, t), "lnb"], writes=[("xn", s, t)],
                     out=xn[:, s, t, :], in0=xn[:, s, t, :], in1=lnb_sb[:], op=ALU.add)
            _dma(S, "sp", xout[rows, :].rearrange("(t p) d -> p t d", p=128), xn[:, s, :, :],
                 reads=[("xn", s, t) for t in range(G)], semkey=("xn_o", s))
    S.barrier()


def exchange_after_k1(C, T):
    S = C.S
    for tb in range(8):
        S.op("pool", "collective_compute", dma=True, inc=1, semkey="cc_a", writes=[("ug", tb)], kind="AllGather",
             op=ALU.bypass, replica_groups=GROUPS, ins=[T["uT"][tb]], outs=[T["ug"][tb]])


def exchange_after_k3(C, T):
    pass


def phase_k2(C, T, bt):
    S = C.S
    qT, kT, vo, szaT, yaT = T["qT"], T["kT"], T["v"], T["szaT"], T["yaT"]
    hbg, zk, zv = T["hbg"], T["zk"], T["zv"]
    with C.phase():
        q_sb = C.sb("q_sb", [128, 2, NT], BF16)
        k_sb = C.sb("k_sb", [128, 2, NK], BF16)
        v_sb = C.sb("v_sb", [128, 2, NK // 128, 128], BF16)
        sza_sb = C.sb("sza_sb", [128, 2, NT], BF16)
        bt_raw = C.sb("bt_raw", [128, 2, NTAB, 128])
        bt_sb = C.sb("bt_sb", [128, 2, NTAB, 128])
        ya_sb = C.sb("ya_sb", [128, 2, NT], BF16)
        ones_b = C.sb("ones_b", [128, 64], BF16)
        NU = 3
        E = C.sb("E", [128, NU, 768])
        Pm = C.sb("Pm", [128, NU, 768], BF16)
        rden = C.sb("rden", [128, 2, 128])
        tmp = C.sb("tmp", [128, 2, 128])
        spa = [C.ps("spa%d" % i, [128, 512]) for i in range(NU)]
        spb = [C.ps("spb%d" % i, [128, 256]) for i in range(NU)]
        sps = [[spa[i][:, :], spb[i][:, :]] for i in range(NU)]
        nd = [C.ps("nd0", [128, 128])] * 2
        dn = [C.ps("dn0", [128, 128])] * 2

        S.op("dve", "memset", writes=["ones"], ap=ones_b[:], constant=1.0)

        def load_pair(p):
            s = p % 2
            prow = slice(p * 128, (p + 1) * 128)
            _dma(S, "sp", q_sb[:, s, :], qT[prow, :], writes=[("q", s)])
            _dma(S, "sp", k_sb[:, s, 256:256 + NT], kT[prow, :], writes=[("k", s)], semkey=("kmain", s))
            _dma(S, "sp", v_sb[:, s, 2:34, :], vo[:, prow].rearrange("(t p) f -> p t f", p=128),
                 writes=[("v", s)], semkey=("vmain", s))
            cases = {}
            for c in range(NCORES):
                j = c % 4
                lst = []
                lst.append((k_sb[:, s, 0:256], zk[:, :] if j == 0 else hbg[(j - 1) * 1024 + p * 128:(j - 1) * 1024 + (p + 1) * 128, 256:512]))
                lst.append((k_sb[:, s, 256 + NT:NK], zk[:, :] if j == 3 else hbg[(j + 1) * 1024 + p * 128:(j + 1) * 1024 + (p + 1) * 128, 0:256]))
                lst.append((v_sb[:, s, 0:2, :], (zv[:, :] if j == 0 else hbg[(j - 1) * 1024 + 768:(j - 1) * 1024 + 1024, prow]).rearrange("(t p) f -> p t f", p=128)))
                lst.append((v_sb[:, s, 34:36, :], (zv[:, :] if j == 3 else hbg[(j + 1) * 1024 + 512:(j + 1) * 1024 + 768, prow]).rearrange("(t p) f -> p t f", p=128)))
                cases[c] = lst
            S.cond_dma("sp", cases, reads=["hbg"], writes=[("k", s), ("v", s)], semkey=("halo", s))
            _dma(S, "sp", sza_sb[:, s, :], szaT[prow, :], writes=[("sza", s)])
            _dma(S, "sp", bt_raw[:], bt[p], writes=["bt_raw"])

        load_pair(0)
        torder = list(range(32))
        units = [(t, hh) for t in torder for hh in range(2)]

        def emit_qk(idx, s):
            t, hh = units[idx]
            cls, vt0 = tile_class(t)
            nkt = CLS_NT[cls]
            hs = idx % NU
            pr = slice(hh * 64, (hh + 1) * 64)
            for kt in range(nkt):
                bank = sps[hs][kt // 4]
                co = (kt % 4) * 128
                kcol = (vt0 + kt) * 128
                last = (kt == nkt - 1) or (kt == 3)
                S.op("pe", "matmul", reads=[("k", s), ("q", s)],
                     writes=[("sps", hs, kt // 4)], signal=last,
                     out=bank[:, co:co + 128], lhsT=k_sb[pr, s, kcol:kcol + 128],
                     rhs=q_sb[pr, s, t * 128:(t + 1) * 128], start=True, stop=True)

        def emit_rest(idx, s):
            t, hh = units[idx]
            cls, vt0 = tile_class(t)
            nkt = CLS_NT[cls]
            toff = CLS_OFF[cls]
            par = t % 2
            hs = idx % NU
            pr = slice(hh * 64, (hh + 1) * 64)
            n0 = min(nkt, 4) * 128
            S.op("act", "activation", reads=[("sps", hs, 0)], writes=[("E", hs, 0)],
                 out=E[:, hs, 0:n0], in_=sps[hs][0][:, 0:n0], func=AF.Exp)
            n1 = (nkt - 4) * 128
            S.op("act", "activation", reads=[("sps", hs, 1)], writes=[("E", hs, 1)],
                 out=E[:, hs, 512:512 + n1], in_=sps[hs][1][:, 0:n1], func=AF.Exp)
            S.op("dve", "tensor_tensor", reads=[("E", hs, 0), ("bt", hh)], writes=[("Pm", hs, 0)],
                 out=Pm[:, hs, 0:512], in0=E[:, hs, 0:512],
                 in1=bt_sb[:, hh, toff:toff + 4, :].rearrange("p a b -> p (a b)"), op=ALU.mult)
            S.op("dve", "tensor_tensor", reads=[("E", hs, 1), ("bt", hh)], writes=[("Pm", hs, 1)],
                 out=Pm[:, hs, 512:nkt * 128], in0=E[:, hs, 512:nkt * 128],
                 in1=bt_sb[:, hh, toff + 4:toff + nkt, :].rearrange("p a b -> p (a b)"), op=ALU.mult)
            for kt in range(nkt):
                S.op("pe", "matmul", reads=[("Pm", hs, kt // 4), ("v", s)],
                     writes=["nd"], signal=False,
                     out=nd[par][pr, 0:128], lhsT=v_sb[:, s, vt0 + kt, hh * 64:(hh + 1) * 64],
                     rhs=Pm[:, hs, kt * 128:(kt + 1) * 128], start=(kt == 0), stop=(kt == nkt - 1))
                S.op("pe", "matmul", reads=[("Pm", hs, kt // 4), "ones"], writes=["dn"], signal=(kt == nkt - 1),
                     out=dn[par][pr, 0:128], lhsT=ones_b[:, :],
                     rhs=Pm[:, hs, kt * 128:(kt + 1) * 128], start=(kt == 0), stop=(kt == nkt - 1))
            if hh == 1:
                S.op("dve", "reciprocal", reads=["dn"], writes=[("rden", par)],
                     out=rden[:, par, :], in_=dn[par][:, 0:128])
                S.op("dve", "tensor_tensor", reads=["nd", ("rden", par)], writes=[("tmp", par)],
                     out=tmp[:, par, :], in0=nd[par][:, 0:128], in1=rden[:, par, :], op=ALU.mult)
                S.op("pool", "tensor_tensor", reads=[("tmp", par), ("sza", s)], writes=[("ya", s)],
                     out=ya_sb[:, s, t * 128:(t + 1) * 128], in0=tmp[:, par, :],
                     in1=sza_sb[:, s, t * 128:(t + 1) * 128], op=ALU.mult)

        for p in range(4):
            s = p % 2
            for hh in range(2):
                S.op("act", "activation", reads=["bt_raw"], writes=[("bt", hh)],
                     out=bt_sb[:, hh, :, :], in_=bt_raw[:, hh, :, :], func=AF.Exp)
            if p + 1 < 4:
                load_pair(p + 1)
            emit_qk(0, s)
            emit_qk(1, s)
            for idx in range(len(units)):
                if idx + 2 < len(units):
                    emit_qk(idx + 2, s)
                emit_rest(idx, s)
            _dma(S, "sp", yaT[p * 128:(p + 1) * 128, :], ya_sb[:, s, :], reads=[("ya", s)], semkey=("ya_o", s))
    S.barrier()


def phase_k3(C, T, cw, gw, gb, lam):
    S = C.S
    ug, hT = T["ug"], T["hT"]
    CH = 2048
    NCH = SEQ // CH
    with C.phase():
        u_sb = C.sb("u_sb", [128, SEQ + 3])
        uc = C.sb("uc", [128, SEQ])
        cw_sb = C.sb("cw_sb", [128, 5])
        gw_sb = C.sb("gw_sb", [128, 4, 128])
        gb_sb = C.sb("gb_sb", [128, 4])
        lam_sb = C.sb("lam_sb", [128, 2])
        cs = C.sb("cs", [128, 12, 2])
        csp = C.sb("csp", [128, 2])
        csp2 = C.sb("csp2", [128, 2])
        r_sb = C.sb("r_sb", [128, 1, CH])
        i_sb = C.sb("i_sb", [128, 1, CH])
        a_sb = C.sb("a_sb", [128, 2, CH])
        m_sb = C.sb("m_sb", [128, 2, CH])
        hr_sb = C.sb("hr_sb", [128, 1, CH])
        yb_sb = C.sb("yb_sb", [128, 2, CH], BF16)
        carry = C.sb("carry", [128, 2])
        NH = CH // 512
        psr = [[C.ps("psr%d_%d" % (i, j), [128, 512]) for j in range(NH)] for i in range(1)]
        psi = [[C.ps("psi%d_%d" % (i, j), [128, 512]) for j in range(NH)] for i in range(1)]

        _dma(S, "sp", cw_sb[:], cw, writes=["cw"])
        _dma(S, "sp", gw_sb[:], gw, writes=["gw"])
        _dma(S, "sp", gb_sb[:], gb, writes=["gb"])
        _dma(S, "sp", lam_sb[:], lam, writes=["lam"])
        S.op("dve", "memset", writes=["upad0"], ap=u_sb[:, 0:2], constant=0.0)
        S.op("dve", "memset", writes=["upad1"], ap=u_sb[:, SEQ + 2:SEQ + 3], constant=0.0)
        LCH = 2048
        NL = SEQ // LCH
        for c in range(NL):
            i, off = (c * LCH) // NT, (c * LCH) % NT
            cases = {}
            for cc in range(NCORES):
                j = cc % 4
                cases[cc] = [(u_sb[:, 2 + c * LCH + q * 512:2 + c * LCH + (q + 1) * 512],
                              ug[off // 512 + q][i * 512 + j * 128:i * 512 + (j + 1) * 128, :]) for q in range(LCH // 512)]
            S.cond_dma("sp", cases, reads=[("ug", off // 512 + q) for q in range(LCH // 512)], writes=[("u", c)],
                       semkey=("u", c % 2))

        V = lambda i: cs[:, i, :]
        k = dict(reads=["cs", "lam"], writes=["cs"])
        S.op("dve", "tensor_scalar", out=V(8), in0=lam_sb[:], scalar1=-1.0, scalar2=None, op0=ALU.mult, **k)
        S.op("dve", "tensor_tensor", out=V(0), in0=V(8), in1=lam_sb[:], op=ALU.max, **k)
        S.op("act", "activation", out=V(1), in_=V(0), func=AF.Exp, scale=-1.0, **k)
        S.op("dve", "tensor_scalar", out=V(2), in0=V(1), scalar1=2.0, scalar2=None, op0=ALU.add, **k)
        S.op("dve", "reciprocal", out=V(2), in_=V(2), **k)
        S.op("dve", "tensor_tensor", out=V(3), in0=V(1), in1=V(2), op=ALU.mult, **k)
        S.op("dve", "tensor_tensor", out=V(4), in0=V(3), in1=V(3), op=ALU.mult, **k)
        S.op("dve", "tensor_scalar", out=V(5), in0=V(4), scalar1=1.0 / 13, scalar2=1.0 / 11, op0=ALU.mult, op1=ALU.add, **k)
        for cden in (9.0, 7.0, 5.0, 3.0, 1.0):
            S.op("dve", "tensor_tensor", out=V(5), in0=V(5), in1=V(4), op=ALU.mult, **k)
            S.op("dve", "tensor_scalar", out=V(5), in0=V(5), scalar1=1.0 / cden, scalar2=None, op0=ALU.add, **k)
        S.op("dve", "tensor_tensor", out=V(5), in0=V(5), in1=V(3), op=ALU.mult, **k)
        S.op("dve", "tensor_scalar", out=V(6), in0=lam_sb[:], scalar1=-1.0, scalar2=0.0, op0=ALU.mult, op1=ALU.max, **k)
        S.op("dve", "scalar_tensor_tensor", out=V(7), in0=V(5), scalar=2.0, in1=V(6), op0=ALU.mult, op1=ALU.add, **k)
        S.op("dve", "tensor_scalar", reads=["cs"], writes=["csp"], out=csp[:], in0=V(7), scalar1=-8.0, scalar2=None, op0=ALU.mult)
        S.op("dve", "tensor_scalar", reads=["cs"], writes=["csp2"], out=csp2[:], in0=V(7), scalar1=-16.0, scalar2=None, op0=ALU.mult)

        def conv_chunk(c):
            lo = c * LCH
            rd = [("u", c), "cw", "upad0", "upad1"] + ([("u", c - 1)] if c > 0 else []) + ([("u", c + 1)] if c + 1 < NL else [])
            o = uc[:, lo:lo + LCH]
            S.op("dve", "tensor_scalar", reads=rd, writes=[("uc", c)], out=o, in0=u_sb[:, lo:lo + LCH],
                 scalar1=cw_sb[:, 0:1], scalar2=cw_sb[:, 4:5], op0=ALU.mult, op1=ALU.add)
            for j in range(1, 4):
                S.op("dve", "scalar_tensor_tensor", reads=rd + [("uc", c)], writes=[("uc", c)], out=o,
                     in0=u_sb[:, lo + j:lo + j + LCH], scalar=cw_sb[:, j:j + 1], in1=o, op0=ALU.mult, op1=ALU.add)
        conv_chunk(0)
        conv_chunk(1)
        UCK = [("uc", c) for c in range(NL)]
        UK = [("u", c) for c in range(NL)]

        def gates(d, n, it):
            sl = it % 2
            lo = n * CH
            uck = ("uc", lo // LCH)
            for g, pst in ((0, psr), (1, psi)):
                for hf in range(NH):
                    S.op("pe", "matmul", reads=["gw", uck], writes=[("ps", g, hf)],
                         out=pst[0][hf][:, :], lhsT=gw_sb[:, d * 2 + g, :],
                         rhs=uc[:, lo + hf * 512:lo + (hf + 1) * 512], start=True, stop=True)
            for hf in range(NH):
                fs = slice(hf * 512, (hf + 1) * 512)
                S.op("act", "activation", reads=[("ps", 0, hf), "gb"], writes=[("r", hf)],
                     out=r_sb[:, 0, fs], in_=psr[0][hf][:, :], func=AF.Sigmoid, bias=gb_sb[:, d * 2:d * 2 + 1])
            for hf in range(NH):
                fs = slice(hf * 512, (hf + 1) * 512)
                S.op("act", "activation", reads=[("ps", 1, hf), "gb"], writes=[("i", hf)],
                     out=i_sb[:, 0, fs], in_=psi[0][hf][:, :], func=AF.Sigmoid, bias=gb_sb[:, d * 2 + 1:d * 2 + 2])
            rk = [("r", hf) for hf in range(NH)]
            ik = [("i", hf) for hf in range(NH)]
            S.op("act", "activation", reads=rk + ["csp"], writes=[("a", sl)],
                 out=a_sb[:, sl, :], in_=r_sb[:, 0, :], func=AF.Exp, scale=csp[:, d:d + 1])
            S.op("dve", "tensor_tensor", reads=[("a", sl)], writes=[("m", sl)],
                 out=m_sb[:, sl, :], in0=a_sb[:, sl, :], in1=a_sb[:, sl, :], op=ALU.mult)
            S.op("act", "activation", reads=[("m", sl)], writes=[("m", sl)], out=m_sb[:, sl, :], in_=m_sb[:, sl, :],
                 func=AF.Sqrt, scale=-1.0, bias=1.0)
            S.op("dve", "tensor_tensor", reads=ik + [uck], writes=ik, out=i_sb[:, 0, :], in0=i_sb[:, 0, :],
                 in1=uc[:, lo:lo + CH], op=ALU.mult)
            first = (d == 0 and n == 0) or (d == 1 and n == NCH - 1)
            if first:
                col = 0 if d == 0 else CH - 1
                S.op("dve", "memset", reads=[("m", sl)], writes=[("m", sl)], ap=m_sb[:, sl, col:col + 1], constant=1.0)
            S.op("dve", "tensor_tensor", reads=[("m", sl)] + ik, writes=[("m", sl)], out=m_sb[:, sl, :],
                 in0=m_sb[:, sl, :], in1=i_sb[:, 0, :], op=ALU.mult)
            return sl

        it = 0
        for n in range(NCH):
            sl = gates(0, n, it)
            if n + 2 < NL:
                conv_chunk(n + 2)
            it += 1
            lo = n * CH
            init = 0.0 if n == 0 else u_sb[:, 2 + lo - 1:2 + lo]
            S.op("dve", "tensor_tensor_scan", reads=[("a", sl), ("m", sl), "hf"], writes=["hf", ("u", n)],
                 out=u_sb[:, 2 + lo:2 + lo + CH], data0=a_sb[:, sl, :], data1=m_sb[:, sl, :], initial=init,
                 op0=ALU.mult, op1=ALU.add)
        for n in range(NCH - 1, -1, -1):
            sl = gates(1, n, it)
            it += 1
            lo = n * CH
            init = 0.0 if n == NCH - 1 else carry[:, 0:1]
            S.op("dve", "tensor_tensor_scan", reads=[("a", sl), ("m", sl), "carry"], writes=["hr"],
                 out=hr_sb[:, 0, ::-1], data0=a_sb[:, sl, ::-1], data1=m_sb[:, sl, ::-1], initial=init,
                 op0=ALU.mult, op1=ALU.add)
            S.op("dve", "tensor_copy", reads=["hr"], writes=["carry"], out=carry[:, 0:1], in_=hr_sb[:, 0, 0:1])
            S.op("dve" if (n % 2 == 0) else "pool", "tensor_tensor", reads=["hr", "hf"], writes=[("yb", sl)],
                 out=yb_sb[:, sl, :], in0=hr_sb[:, 0, :], in1=u_sb[:, 2 + lo:2 + lo + CH], op=ALU.add)
            _dma(S, "sp", hT[lo // NT][:, lo % NT:lo % NT + CH], yb_sb[:, sl, :], reads=[("yb", sl)],
                 writes=[("hT_d", n)], semkey=("yb_o", sl))
            if lo % NT == 0:
                i4 = lo // NT
                S.op("pool", "collective_compute", dma=True, inc=1, semkey="cc_h",
                     reads=[("hT_d", n + q) for q in range(NT // CH)], writes=[("hg", i4)], kind="AllGather",
                     op=ALU.bypass, replica_groups=GROUPS, ins=[hT[i4]], outs=[T["hg"][i4]])
    S.barrier()


def phase_k4(C, T, w_ba, w_bb, w_out, yout, fin=None):
    S = C.S
    yaT, hg, szbT, sgaT, sgbT, xres = T["yaT"], T["hg"], T["szbT"], T["sgaT"], T["sgbT"], T["xres"]
    with C.phase():
        wba = C.sb("wba", [128, 4, D], BF16)
        wbb = C.sb("wbb", [128, 4, D], BF16)
        wo = C.sb("wo", [128, 8, D], BF16)
        wst = C.sb("wst", [128, 2, D])
        ya = C.sb("ya", [128, 2, 4, 512], BF16)
        hb = C.sb("hb", [128, 2, 4, 512], BF16)
        szb = C.sb("szb", [128, 2, 4, 512], BF16)
        yb = C.sb("yb", [128, 2, 4, 512], BF16)
        sga = C.sb("sga", [128, 4, 512], BF16)
        sgb = C.sb("sgb", [128, 4, 512], BF16)
        t1 = C.sb("t1", [128, 2, 512])
        t2 = C.sb("t2", [128, 2, 512])
        mT = C.sb("mT", [128, 2, 8, 512], BF16)
        xr = C.sb("xr", [128, 2, D])
        yo = C.sb("yo", [128, 2, D])
        pa = [C.ps("pa%d" % i, [128, 512]) for i in range(2)]
        pb = [C.ps("pb%d" % i, [128, 512]) for i in range(2)]
        po = [C.ps("po%d" % i, [128, 512]) for i in range(4)]
        if fin is not None:
            lng_sb = C.sb("lng_sb", [128, D])
            lnb_sb = C.sb("lnb_sb", [128, D])
            stats = C.sb("stats", [128, 2, 12])
            mv = C.sb("mv", [128, 2, 2])
            rstd = C.sb("rstd", [128, 2, 1])
            _dma(S, "sp", lng_sb[:], fin[0], writes=["lng"])
            _dma(S, "sp", lnb_sb[:], fin[1], writes=["lnb"])

        def load_w(lst, i):
          for (src, dst, nk, key) in lst:
            for kc in range(nk):
                sl = i % 2
                _dma(S, "sp", wst[:, sl, :], src[kc * 128:(kc + 1) * 128, :], writes=[("wst", sl)])
                if i % 2 == 0:
                    S.op("dve", "tensor_copy", reads=[("wst", sl)], writes=[(key, kc)], out=dst[:, kc, :], in_=wst[:, sl, :])
                else:
                    S.op("act", "activation", reads=[("wst", sl)], writes=[(key, kc)], out=dst[:, kc, :], in_=wst[:, sl, :],
                         func=AF.Copy)
                i += 1

        def load_blk(tb):
            bs = tb % 2
            tsl = slice(tb * 512, (tb + 1) * 512)
            _dma(S, "sp", ya[:, bs, :, :], yaT[:, tsl].rearrange("(k p) t -> p k t", p=128), writes=[("ya", bs)])
            cases = {}
            for cc in range(NCORES):
                j = cc % 4
                cases[cc] = [(hb[:, bs, :, :], hg[j][:, tb * 512:(tb + 1) * 512].rearrange("(k p) t -> p k t", p=128))]
            S.cond_dma("sp", cases, reads=[("hg", i4) for i4 in range(4)], writes=[("hb", bs)], semkey=("hb", bs))
            _dma(S, "sp", szb[:, bs, :, :], szbT[:, tsl].rearrange("(k p) t -> p k t", p=128), writes=[("szb", bs)])

        def mk_yb(tb):
            bs_ = tb % 2
            S.op("dve", "tensor_tensor", reads=[("hb", bs_), ("szb", bs_)], writes=[("yb", bs_)],
                 out=yb[:, bs_, :, :], in0=hb[:, bs_, :, :], in1=szb[:, bs_, :, :], op=ALU.mult)

        load_w(((w_ba, wba, 4, "wba"), (w_bb, wbb, 4, "wbb")), 0)
        load_blk(0)
        mk_yb(0)
        load_w(((w_out, wo, 8, "wo"),), 0)
        pcnt = 0
        ocnt = 0
        for tb in range(NT // 512):
            bs = tb % 2
            if tb + 1 < NT // 512:
                load_blk(tb + 1)
            for oc in range(8):
                ps_ = pcnt % 2
                gs = pcnt % 4
                pcnt += 1
                tsl_ = slice(tb * 512, (tb + 1) * 512)
                _dma(S, "sp", sga[:, gs, :], sgaT[oc * 128:(oc + 1) * 128, tsl_], writes=[("sga", gs)])
                _dma(S, "sp", sgb[:, gs, :], sgbT[oc * 128:(oc + 1) * 128, tsl_], writes=[("sgb", gs)])
                for kc in range(4):
                    S.op("pe", "matmul", reads=[("wba", kc), ("ya", bs)], writes=[("pa", ps_)], signal=(kc == 3),
                         out=pa[ps_][:, :], lhsT=wba[:, kc, oc * 128:(oc + 1) * 128], rhs=ya[:, bs, kc, :],
                         start=(kc == 0), stop=(kc == 3))
                for kc in range(4):
                    S.op("pe", "matmul", reads=[("wbb", kc), ("yb", bs)], writes=[("pb", ps_)], signal=(kc == 3),
                         out=pb[ps_][:, :], lhsT=wbb[:, kc, oc * 128:(oc + 1) * 128], rhs=yb[:, bs, kc, :],
                         start=(kc == 0), stop=(kc == 3))
                S.op("dve", "tensor_tensor", reads=[("pa", ps_), ("sga", gs)], writes=[("t1", ps_)],
                     out=t1[:, ps_, :], in0=pa[ps_][:, :], in1=sga[:, gs, :], op=ALU.mult)
                S.op("dve", "tensor_tensor", reads=[("pb", ps_), ("sgb", gs)], writes=[("t2", ps_)],
                     out=t2[:, ps_, :], in0=pb[ps_][:, :], in1=sgb[:, gs, :], op=ALU.mult)
                S.op("pool" if oc % 2 == 0 else "dve", "tensor_tensor", reads=[("t1", ps_), ("t2", ps_)],
                     writes=[("mT", bs, oc)], out=mT[:, bs, oc, :], in0=t1[:, ps_, :], in1=t2[:, ps_, :], op=ALU.add)
                if oc == 7 and tb + 1 < NT // 512:
                    mk_yb(tb + 1)
            for tt in range(4):
                g = tb * 4 + tt
                xs = g % 2
                _dma(S, "sp", xr[:, xs, :], xres[g * 128:(g + 1) * 128, :], writes=[("xr", xs)])
                for half in range(2):
                    ob = ocnt % 4
                    ocnt += 1
                    for kc in range(8):
                        S.op("pe", "matmul", reads=[("mT", bs, kc), ("wo", kc)], writes=[("po", ob)], signal=(kc == 7),
                             out=po[ob][:, :], lhsT=mT[:, bs, kc, tt * 128:(tt + 1) * 128],
                             rhs=wo[:, kc, half * 512:(half + 1) * 512], start=(kc == 0), stop=(kc == 7))
                    S.op("dve", "scalar_tensor_tensor", reads=[("xr", xs), ("po", ob)], writes=[("yo", xs, half)],
                         out=yo[:, xs, half * 512:(half + 1) * 512], in0=xr[:, xs, half * 512:(half + 1) * 512],
                         scalar=ALPHA, in1=po[ob][:, :], op0=ALU.mult, op1=ALU.add)
                yk = [("yo", xs, 0), ("yo", xs, 1)]
                if fin is not None:
                    for hh in range(2):
                        S.op("dve", "bn_stats", reads=yk, writes=[("stats", xs, hh)],
                             out=stats[:, xs, hh * 6:(hh + 1) * 6], in_=yo[:, xs, hh * 512:(hh + 1) * 512])
                    S.op("dve", "bn_aggr", reads=[("stats", xs, 0), ("stats", xs, 1)], writes=[("mv", xs)],
                         out=mv[:, xs, :], in_=stats[:, xs, :])
                    S.op("act", "activation", reads=[("mv", xs)], writes=[("rstd", xs)],
                         out=rstd[:, xs, :], in_=mv[:, xs, 1:2], func=AF.Sqrt, bias=1e-5, scale=1.0)
                    S.op("dve", "reciprocal", reads=[("rstd", xs)], writes=[("rstd", xs)], out=rstd[:, xs, :],
                         in_=rstd[:, xs, :])
                    S.op("dve", "scalar_tensor_tensor", reads=[("mv", xs), ("rstd", xs)], writes=[("nb", xs)],
                         out=mv[:, xs, 1:2], in0=mv[:, xs, 0:1], scalar=-1.0, in1=rstd[:, xs, :], op0=ALU.mult, op1=ALU.mult)
                    S.op("act", "activation", reads=yk + [("nb", xs), ("rstd", xs)], writes=yk,
                         out=yo[:, xs, :], in_=yo[:, xs, :], func=AF.Identity, scale=rstd[:, xs, :], bias=mv[:, xs, 1:2])
                    S.op("dve", "tensor_tensor", reads=yk + ["lng"], writes=yk,
                         out=yo[:, xs, :], in0=yo[:, xs, :], in1=lng_sb[:], op=ALU.mult)
                    S.op("dve", "tensor_tensor", reads=yk + ["lnb"], writes=yk,
                         out=yo[:, xs, :], in0=yo[:, xs, :], in1=lnb_sb[:], op=ALU.add)
                    _dma(S, "act", fin[2][g * 128:(g + 1) * 128, :], yo[:, xs, :], reads=yk, semkey=("yo_o", xs))
                else:
                    _dma(S, "act", yout[g * 128:(g + 1) * 128, :], yo[:, xs, :], reads=yk, semkey=("yo_o", xs))
    S.barrier()


def build_fused():
    C = Ctx()
    xin = C.din("xin", [NT, D])
    lns = C.din("lns", [3, 2, 128, D])
    ident = C.din("ident", [128, 128])
    w_in = C.din("w_in", [2, D, 5120])
    bm = C.din("bm", [2, 128, 16])
    bt = C.din("bt", [2, 4, 128, 2, NTAB, 128])
    cw = C.din("cw", [2, 128, 5])
    gw = C.din("gw", [2, 128, 4, 128])
    gb = C.din("gb", [2, 128, 4])
    lam = C.din("lam", [2, 128, 2])
    w_ba = C.din("w_ba", [2, 512, D])
    w_bb = C.din("w_bb", [2, 512, D])
    w_out = C.din("w_out", [2, D, D])
    zk = C.din("zk", [128, 256], BF16)
    zv = C.din("zv", [256, 128], BF16)
    out = C.dout("out", [NT, D])
    T = dict(ident=ident[:, :], zk=zk, zv=zv)
    T["xres"] = C.dint("xres", [NT, D])
    T["ybuf"] = C.dint("ybuf", [NT, D])
    for n in ("qT", "kT"):
        T[n] = C.dint(n, [512, NT], BF16)
    T["v"] = C.dint("v", [NT, 512], BF16)
    for n in ("szaT", "szbT"):
        T[n] = C.dint(n, [512, NT], BF16)
    T["uT"] = [C.dint("uT%d" % i, [512, 512]) for i in range(8)]
    for n in ("sgaT", "sgbT"):
        T[n] = C.dint(n, [D, NT], BF16)
    T["ug"] = [C.dint("ug%d" % i, [4 * 512, 512]) for i in range(8)]
    T["hb"] = C.dint("hb", [1024, 512], BF16)
    T["hbg"] = C.dint("hbg", [4 * 1024, 512], BF16)
    T["yaT"] = C.dint("yaT", [512, NT], BF16)
    T["hT"] = [C.dint("hT%d" % i, [128, NT], BF16) for i in range(4)]
    T["hg"] = [C.dint("hg%d" % i, [4 * 128, NT], BF16) for i in range(4)]
    yin = xin
    for l in range(2):
        phase_k1(C, T, yin, lns[l, 0], lns[l, 1], T["xres"], w_in=w_in[l], bm=bm[l])
        exchange_after_k1(C, T)
        phase_k2(C, T, bt[l])
        phase_k3(C, T, cw[l], gw[l], gb[l], lam[l])
        exchange_after_k3(C, T)
        phase_k4(C, T, w_ba[l], w_bb[l], w_out[l], T["ybuf"], fin=(lns[2, 0], lns[2, 1], out) if l == 1 else None)
        yin = T["ybuf"]
    return C.done()


def _bias_tables(rpb_l, j):
    H = 8
    tab = np.full((H, NTAB, 128, 128), NEG, np.float32)
    kp = np.arange(128)
    qi = np.arange(128)
    ck = (kp % 64)[:, None]
    cq = (qi % 64)[None, :]
    cs = np.clip(cq - 8, 0, 48)
    colok = (ck >= cs) & (ck <= cs + 15)
    dc = ck - cq + 15
    for cls, (r_loc, win_lo) in enumerate(((0, -4), (2, -4), (8, -4), (62, -6), (60, -4))):
        r = 64 * j + r_loc
        rq = r + (qi // 64)[None, :]
        rs = np.clip(rq - 4, 0, 248)
        for kt in range(CLS_NT[cls]):
            rk = r + win_lo + 2 * kt + (kp // 64)[:, None]
            ok = colok & (rk >= rs) & (rk <= rs + 7)
            dr = rk - rq + 7
            drc = np.clip(dr, 0, 14)
            dcc = np.clip(dc, 0, 30)
            vals = rpb_l[:, drc, dcc]
            tab[:, CLS_OFF[cls] + kt] = np.where(ok[None], vals, NEG)
    t = tab.reshape(4, 2, NTAB, 128, 128).transpose(0, 3, 1, 2, 4)
    return np.ascontiguousarray(t)


_NC = []


def _rep(v):
    v = np.asarray(v, np.float32)
    return np.broadcast_to(v[None, :], (128, v.shape[0]))


def kernel(x, emb_ln_g, emb_ln_b, w_in, rpb, conv_w, conv_b, lru_gate_w, lru_gate_b, lru_lambda,
           w_branch_attn, w_branch_lru, b_merge, w_out, ln_g, ln_b):
    import ml_dtypes
    f = lambda a: np.ascontiguousarray(np.asarray(a, np.float32))
    x = f(x)
    B = x.shape[0]
    if not _NC:
        _NC.append(build_fused())
    nc = _NC[0]
    lns = f(np.stack([np.stack([_rep(emb_ln_g), _rep(emb_ln_b)]), np.stack([_rep(ln_g[0]), _rep(ln_b[0])]),
                      np.stack([_rep(ln_g[1]), _rep(ln_b[1])])]))
    bm = f(np.stack([np.asarray(b_merge[l]).reshape(2, 8, 128).transpose(2, 0, 1).reshape(128, 16) for l in range(2)]))
    common = dict(lns=lns, ident=np.eye(128, dtype=np.float32), w_in=f(w_in), bm=bm, w_ba=f(w_branch_attn),
                  w_bb=f(w_branch_lru), w_out=f(w_out), zk=np.zeros((128, 256), ml_dtypes.bfloat16),
                  zv=np.zeros((256, 128), ml_dtypes.bfloat16))
    perj = []
    for j in range(4):
        ch = slice(j * 128, (j + 1) * 128)
        bt = np.stack([_bias_tables(f(rpb[l]), j) for l in range(2)])
        cw = np.stack([np.concatenate([f(conv_w[l])[:, ch].T, f(conv_b[l])[ch][:, None]], axis=1) for l in range(2)])
        gw = np.zeros((2, 128, 4, 128), np.float32)
        gb = np.zeros((2, 128, 4), np.float32)
        for l in range(2):
            for d in range(2):
                for g in range(2):
                    for bb in range(2):
                        gw[l, bb * 64:(bb + 1) * 64, d * 2 + g, bb * 64:(bb + 1) * 64] = f(lru_gate_w[l][d, g, 2 * j + bb])
                        gb[l, bb * 64:(bb + 1) * 64, d * 2 + g] = f(lru_gate_b[l][d, g, 2 * j + bb])
        lam = np.stack([f(lru_lambda[l])[:, ch].T for l in range(2)])
        perj.append(dict(bt=f(bt), cw=f(cw), gw=gw, gb=gb, lam=f(lam)))
    in_maps = []
    for c in range(NCORES):
        b, j = c // 4, c % 4
        m = dict(common)
        m.update(perj[j])
        m["xin"] = f(x[b, j * NT:(j + 1) * NT])
        in_maps.append(m)
    res = run_bass_kernel_spmd(nc, in_maps, core_ids=list(range(NCORES)))
    out = np.empty((B, SEQ, D), np.float32)
    for c in range(NCORES):
        out[c // 4, (c % 4) * NT:(c % 4 + 1) * NT] = res.results[c]["out"]
    return out
```
